# Optimizing a Trainium2 kernel written in Bass

```python
import math
import jax, jax.numpy as jnp
from jax import lax
import numpy as np

D_MODEL = 1024
BATCH = 4
SEQ = 4096
DEPTH = 4

N_MIXERS = 3
N_MLA = (DEPTH + 2) // 3
N_DIFF = (DEPTH + 1) // 3
N_LRU = DEPTH // 3
N_META = 16
Q_BLOCK = 128
NORM_EPS = 1e-6
MLA_HEADS = 16
MLA_Q_RANK = 384
MLA_KV_RANK = 256
MLA_NOPE = 128
MLA_ROPE = 64
MLA_V = 128
MLA_IN_WIDTH = MLA_Q_RANK + MLA_KV_RANK + MLA_ROPE
ROPE_THETA = 10000.0
DIFF_HEAD_DIM = 64
DIFF_HEADS = D_MODEL // (2 * DIFF_HEAD_DIM)
DIFF_WIDTH = DIFF_HEADS * 2 * DIFF_HEAD_DIM
LRU_WIDTH = 1536
LRU_BLOCKS = 6
LRU_BLOCK_W = LRU_WIDTH // LRU_BLOCKS
CONV_WIDTH = 4
CONV_PAD = (2, 1)
LRU_C = 8.0
FFN_HIDDEN = -(-(8 * D_MODEL) // (3 * 256)) * 256

kernel_name = 'hybrid_mla_diffattn_rglru_encoder'


def _rms_norm(x, g):
    xf = x.astype(jnp.float32)
    y = xf * lax.rsqrt(jnp.mean(xf * xf, axis=-1, keepdims=True) + NORM_EPS)
    return (y * g.astype(jnp.float32)).astype(x.dtype)


def _rope_tables(n_pos):
    pos = jnp.arange(n_pos, dtype=jnp.float32)
    inv_freq = ROPE_THETA ** (-jnp.arange(0, MLA_ROPE, 2, dtype=jnp.float32) / MLA_ROPE)
    ang = pos[:, None] * inv_freq[None, :]
    return jnp.cos(ang), jnp.sin(ang)


def _rope(x, cos, sin):
    xf = x.astype(jnp.float32)
    x1, x2 = jnp.split(xf, 2, axis=-1)
    return jnp.concatenate([x1 * cos - x2 * sin, x2 * cos + x1 * sin], axis=-1).astype(x.dtype)


def _sweep_query_blocks(block_fn, q_arrays):
    b, n_pos = q_arrays[0].shape[:2]
    n_real = n_pos - N_META
    nb = n_real // Q_BLOCK
    meta_out = block_fn(tuple(a[:, :N_META] for a in q_arrays), jnp.arange(N_META))
    blocks = tuple(jnp.moveaxis(a[:, N_META:].reshape(b, nb, Q_BLOCK, *a.shape[2:]), 1, 0) for a in q_arrays)
    pos_blocks = (N_META + jnp.arange(n_real)).reshape(nb, Q_BLOCK)
    outs = lax.map(lambda args: block_fn(args[0], args[1]), (blocks, pos_blocks))
    outs = jnp.moveaxis(outs, 0, 1)
    outs = outs.reshape(b, n_real, *outs.shape[3:])
    return jnp.concatenate([meta_out.astype(outs.dtype), outs], axis=1)


def _mla(h, w_in, q_norm_g, kv_norm_g, w_uq, w_ukv, w_o, cos, sin):
    b, n_pos, _ = h.shape
    c_q, c_kv, k_rope = jnp.split(h @ w_in, [MLA_Q_RANK, MLA_Q_RANK + MLA_KV_RANK], axis=-1)
    q = (_rms_norm(c_q, q_norm_g) @ w_uq).reshape(b, n_pos, MLA_HEADS, MLA_NOPE + MLA_ROPE)
    q_nope = q[..., :MLA_NOPE]
    q_rope = _rope(q[..., MLA_NOPE:], cos[:, None, :], sin[:, None, :])
    kv = (_rms_norm(c_kv, kv_norm_g) @ w_ukv).reshape(b, n_pos, MLA_HEADS, MLA_NOPE + MLA_V)
    k_nope, v = kv[..., :MLA_NOPE], kv[..., MLA_NOPE:].astype(jnp.float32)
    k_rope = _rope(k_rope, cos, sin)
    scale = (MLA_NOPE + MLA_ROPE) ** -0.5

    def block(qs, _qpos):
        qn, qr = qs
        s = (jnp.einsum('bqhd,bkhd->bhqk', qn, k_nope, preferred_element_type=jnp.float32)
             + jnp.einsum('bqhr,bkr->bhqk', qr, k_rope, preferred_element_type=jnp.float32))
        p = jax.nn.softmax(s * scale, axis=-1)
        return jnp.einsum('bhqk,bkhd->bqhd', p, v)

    o = _sweep_query_blocks(block, (q_nope, q_rope))
    return o.reshape(b, n_pos, MLA_HEADS * MLA_V).astype(h.dtype) @ w_o


def _diff_attn(h, w_in, lam_p, subln_g, w_o, lambda_init):
    b, n_pos, _ = h.shape
    q, k, v = jnp.split(h @ w_in, 3, axis=-1)
    q = q.reshape(b, n_pos, DIFF_HEADS, 2, DIFF_HEAD_DIM)
    k = k.reshape(b, n_pos, DIFF_HEADS, 2, DIFF_HEAD_DIM)
    v = v.reshape(b, n_pos, DIFF_HEADS, 2 * DIFF_HEAD_DIM).astype(jnp.float32)
    lp = lam_p.astype(jnp.float32)
    lam = jnp.exp(jnp.sum(lp[0] * lp[1])) - jnp.exp(jnp.sum(lp[2] * lp[3])) + lambda_init
    slopes = 2.0 ** (-8.0 * jnp.arange(1, DIFF_HEADS + 1, dtype=jnp.float32) / DIFF_HEADS)
    k_pos = jnp.arange(n_pos)
    scale = DIFF_HEAD_DIM ** -0.5

    def block(qs, q_pos):
        (qb,) = qs
        s = jnp.einsum('bqhcd,bkhcd->bchqk', qb, k, preferred_element_type=jnp.float32) * scale
        dist = jnp.abs(q_pos[:, None] - k_pos[None, :]).astype(jnp.float32)
        s = s - slopes[:, None, None] * dist
        p = jax.nn.softmax(s, axis=-1)
        p = p[:, 0] - lam * p[:, 1]
        return jnp.einsum('bhqk,bkhd->bqhd', p, v)

    o = _sweep_query_blocks(block, (q,))
    o = _rms_norm(o, subln_g) * (1.0 - lambda_init)
    return o.reshape(b, n_pos, DIFF_WIDTH).astype(h.dtype) @ w_o


def _scan_combine(e1, e2):
    a1, b1 = e1
    a2, b2 = e2
    return a1 * a2, a2 * b1 + b2


def _rg_lru(x, w_g, b_g, lam, reverse):
    b, n_pos, _ = x.shape
    xf = x.astype(jnp.float32)
    xblk = xf.reshape(b, n_pos, LRU_BLOCKS, LRU_BLOCK_W)
    gates = jnp.einsum('blnc,gncd->gblnd', xblk, w_g.astype(jnp.float32)).reshape(2, b, n_pos, LRU_WIDTH)
    gates = jax.nn.sigmoid(gates + b_g.astype(jnp.float32)[:, None, None, :])
    r, i = gates[0], gates[1]
    log_a = -LRU_C * r * jax.nn.softplus(-lam.astype(jnp.float32))
    a = jnp.exp(log_a)
    u = jnp.sqrt(-jnp.expm1(2.0 * log_a)) * (i * xf)
    _, hs = lax.associative_scan(_scan_combine, (a, u), axis=1, reverse=reverse)
    return hs


def _rglru_block(h, w_in, conv_w, conv_b, w_gates, b_gates, lam, w_o):
    xb, gate = jnp.split(h @ w_in, 2, axis=-1)
    xb = lax.conv_general_dilated(xb, conv_w[:, None, :].astype(xb.dtype), window_strides=(1,),
                                  padding=[CONV_PAD], dimension_numbers=('NWC', 'WIO', 'NWC'),
                                  feature_group_count=LRU_WIDTH) + conv_b
    y = (_rg_lru(xb, w_gates[0], b_gates[0], lam[0], reverse=False)
         + _rg_lru(xb, w_gates[1], b_gates[1], lam[1], reverse=True))
    return (y.astype(h.dtype) * jax.nn.gelu(gate)) @ w_o


def _swiglu(h, w_in, w_out):
    g, u = jnp.split(h @ w_in, 2, axis=-1)
    return (jax.nn.silu(g) * u) @ w_out


def setup_inputs(seed: int = 0) -> dict:
    key = jax.random.key(seed)
    ks = iter(jax.random.split(key, 24))

    def nrm(shape, scale):
        return scale * jax.random.normal(next(ks), shape, jnp.float32)

    D = D_MODEL
    x = nrm((BATCH, SEQ, D), 1.0)
    meta_tokens = nrm((N_META, D), 1.0)
    norm_g = 1.0 + nrm((DEPTH, 4, D), 0.05)
    mla_w_in = nrm((N_MLA, D, MLA_IN_WIDTH), D ** -0.5)
    mla_q_norm = 1.0 + nrm((N_MLA, MLA_Q_RANK), 0.05)
    mla_kv_norm = 1.0 + nrm((N_MLA, MLA_KV_RANK), 0.05)
    mla_w_uq = nrm((N_MLA, MLA_Q_RANK, MLA_HEADS * (MLA_NOPE + MLA_ROPE)), MLA_Q_RANK ** -0.5)
    mla_w_ukv = nrm((N_MLA, MLA_KV_RANK, MLA_HEADS * (MLA_NOPE + MLA_V)), MLA_KV_RANK ** -0.5)
    mla_w_o = nrm((N_MLA, MLA_HEADS * MLA_V, D), (MLA_HEADS * MLA_V) ** -0.5)
    diff_w_in = nrm((N_DIFF, D, 3 * DIFF_WIDTH), D ** -0.5)
    diff_lambda = nrm((N_DIFF, 4, DIFF_HEAD_DIM), 0.1)
    diff_subln = 1.0 + nrm((N_DIFF, 2 * DIFF_HEAD_DIM), 0.05)
    diff_w_o = nrm((N_DIFF, DIFF_WIDTH, D), DIFF_WIDTH ** -0.5)
    lru_w_in = nrm((N_LRU, D, 2 * LRU_WIDTH), D ** -0.5)
    lru_conv_w = nrm((N_LRU, CONV_WIDTH, LRU_WIDTH), CONV_WIDTH ** -0.5)
    lru_conv_b = nrm((N_LRU, LRU_WIDTH), 0.01)
    lru_w_gates = nrm((N_LRU, 2, 2, LRU_BLOCKS, LRU_BLOCK_W, LRU_BLOCK_W), LRU_BLOCK_W ** -0.5)
    lru_b_gates = nrm((N_LRU, 2, 2, LRU_WIDTH), 0.01)
    a_c = jax.random.uniform(next(ks), (N_LRU, 2, LRU_WIDTH), jnp.float32, 0.9, 0.999)
    s = a_c ** (1.0 / LRU_C)
    lru_lambda = jnp.log(s) - jnp.log1p(-s)
    lru_w_o = nrm((N_LRU, LRU_WIDTH, D), LRU_WIDTH ** -0.5)
    ffn_w_in = nrm((DEPTH, D, 2 * FFN_HIDDEN), D ** -0.5)
    ffn_w_out = nrm((DEPTH, FFN_HIDDEN, D), FFN_HIDDEN ** -0.5)
    return {'x': x, 'meta_tokens': meta_tokens, 'norm_g': norm_g,
            'mla_w_in': mla_w_in, 'mla_q_norm': mla_q_norm, 'mla_kv_norm': mla_kv_norm,
            'mla_w_uq': mla_w_uq, 'mla_w_ukv': mla_w_ukv, 'mla_w_o': mla_w_o,
            'diff_w_in': diff_w_in, 'diff_lambda': diff_lambda, 'diff_subln': diff_subln, 'diff_w_o': diff_w_o,
            'lru_w_in': lru_w_in, 'lru_conv_w': lru_conv_w, 'lru_conv_b': lru_conv_b,
            'lru_w_gates': lru_w_gates, 'lru_b_gates': lru_b_gates, 'lru_lambda': lru_lambda, 'lru_w_o': lru_w_o,
            'ffn_w_in': ffn_w_in, 'ffn_w_out': ffn_w_out}


def reference(x, meta_tokens, norm_g,
              mla_w_in, mla_q_norm, mla_kv_norm, mla_w_uq, mla_w_ukv, mla_w_o,
              diff_w_in, diff_lambda, diff_subln, diff_w_o,
              lru_w_in, lru_conv_w, lru_conv_b, lru_w_gates, lru_b_gates, lru_lambda, lru_w_o,
              ffn_w_in, ffn_w_out):
    b = x.shape[0]
    meta = jnp.broadcast_to(meta_tokens[None].astype(x.dtype), (b, N_META, D_MODEL))
    h = jnp.concatenate([meta, x], axis=1)
    cos, sin = _rope_tables(h.shape[1])
    for i in range(DEPTH):
        kind, j = i % N_MIXERS, i // N_MIXERS
        g = norm_g[i]
        u = _rms_norm(h, g[0])
        if kind == 0:
            u = _mla(u, mla_w_in[j], mla_q_norm[j], mla_kv_norm[j], mla_w_uq[j], mla_w_ukv[j], mla_w_o[j], cos, sin)
        elif kind == 1:
            lambda_init = 0.8 - 0.6 * math.exp(-0.3 * i)
            u = _diff_attn(u, diff_w_in[j], diff_lambda[j], diff_subln[j], diff_w_o[j], lambda_init)
        else:
            u = _rglru_block(u, lru_w_in[j], lru_conv_w[j], lru_conv_b[j], lru_w_gates[j], lru_b_gates[j],
                             lru_lambda[j], lru_w_o[j])
        h = h + _rms_norm(u, g[1])
        u = _swiglu(_rms_norm(h, g[2]), ffn_w_in[i], ffn_w_out[i])
        h = h + _rms_norm(u, g[3])
    return h[:, N_META:]
```

```python
import math
from contextlib import ExitStack

import numpy as np
import ml_dtypes
import concourse.bass as bass
import concourse.mybir as mybir
from concourse.bass_utils import run_bass_kernel_spmd

F32 = mybir.dt.float32
BF16 = mybir.dt.bfloat16
ALU = mybir.AluOpType
AF = mybir.ActivationFunctionType
NPBF = ml_dtypes.bfloat16

D = 1024
NREAL = 2048
NMETA = 16
NOWN = NREAL + NMETA
NFULL = 2 * NREAL + NMETA
SBW = 688
SUBS = [(0, 512), (512, 176)]
SUPER = [(0, SUBS), (688, SUBS), (1376, SUBS)]
KT = [(t * 128, 128) for t in range(32)] + [(4096, 16)]
QB = [(0, 512), (512, 512), (1024, 512), (1536, 512), (2048, 16)]
KB = [(i * 512, 512) for i in range(8)] + [(4096, 16)]
NV = 164
AW = 48000
KINDS = ["mla", "diff", "lru", "mla"]
JIDX = [0, 0, 0, 1]
EPS = 1e-6


class Op:
    __slots__ = ("eng", "fn", "deps", "signal", "sigval", "dma", "sem", "cc", "idx")

    def __init__(self, eng, fn, dma, cc=False):
        self.cc = cc
        self.eng = eng
        self.fn = fn
        self.deps = []
        self.signal = dma
        self.sigval = 0
        self.dma = dma
        self.sem = None


class Prog:
    ENGS = ("pe", "act", "dve", "pool", "sp")
    NDMASEM = 8

    def __init__(self, nc):
        self.nc = nc
        self.ops = []
        self.last_w = {}
        self.readers = {}
        self.kids = {}
        self.fence = []
        self.fence_set = set()

    def _keys(self, b):
        if "#" in b:
            par = b.split("#")[0]
            self.kids.setdefault(par, set()).add(b)
            return (b, par)
        return (b,) + tuple(self.kids.get(b, ()))

    def add(self, eng, fn, reads=(), writes=(), dma=False, cc=False):
        op = Op(eng, fn, dma, cc)
        deps = set()
        for b in reads:
            for k in self._keys(b):
                w = self.last_w.get(k)
                if w is not None:
                    deps.add(w)
        for b in writes:
            for k in self._keys(b):
                w = self.last_w.get(k)
                if w is not None:
                    deps.add(w)
                deps.update(self.readers.get(k, ()))
        for b in reads:
            self.readers.setdefault(b, []).append(op)
        for b in writes:
            self.last_w[b] = op
            self.readers[b] = []
            if "#" not in b:
                for k in self.kids.get(b, ()):
                    self.last_w[k] = op
                    self.readers[k] = []
        deps.update(self.fence)
        deps.discard(op)
        latest = {}
        final = []
        for d in deps:
            if d.dma:
                final.append(d)
            else:
                cur = latest.get(d.eng)
                if cur is None or cur.idx < d.idx:
                    latest[d.eng] = d
        final.extend(latest.values())
        for d in final:
            if d.eng == "pe" and eng == "pe" and not d.dma and not dma and d not in self.fence_set:
                continue
            d.signal = True
            op.deps.append(d)
        op.idx = len(self.ops)
        self.ops.append(op)
        return op

    def barrier(self):
        tails = {}
        for op in self.ops:
            tails[(op.eng, op.dma)] = op
        dm = {}
        for op in self.ops:
            if op.dma:
                dm.setdefault(op.eng, []).append(op)
        fence = list(tails.values())
        for e, lst in dm.items():
            fence.extend(lst[-self.NDMASEM:])
        self.fence = fence
        self.fence_set = set(fence)
        for o in fence:
            o.signal = True

    def emit(self):
        nc = self.nc
        cnt = {e: 0 for e in self.ENGS}
        dma_rr = {e: 0 for e in self.ENGS}
        dma_last = {}
        dma_cnt = {}
        for op in self.ops:
            if op.cc:
                k = ("cc", 0)
                dma_cnt[k] = dma_cnt.get(k, 0) + 1
                op.sem = k
                op.sigval = dma_cnt[k]
            elif op.dma:
                k = (op.eng, dma_rr[op.eng] % self.NDMASEM)
                dma_rr[op.eng] += 1
                prev = dma_last.get(k)
                if prev is not None:
                    op.deps.append(prev)
                dma_last[k] = op
                dma_cnt[k] = dma_cnt.get(k, 0) + 16
                op.sem = k
                op.sigval = dma_cnt[k]
            elif op.signal:
                cnt[op.eng] += 1
                op.sem = op.eng
                op.sigval = cnt[op.eng]
        with ExitStack() as st:
            sems = {}
            for e in self.ENGS:
                sems[e] = st.enter_context(nc.semaphore("s_" + e))
            for k in dma_cnt:
                sems[k] = st.enter_context(nc.semaphore("d_%s%d" % k))
            block = st.enter_context(nc.Block())
            engobj = {"pe": block.tensor, "act": block.scalar, "dve": block.vector,
                      "pool": block.gpsimd, "sp": block.sync}
            for e in self.ENGS:
                myops = [op for op in self.ops if op.eng == e]
                if not myops:
                    continue

                def body(eng, myops=myops):
                    waited = {}
                    for op in myops:
                        need = {}
                        for d in op.deps:
                            if need.get(d.sem, 0) < d.sigval:
                                need[d.sem] = d.sigval
                        for k, v in need.items():
                            if waited.get(k, 0) < v:
                                eng.wait_ge(sems[k], v)
                                waited[k] = v
                        ins = op.fn(eng)
                        if op.signal:
                            ins.then_inc(sems[op.sem], 16 if (op.dma and not op.cc) else 1)

                engobj[e](body)


class T:
    def __init__(self, ap, name):
        self.ap = ap
        self.name = name


class Rot:
    def __init__(self, items):
        self.items = items
        self.i = 0

    def get(self):
        t = self.items[self.i % len(self.items)]
        self.i += 1
        return t


class Bld:
    def __init__(self, nc, st):
        self.nc = nc
        self.P = Prog(nc)
        self.arena = st.enter_context(nc.sbuf_tensor("arena", [128, AW], F32))
        self.psum = st.enter_context(nc.psum_tensor("psum", [128, 8, 512], F32))
        self.off = 0
        self.gen = 0
        self.dr = {}
        self.ev = 0
        self.psr = Rot([self.ps(i) for i in range(4, 8)])

    def alloc(self, name, shape, dt):
        n = 1
        for s in shape:
            n *= s
        size = 4 if dt == F32 else 2
        words = (n * size + 3) // 4
        words = (words + 7) // 8 * 8
        assert self.off + words <= AW, (name, self.off, words)
        ap = self.arena[:, self.off:self.off + words]
        self.off += words
        if dt != F32:
            ap = ap.bitcast(dt)
        ap = ap[:, 0:n]
        if len(shape) == 2:
            ap = ap.rearrange("p (a b) -> p a b", b=shape[1])
        elif len(shape) == 3:
            ap = ap.rearrange("p (a b c) -> p a b c", b=shape[1], c=shape[2])
        return T(ap, "%s@%d" % (name, self.gen))

    def phase(self, keep):
        self.P.barrier()
        self.off = keep
        self.gen += 1

    def ps(self, i):
        return T(self.psum[:, i, :], "ps%d" % i)

    def dram(self, name, shape, dt, kind):
        t = self.nc.dram_tensor(name, list(shape), dt, kind=kind).ap()
        self.dr[name] = t
        return t

    def mm(self, out, lhsT, rhs, start, stop, r, w):
        self.P.add("pe", lambda e: e.matmul(out, lhsT, rhs, start=start, stop=stop), r, w)

    def act(self, out, in_, func, r, w, bias=None, scale=None):
        kw = {}
        if bias is not None:
            kw["bias"] = bias
        if scale is not None:
            kw["scale"] = scale
        self.P.add("act", lambda e: e.activation(out, in_, func, **kw), r, w)

    def tt(self, out, in0, in1, op, r, w, eng="dve"):
        self.P.add(eng, lambda e: e.tensor_tensor(out, in0, in1, op), r, w)

    def ts(self, out, in0, s1, s2, op0, op1, r, w, eng="dve"):
        if s2 is None:
            self.P.add(eng, lambda e: e.tensor_scalar(out, in0, s1, None, op0), r, w)
        else:
            self.P.add(eng, lambda e: e.tensor_scalar(out, in0, s1, s2, op0, op1), r, w)

    def stt(self, out, in0, scalar, in1, op0, op1, r, w, eng="dve"):
        self.P.add(eng, lambda e: e.scalar_tensor_tensor(out, in0, scalar, in1, op0, op1), r, w)

    def copy(self, out, in_, r, w, eng=None):
        if eng is None:
            eng = "act" if self.ev % 2 == 0 else "dve"
            self.ev += 1
        if eng == "act":
            self.P.add("act", lambda e: e.copy(out, in_), r, w)
        else:
            self.P.add(eng, lambda e: e.tensor_copy(out, in_), r, w)

    def dma(self, out, in_, r, w, q="sp"):
        self.P.add(q, lambda e: e.dma_start(out=out, in_=in_), r, w, dma=True)

    def memset(self, ap, val, w, eng="dve"):
        self.P.add(eng, lambda e: e.memset(ap, val), (), w)

    def setup_consts(self, cst_dram):
        self.ones = self.alloc("ones", [128], BF16)
        self.cst = self.alloc("cst", [4], F32)
        self.memset(self.ones.ap, 1.0, [self.ones.name])
        self.onesf = self.alloc("onesf", [128], F32)
        self.memset(self.onesf.ap, 1.0, [self.onesf.name])
        self.dma(self.cst.ap, cst_dram, [], [self.cst.name])
        self.eps = self.cst.ap[:, 0:1]
        self.one = self.cst.ap[:, 1:2]
        self.base = self.off

    def load_vecs(self, l):
        v = self.alloc("vec", [NV], F32)
        self.dma(v.ap, self.dr["vecs"][l], [], [v.name])
        return v

    def wload(self, rot, src, shape):
        t = rot.get()
        ap = t.ap
        if len(shape) == 2:
            dst = ap[:, 0:shape[0], 0:shape[1]]
        else:
            dst = ap
        self.dma(dst, src, [], [t.name], q="pool")
        return t

    def rms_stats(self, x, C, subs, sq, rstd, dim):
        for si, (c0, n) in enumerate(subs):
            self.act(sq.ap[:, 0:C, c0:c0 + n], x.ap[:, 0:C, c0:c0 + n], AF.Square,
                     [x.name + "#%d" % si], [sq.name + "#%d" % si])
            ps = self.psr.get()
            for c in range(C):
                self.mm(ps.ap[:, 0:n], self.ones.ap, sq.ap[:, c, c0:c0 + n], c == 0, c == C - 1,
                        [sq.name + "#%d" % si, self.ones.name], [ps.name])
            self.act(rstd.ap[:, c0:c0 + n], ps.ap[:, 0:n], AF.Sqrt, [ps.name, self.cst.name],
                     [rstd.name + "#%d" % si], bias=self.eps, scale=1.0 / dim)
            rs = rstd.ap[:, c0:c0 + n]
            self.P.add("dve", lambda e, rs=rs: e.reciprocal(rs, rs), [rstd.name + "#%d" % si],
                       [rstd.name + "#%d" % si])

    def load_h(self, hT, s0, subs, h):
        src = hT.rearrange("(c p) n -> p c n", p=128)
        for si, (c0, n) in enumerate(subs):
            self.dma(h.ap[:, :, c0:c0 + n], src[:, :, s0 + c0:s0 + c0 + n], [], [h.name + "#%d" % si])
        return h

    def norm_to_bf16(self, h, subs, vec, gcol, C, dim, bufs):
        sq, rstd, u = bufs
        self.rms_stats(h, C, subs, sq, rstd, dim)
        for si, (c0, n) in enumerate(subs):
            for c in range(C):
                self.stt(u.ap[:, c, c0:c0 + n], h.ap[:, c, c0:c0 + n], vec.ap[:, gcol + c:gcol + c + 1],
                         rstd.ap[:, c0:c0 + n], ALU.mult, ALU.mult,
                         [h.name + "#%d" % si, rstd.name + "#%d" % si, vec.name], [u.name + "#%d" % si])
        return u

    def exchange(self, l):
        k = KINDS[l]
        self.phase(self.base)
        pairs = []
        if k == "mla":
            pairs.append((self.dr["ks%d" % l], self.dr["ksf%d" % l].rearrange("r p n -> (r p) n")))
        elif k == "diff":
            for i in range(4):
                pairs.append((self.dr["ksk%d" % l][256 * i:256 * (i + 1), :], self.dr["ksfk%d" % l][i]))
            for i in range(4):
                rows = 512 if i < 3 else 528
                pairs.append((self.dr["ksv%d" % l][512 * i:512 * i + rows, :], self.dr["ksfv%d" % l][i, 0:2 * rows, :]))
        else:
            for i in range(12):
                pairs.append((self.dr["ksx%d" % l][128 * i:128 * (i + 1), :], self.dr["ksfx%d" % l][i]))
        for (src, dst) in pairs:
            self.P.add("pool", lambda e, src=src, dst=dst: e.collective_compute(
                "AllGather", ALU.bypass, replica_groups=[[0, 1], [2, 3], [4, 5], [6, 7]], ins=[src], outs=[dst]),
                [], [], dma=True, cc=True)

    def phaseA_mla(self, l):
        j = JIDX[l]
        hT = self.dr["hT%d" % l]
        ks = self.dr["ks%d" % l]
        cqd = self.dr["cqn%d" % l]
        win = self.dr["mla_win"]
        self.phase(self.base)
        vec = self.load_vecs(l)
        cs = self.alloc("cs", [NOWN], F32)
        self.dma(cs.ap, self.dr["cs"], [], [cs.name])
        hR = Rot([self.alloc("h%d" % i, [8, SBW], F32) for i in range(2)])
        nb = (self.alloc("sq", [8, SBW], BF16), self.alloc("rstd", [SBW], F32), self.alloc("u", [8, SBW], BF16))
        cq = self.alloc("cq", [3, SBW], F32)
        ckv = self.alloc("ckv", [2, SBW], F32)
        nbq = (nb[0], nb[1], self.alloc("cqn", [3, SBW], BF16))
        nbk = (nb[0], nb[1], self.alloc("ckvn", [2, SBW], BF16))
        krb = self.alloc("krb", [SBW], BF16)
        t1 = self.alloc("t1", [512], F32)
        t2 = self.alloc("t2", [512], F32)
        wrot = Rot([self.alloc("w%d" % i, [8, 128], BF16) for i in range(3)])
        hn = self.load_h(hT, SUPER[0][0], SUPER[0][1], hR.get())
        for k, (s0, subs) in enumerate(SUPER):
            W = sum(n for _, n in subs)
            h = hn
            if k + 1 < len(SUPER):
                hn = self.load_h(hT, SUPER[k + 1][0], SUPER[k + 1][1], hR.get())
            u = self.norm_to_bf16(h, subs, vec, 0, 8, 1024, nb)
            for m in range(6):
                wt = self.wload(wrot, win[j, m], [8, 128])
                for si, (c0, n) in enumerate(subs):
                    ps = self.psr.get()
                    for kc in range(8):
                        self.mm(ps.ap[:, 0:n], wt.ap[:, kc, :], u.ap[:, kc, c0:c0 + n], kc == 0, kc == 7,
                                [wt.name, u.name + "#%d" % si], [ps.name])
                    if m < 3:
                        self.copy(cq.ap[:, m, c0:c0 + n], ps.ap[:, 0:n], [ps.name], [cq.name + "#%d" % si])
                    elif m < 5:
                        self.copy(ckv.ap[:, m - 3, c0:c0 + n], ps.ap[:, 0:n], [ps.name], [ckv.name + "#%d" % si])
                    else:
                        g0 = s0 + c0
                        self.tt(t1.ap[0:64, 0:n], ps.ap[0:64, 0:n], cs.ap[0:64, g0:g0 + n], ALU.mult,
                                [ps.name, cs.name], [t1.name])
                        self.tt(t2.ap[0:64, 0:n], ps.ap[64:128, 0:n], cs.ap[64:128, g0:g0 + n], ALU.mult,
                                [ps.name, cs.name], [t2.name])
                        self.tt(krb.ap[0:64, c0:c0 + n], t1.ap[0:64, 0:n], t2.ap[0:64, 0:n], ALU.add,
                                [t1.name, t2.name], [krb.name + "#%d" % si], eng="pool")
            cqn = self.norm_to_bf16(cq, subs, vec, 32, 3, 384, nbq)
            ckvn = self.norm_to_bf16(ckv, subs, vec, 35, 2, 256, nbk)
            self.dma(cqd.rearrange("(c p) n -> p c n", p=128)[:, :, s0:s0 + W], cqn.ap[:, :, 0:W],
                     [cqn.name], ["cqn%d#%d" % (l, s0)])
            self.dma(ks[0:256, :].rearrange("(c p) n -> p c n", p=128)[:, :, s0:s0 + W], ckvn.ap[:, :, 0:W],
                     [ckvn.name], ["ks%d#a%d" % (l, s0)])
            self.dma(ks[256:320, s0:s0 + W], krb.ap[0:64, 0:W], [krb.name], ["ks%d#b%d" % (l, s0)])

    def phaseB_mla(self, l):
        j = JIDX[l]
        ksf = self.dr["ksf%d" % l]
        cqd = self.dr["cqn%d" % l]
        oall = self.dr["oall%d" % l]
        wuq = self.dr["mla_wuq"]
        wukv = self.dr["mla_wukv"]
        SCALE = 192.0 ** -0.5
        self.phase(self.base)
        cs = self.alloc("cs", [NOWN], F32)
        self.dma(cs.ap, self.dr["cs"], [], [cs.name])
        ckv = self.alloc("ckvf", [2, NFULL], BF16)
        kr = self.alloc("krf", [NFULL], BF16)
        cq = self.alloc("cqf", [3, NOWN], BF16)
        for r_ in range(2):
            self.dma(ckv.ap[:, :, r_ * 2048:(r_ + 1) * 2048],
                     ksf[r_, 0:256, 0:2048].rearrange("(c p) n -> p c n", p=128), ["ksf%d" % l], [ckv.name])
            self.dma(kr.ap[0:64, r_ * 2048:(r_ + 1) * 2048], ksf[r_, 256:320, 0:2048], ["ksf%d" % l], [kr.name])
        self.dma(ckv.ap[:, :, 4096:4112], ksf[0, 0:256, 2048:2064].rearrange("(c p) n -> p c n", p=128),
                 ["ksf%d" % l], [ckv.name])
        self.dma(kr.ap[0:64, 4096:4112], ksf[0, 256:320, 2048:2064], ["ksf%d" % l], [kr.name])
        self.dma(cq.ap, cqd.rearrange("(c p) n -> p c n", p=128), ["cqn%d" % l], [cq.name])
        KhR = Rot([self.alloc("Kh%d" % i, [NFULL], BF16) for i in range(2)])
        VhR = Rot([self.alloc("Vh%d" % i, [33, 128], BF16) for i in range(2)])
        QnR = Rot([self.alloc("Qn%d" % i, [NOWN], BF16) for i in range(2)])
        QrR = Rot([self.alloc("Qr%d" % i, [NOWN], BF16) for i in range(2)])
        for qt_ in QrR.items:
            self.memset(qt_.ap[64:128, :], 0.0, [qt_.name])
        self.memset(kr.ap[64:128, :], 0.0, [kr.name])
        OhR = Rot([self.alloc("Oh%d" % i, [NOWN], BF16) for i in range(2)])
        PtR = Rot([self.alloc("Pt%d" % i, [512], BF16) for i in range(6)])
        wqR = Rot([self.alloc("wq%d" % i, [3, 256], BF16) for i in range(2)])
        wkR = Rot([self.alloc("wk%d" % i, [2, 256], BF16) for i in range(2)])
        t1 = self.alloc("t1", [512], F32)
        t2 = self.alloc("t2", [512], F32)
        rsR = Rot([self.alloc("rs%d" % i, [512], F32) for i in range(2)])
        accR = Rot([(self.ps(0), self.ps(1)), (self.ps(2), self.ps(3))])
        saR = Rot([(self.alloc("saP%d" % i, [512], F32), self.alloc("saD%d" % i, [512], F32)) for i in range(2)])
        def gen_list(hh):
            wq = self.wload(wqR, wuq[j, hh], [3, 256])
            wk = self.wload(wkR, wukv[j, hh], [2, 256])
            Kh, Vh, Qn, Qr, Oh = KhR.get(), VhR.get(), QnR.get(), QrR.get(), OhR.get()
            ops = []

            def gK(c0, n):
                ps = self.psr.get()
                for kc in range(2):
                    self.mm(ps.ap[:, 0:n], wk.ap[:, kc, 0:128], ckv.ap[:, kc, c0:c0 + n], kc == 0, kc == 1,
                            [wk.name, ckv.name], [ps.name])
                self.copy(Kh.ap[:, c0:c0 + n], ps.ap[:, 0:n], [ps.name], [Kh.name + "#%d" % (c0 // 512)])

            def gV(g):
                ps = self.psr.get()
                tl = list(range(g, min(g + 4, 33)))
                for t in tl:
                    k0, kn = KT[t]
                    for kc in range(2):
                        self.mm(ps.ap[0:kn, (t - g) * 128:(t - g + 1) * 128], ckv.ap[:, kc, k0:k0 + kn],
                                wk.ap[:, kc, 128:256], kc == 0, kc == 1, [wk.name, ckv.name], [ps.name])
                if len(tl) == 4:
                    self.copy(Vh.ap[:, g:g + 4, :], ps.ap[:, 0:512].rearrange("p (a b) -> p a b", b=128),
                              [ps.name], [Vh.name + "#%d" % (g // 4)])
                else:
                    self.copy(Vh.ap[0:16, 32, :], ps.ap[0:16, 0:128], [ps.name], [Vh.name + "#8"])

            def gQ(qi, c0, n):
                ps = self.psr.get()
                for kc in range(3):
                    self.mm(ps.ap[:, 0:n], wq.ap[:, kc, 0:128], cq.ap[:, kc, c0:c0 + n], kc == 0, kc == 2,
                            [wq.name, cq.name], [ps.name])
                self.copy(Qn.ap[:, c0:c0 + n], ps.ap[:, 0:n], [ps.name], [Qn.name + "#%d" % qi])
                ps = self.psr.get()
                for kc in range(3):
                    self.mm(ps.ap[:, 0:n], wq.ap[:, kc, 128:256], cq.ap[:, kc, c0:c0 + n], kc == 0, kc == 2,
                            [wq.name, cq.name], [ps.name])
                self.tt(t1.ap[0:64, 0:n], ps.ap[0:64, 0:n], cs.ap[0:64, c0:c0 + n], ALU.mult,
                        [ps.name, cs.name], [t1.name])
                self.tt(t2.ap[0:64, 0:n], ps.ap[64:128, 0:n], cs.ap[64:128, c0:c0 + n], ALU.mult,
                        [ps.name, cs.name], [t2.name])
                self.tt(Qr.ap[0:64, c0:c0 + n], t1.ap[0:64, 0:n], t2.ap[0:64, 0:n], ALU.add,
                        [t1.name, t2.name], [Qr.name + "#%d" % qi], eng="pool")

            for (c0, n) in KB:
                ops.append(lambda c0=c0, n=n: gK(c0, n))
            for g in range(0, 33, 4):
                ops.append(lambda g=g: gV(g))
            for qi, (c0, n) in enumerate(QB):
                ops.append(lambda qi=qi, c0=c0, n=n: gQ(qi, c0, n))
            return (Kh, Vh, Qn, Qr, Oh), ops

        nxt_bufs, nxt_ops = gen_list(0)
        for f_ in nxt_ops:
            f_()
        for hh in range(16):
            Kh, Vh, Qn, Qr, Oh = nxt_bufs
            if hh + 1 < 16:
                nxt_bufs, nxt_ops = gen_list(hh + 1)
            else:
                nxt_ops = []
            per = (len(nxt_ops) + 3) // 4
            for qi, (q0, qn) in enumerate(QB):
                psO, psS = accR.get()
                saP, saD = saR.get()
                LAG = 2
                pend = []
                for i in range(33 + LAG):
                    if i < 33:
                        k0, kn = KT[i]
                        pT = self.psr.get()
                        self.mm(pT.ap[0:kn, 0:qn], Kh.ap[:, k0:k0 + kn], Qn.ap[:, q0:q0 + qn], True, False,
                                [Kh.name + "#%d" % (k0 // 512), Qn.name + "#%d" % qi], [pT.name])
                        self.mm(pT.ap[0:kn, 0:qn], kr.ap[:, k0:k0 + kn], Qr.ap[:, q0:q0 + qn], False, True,
                                [kr.name, Qr.name + "#%d" % qi], [pT.name])
                        pt = PtR.get()
                        self.act(pt.ap[0:kn, 0:qn], pT.ap[0:kn, 0:qn], AF.Exp, [pT.name], [pt.name], scale=SCALE)
                        pend.append((i, pt))
                    if i >= LAG:
                        t, pt = pend.pop(0)
                        k0, kn = KT[t]
                        self.mm(psO.ap[:, 0:qn], Vh.ap[0:kn, t, :], pt.ap[0:kn, 0:qn], t == 0, t == 32,
                                [Vh.name + "#%d" % (t // 4), pt.name], [psO.name])
                        if t % 3 == 0:
                            self.mm(psS.ap[:, 0:qn], self.ones.ap[0:kn, :], pt.ap[0:kn, 0:qn], t == 0, False,
                                    [self.ones.name, pt.name], [psS.name])
                        else:
                            sa, se = (saP, "pool") if t % 3 == 2 else (saD, "dve")
                            if t < 3:
                                self.copy(sa.ap[0:kn, 0:qn], pt.ap[0:kn, 0:qn], [pt.name], [sa.name], eng=se)
                            else:
                                self.tt(sa.ap[0:kn, 0:qn], sa.ap[0:kn, 0:qn], pt.ap[0:kn, 0:qn], ALU.add,
                                        [sa.name, pt.name], [sa.name], eng=se)
                self.mm(psS.ap[:, 0:qn], self.onesf.ap, saP.ap[:, 0:qn], False, False, [self.onesf.name, saP.name], [psS.name])
                self.mm(psS.ap[:, 0:qn], self.onesf.ap, saD.ap[:, 0:qn], False, True, [self.onesf.name, saD.name], [psS.name])
                rs = rsR.get()
                rsa = rs.ap[:, 0:qn]
                pss = psS.ap[:, 0:qn]
                self.P.add("dve", lambda e, rsa=rsa, pss=pss: e.reciprocal(rsa, pss), [psS.name], [rs.name])
                self.tt(Oh.ap[:, q0:q0 + qn], psO.ap[:, 0:qn], rsa, ALU.mult, [psO.name, rs.name],
                        [Oh.name + "#%d" % qi])
                if qi < 4:
                    for f_ in nxt_ops[qi * per:(qi + 1) * per]:
                        f_()
            self.dma(oall[hh * 128:(hh + 1) * 128, :], Oh.ap, [Oh.name], ["oall%d#%d" % (l, hh)])

    def phaseA_diff(self, l):
        hT = self.dr["hT%d" % l]
        ksk = self.dr["ksk%d" % l]
        ksv = self.dr["ksv%d" % l]
        qTd = self.dr["qT%d" % l]
        wqk = self.dr["diff_wqk"]
        wv = self.dr["diff_wv"]
        self.phase(self.base)
        vec = self.load_vecs(l)
        hR = Rot([self.alloc("h%d" % i, [8, SBW], F32) for i in range(2)])
        nb = (self.alloc("sq", [8, SBW], BF16), self.alloc("rstd", [SBW], F32), self.alloc("u", [8, SBW], BF16))
        qkR = Rot([self.alloc("qk%d" % i, [8, SBW], BF16) for i in range(2)])
        wrot = Rot([self.alloc("w%d" % i, [8, 128], BF16) for i in range(3)])
        wvR = Rot([self.alloc("wv%d" % i, [8, 512], BF16) for i in range(2)])
        vtR = Rot([self.alloc("vt%d" % i, [1024], BF16) for i in range(3)])
        wvt = [self.wload(wvR, wv[hf], [8, 512]) for hf in range(2)]
        hn = self.load_h(hT, SUPER[0][0], SUPER[0][1], hR.get())
        for k, (s0, subs) in enumerate(SUPER):
            W = sum(n for _, n in subs)
            h = hn
            if k + 1 < len(SUPER):
                hn = self.load_h(hT, SUPER[k + 1][0], SUPER[k + 1][1], hR.get())
            u = self.norm_to_bf16(h, subs, vec, 0, 8, 1024, nb)
            for part, dst in ((0, qTd), (1, ksk)):
                qk = qkR.get()
                for m in range(8):
                    wt = self.wload(wrot, wqk[part * 8 + m], [8, 128])
                    for si, (c0, n) in enumerate(subs):
                        ps = self.psr.get()
                        for kc in range(8):
                            self.mm(ps.ap[:, 0:n], wt.ap[:, kc, :], u.ap[:, kc, c0:c0 + n], kc == 0, kc == 7,
                                    [wt.name, u.name + "#%d" % si], [ps.name])
                        self.copy(qk.ap[:, m, c0:c0 + n], ps.ap[:, 0:n], [ps.name], [qk.name + "#%d" % si])
                self.dma(dst.rearrange("(c p) n -> p c n", p=128)[:, :, s0:s0 + W], qk.ap[:, :, 0:W],
                         [qk.name], ["%s%d#%d" % ("qT" if part == 0 else "ksk", l, s0)])
            toks = []
            for si, (c0, n) in enumerate(subs):
                for a_ in range(0, n, 128):
                    toks.append((si, c0 + a_, min(128, n - a_)))
            for (si, a0, an) in toks:
                vt = vtR.get()
                for hf in range(2):
                    ps = self.psr.get()
                    for kc in range(8):
                        self.mm(ps.ap[0:an, :], u.ap[:, kc, a0:a0 + an], wvt[hf].ap[:, kc, :], kc == 0, kc == 7,
                                [wvt[hf].name, u.name + "#%d" % si], [ps.name])
                    self.copy(vt.ap[0:an, hf * 512:(hf + 1) * 512], ps.ap[0:an, :], [ps.name], [vt.name])
                self.dma(ksv[s0 + a0:s0 + a0 + an, :], vt.ap[0:an, :], [vt.name], ["ksv%d#%d" % (l, s0 + a0)])

    def phaseB_diff(self, l):
        kf = self.dr["ksfk%d" % l]
        vf = self.dr["ksfv%d" % l]
        qTd = self.dr["qT%d" % l]
        oall = self.dr["oall%d" % l]
        lambda_init = 0.8 - 0.6 * math.exp(-0.3 * l)
        self.phase(self.base)
        vec = self.load_vecs(l)
        ttr = self.alloc("ttr", [6032], F32)
        ttm = self.alloc("ttm", [4000], F32)
        lp = self.alloc("lp", [256], F32)
        lt = self.alloc("lt", [8], F32)
        self.dma(ttr.ap, self.dr["ttr"], [], [ttr.name])
        self.dma(ttm.ap, self.dr["ttm"], [], [ttm.name])
        self.dma(lp.ap, self.dr["diff_lam"].partition_broadcast(128), [], [lp.name])
        self.tt(lp.ap[:, 0:64], lp.ap[:, 0:64], lp.ap[:, 64:128], ALU.mult, [lp.name], [lp.name])
        self.tt(lp.ap[:, 128:192], lp.ap[:, 128:192], lp.ap[:, 192:256], ALU.mult, [lp.name], [lp.name])
        a0 = lt.ap[:, 0:1]
        a1 = lt.ap[:, 1:2]
        l0 = lp.ap[:, 0:64]
        l1 = lp.ap[:, 128:192]
        self.P.add("dve", lambda e: e.reduce_sum(a0, l0, mybir.AxisListType.X), [lp.name], [lt.name])
        self.P.add("dve", lambda e: e.reduce_sum(a1, l1, mybir.AxisListType.X), [lp.name], [lt.name])
        self.act(lt.ap[:, 2:4], lt.ap[:, 0:2], AF.Exp, [lt.name], [lt.name])
        self.tt(lt.ap[:, 4:5], lt.ap[:, 3:4], lt.ap[:, 2:3], ALU.subtract, [lt.name], [lt.name])
        self.ts(lt.ap[:, 5:6], lt.ap[:, 4:5], -lambda_init, None, ALU.add, None, [lt.name], [lt.name])
        neglam = lt.ap[:, 5:6]
        KhR = Rot([self.alloc("Kh%d" % i, [NFULL], BF16) for i in range(2)])
        VhR = Rot([self.alloc("Vh%d" % i, [33, 128], BF16) for i in range(2)])
        QhR = Rot([(self.alloc("Qz0_%d" % i, [NOWN], BF16), self.alloc("Qz1_%d" % i, [NOWN], BF16)) for i in range(2)])
        for (qa_, qb_) in QhR.items:
            self.memset(qa_.ap[64:128, :], 0.0, [qa_.name])
            self.memset(qb_.ap[0:64, :], 0.0, [qb_.name])
        OhR = Rot([self.alloc("Oh%d" % i, [NOWN], BF16) for i in range(2)])
        PtR = Rot([self.alloc("Pt%d" % i, [512], BF16) for i in range(10)])
        PeR = Rot([self.alloc("Pe%d" % i, [512], BF16) for i in range(6)])
        EhR = Rot([(self.alloc("Er%d" % i, [6032], BF16), self.alloc("Em%d" % i, [4000], BF16)) for i in range(2)])
        accR = Rot([(self.ps(0), self.ps(1)), (self.ps(2), self.ps(3))])
        saR = Rot([self.alloc("sa%d" % i, [512], F32) for i in range(2)])
        o0 = self.alloc("o0", [512], F32)
        o1 = self.alloc("o1", [512], F32)
        r0 = self.alloc("r0", [512], F32)
        r1 = self.alloc("r1", [512], F32)
        osq = self.alloc("osq", [512], BF16)
        orr = self.alloc("orr", [512], F32)
        for hh in range(8):
            slope = 2.0 ** (-(hh + 1))
            Kh, Vh, Qh, Oh = KhR.get(), VhR.get(), QhR.get(), OhR.get()
            Er, Em = EhR.get()
            self.act(Er.ap, ttr.ap, AF.Exp, [ttr.name], [Er.name], scale=-slope)
            self.act(Em.ap, ttm.ap, AF.Exp, [ttm.name], [Em.name], scale=-slope)
            rows = slice(hh * 128, (hh + 1) * 128)
            kc_, ko_ = hh // 2, (hh % 2) * 128
            for r_ in range(2):
                self.dma(Kh.ap[:, r_ * 2048:(r_ + 1) * 2048], kf[kc_, r_ * 256 + ko_:r_ * 256 + ko_ + 128, 0:2048],
                         ["ksfk%d" % l], [Kh.name])
                for i4 in range(4):
                    rws = 512 if i4 < 3 else 528
                    self.dma(Vh.ap[:, r_ * 16 + 4 * i4:r_ * 16 + 4 * i4 + 4, :],
                             vf[i4, r_ * rws:r_ * rws + 512, rows].rearrange("(t p) d -> p t d", p=128),
                             ["ksfv%d" % l], [Vh.name])
            self.dma(Kh.ap[:, 4096:4112], kf[kc_, ko_:ko_ + 128, 2048:2064], ["ksfk%d" % l], [Kh.name])
            self.dma(Vh.ap[0:16, 32, :], vf[3, 512:528, rows], ["ksfv%d" % l], [Vh.name])
            self.dma(Qh[0].ap[0:64, :], qTd[hh * 128:hh * 128 + 64, :], ["qT%d" % l], [Qh[0].name])
            self.dma(Qh[1].ap[64:128, :], qTd[hh * 128 + 64:hh * 128 + 128, :], ["qT%d" % l], [Qh[1].name])
            for qi, (q0, qn) in enumerate(QB):
                for c, (rr, oo) in enumerate(((r0, o0), (r1, o1))):
                    psO, psS = accR.get()
                    LAG = 5
                    pend = []
                    for i in range(33 + LAG):
                        if i < 33:
                            k0, kn = KT[i]
                            base_k = (NMETA + 128 * i) if i < 32 else 0
                            if qi < 4:
                                m0_ = (NMETA + 512 * qi) - base_k + 3968
                                Dta = Er.ap[0:kn, m0_:m0_ + qn]
                                Dtn = Er.name
                            else:
                                m0_ = 3984 - base_k
                                Dta = Em.ap[0:kn, m0_:m0_ + qn]
                                Dtn = Em.name
                            pT = self.psr.get()
                            self.mm(pT.ap[0:kn, 0:qn], Kh.ap[:, k0:k0 + kn],
                                    Qh[c].ap[:, q0:q0 + qn], True, True, [Kh.name, Qh[c].name], [pT.name])
                            pe = PeR.get()
                            self.act(pe.ap[0:kn, 0:qn], pT.ap[0:kn, 0:qn], AF.Exp, [pT.name], [pe.name], scale=0.125)
                            pt = PtR.get()
                            self.tt(pt.ap[0:kn, 0:qn], pe.ap[0:kn, 0:qn], Dta, ALU.mult, [pe.name, Dtn], [pt.name],
                                    eng=("dve" if i % 2 == 0 else "pool"))
                            pend.append((i, pt))
                        if i >= LAG:
                            t, pt = pend.pop(0)
                            k0, kn = KT[t]
                            self.mm(psO.ap[:, 0:qn], Vh.ap[0:kn, t, :], pt.ap[0:kn, 0:qn], t == 0, t == 32,
                                    [Vh.name, pt.name], [psO.name])
                            self.mm(psS.ap[:, 0:qn], self.ones.ap[0:kn, :], pt.ap[0:kn, 0:qn], t == 0, t == 32,
                                    [self.ones.name, pt.name], [psS.name])
                    rra = rr.ap[:, 0:qn]
                    pss = psS.ap[:, 0:qn]
                    self.P.add("dve", lambda e, rra=rra, pss=pss: e.reciprocal(rra, pss), [psS.name], [rr.name])
                    self.tt(oo.ap[:, 0:qn], psO.ap[:, 0:qn], rra, ALU.mult, [psO.name, rr.name], [oo.name])
                self.stt(o0.ap[:, 0:qn], o1.ap[:, 0:qn], neglam, o0.ap[:, 0:qn], ALU.mult, ALU.add,
                         [o0.name, o1.name, lt.name], [o0.name])
                self.act(osq.ap[:, 0:qn], o0.ap[:, 0:qn], AF.Square, [o0.name], [osq.name])
                ps = self.psr.get()
                self.mm(ps.ap[:, 0:qn], self.ones.ap, osq.ap[:, 0:qn], True, True, [osq.name, self.ones.name], [ps.name])
                self.act(orr.ap[:, 0:qn], ps.ap[:, 0:qn], AF.Sqrt, [ps.name, self.cst.name], [orr.name],
                         bias=self.eps, scale=1.0 / 128)
                ora = orr.ap[:, 0:qn]
                self.P.add("dve", lambda e, ora=ora: e.reciprocal(ora, ora), [orr.name], [orr.name])
                self.tt(o0.ap[:, 0:qn], o0.ap[:, 0:qn], ora, ALU.mult, [o0.name, orr.name], [o0.name])
                self.ts(Oh.ap[:, q0:q0 + qn], o0.ap[:, 0:qn], vec.ap[:, 32:33], 1.0 - lambda_init, ALU.mult, ALU.mult,
                        [o0.name, vec.name], [Oh.name + "#%d" % qi])
            self.dma(oall[rows, :], Oh.ap, [Oh.name], ["oall%d#%d" % (l, hh)])

    def phaseA_lru(self, l):
        hT = self.dr["hT%d" % l]
        ksx = self.dr["ksx%d" % l]
        ggd = self.dr["gg%d" % l]
        win = self.dr["lru_win"]
        self.phase(self.base)
        vec = self.load_vecs(l)
        hR = Rot([self.alloc("h%d" % i, [8, SBW], F32) for i in range(2)])
        nb = (self.alloc("sq", [8, SBW], BF16), self.alloc("rstd", [SBW], F32), self.alloc("u", [8, SBW], BF16))
        wrot = Rot([self.alloc("w%d" % i, [8, 128], BF16) for i in range(3)])
        xoR = Rot([self.alloc("xo%d" % i, [SBW], F32) for i in range(2)])
        goR = Rot([self.alloc("go%d" % i, [SBW], BF16) for i in range(2)])
        gx = self.alloc("gx", [512], F32)
        g2 = self.alloc("g2", [512], F32)
        hn = self.load_h(hT, SUPER[0][0], SUPER[0][1], hR.get())
        for k, (s0, subs) in enumerate(SUPER):
            W = sum(n for _, n in subs)
            h = hn
            if k + 1 < len(SUPER):
                hn = self.load_h(hT, SUPER[k + 1][0], SUPER[k + 1][1], hR.get())
            u = self.norm_to_bf16(h, subs, vec, 0, 8, 1024, nb)
            for m in range(24):
                wt = self.wload(wrot, win[m], [8, 128])
                xo = xoR.get() if m < 12 else goR.get()
                for si, (c0, n) in enumerate(subs):
                    ps = self.psr.get()
                    for kc in range(8):
                        self.mm(ps.ap[:, 0:n], wt.ap[:, kc, :], u.ap[:, kc, c0:c0 + n], kc == 0, kc == 7,
                                [wt.name, u.name + "#%d" % si], [ps.name])
                    if m < 12:
                        self.copy(xo.ap[:, c0:c0 + n], ps.ap[:, 0:n], [ps.name], [xo.name + "#%d" % si])
                    else:
                        gxa = gx.ap[:, 0:n]
                        g2a = g2.ap[:, 0:n]
                        self.copy(gxa, ps.ap[:, 0:n], [ps.name], [gx.name], eng="act")
                        self.act(g2a, ps.ap[:, 0:n], AF.Square, [ps.name], [g2.name], scale=math.sqrt(0.044715))
                        self.stt(g2a, g2a, 1.0, gxa, ALU.add, ALU.mult, [g2.name, gx.name], [g2.name])
                        self.act(g2a, g2a, AF.Sigmoid, [g2.name], [g2.name], scale=1.5957691216057308)
                        self.tt(xo.ap[:, c0:c0 + n], g2a, gxa, ALU.mult, [g2.name, gx.name], [xo.name + "#%d" % si],
                                eng="pool")
                if m < 12:
                    self.dma(ksx[m * 128:(m + 1) * 128, s0:s0 + W], xo.ap[:, 0:W], [xo.name], ["ksx%d#%d_%d" % (l, m, s0)])
                else:
                    mm_ = m - 12
                    self.dma(ggd[mm_ * 128:(mm_ + 1) * 128, s0:s0 + W], xo.ap[:, 0:W], [xo.name],
                             ["gg%d#%d_%d" % (l, mm_, s0)])

    def phaseB_lru(self, l):
        xf = self.dr["ksfx%d" % l]
        ggd = self.dr["gg%d" % l]
        oall = self.dr["oall%d" % l]
        wg = self.dr["lru_wg"]
        self.phase(self.base)
        vec = self.load_vecs(l)
        cn = self.alloc("cneg", [24], F32)
        cn2 = self.alloc("cneg2", [24], F32)
        self.act(cn.ap, vec.ap[:, 140:164], AF.Exp, [vec.name], [cn.name], scale=-1.0)
        self.act(cn.ap, cn.ap, AF.Ln, [cn.name, self.cst.name], [cn.name], bias=self.one, scale=1.0)
        self.ts(cn.ap, cn.ap, -8.0, None, ALU.mult, None, [cn.name], [cn.name])
        self.ts(cn2.ap, cn.ap, 2.0, None, ALU.mult, None, [cn.name], [cn2.name])
        L = NFULL
        xb = self.alloc("xb", [2, L + 3], F32)
        xc = self.alloc("xc", [2, L], F32)
        xcb = self.alloc("xcb", [2, L], BF16)
        taA = self.alloc("ta", [L], F32)
        tiA = self.alloc("ti", [L], F32)
        hsA = self.alloc("hsA", [L], F32)
        hsB = self.alloc("hsB", [L], F32)
        yo = self.alloc("yo", [NOWN], F32)
        ggt = self.alloc("ggt", [NOWN], BF16)
        yb = self.alloc("yb", [NOWN], BF16)
        wgR = Rot([self.alloc("wg%d" % i, [2, 128], BF16) for i in range(4)])
        m0 = self.cst.ap[:, 2:3]
        m1 = self.cst.ap[:, 3:4]
        setA = (taA.ap, taA.name, tiA.ap, tiA.name, hsA)
        setB = (xb.ap[:, 0, 0:L], xb.name + "#0", xb.ap[:, 1, 0:L], xb.name + "#1", hsB)
        for b in range(6):
            for c2 in range(2):
                ct = 2 * b + c2
                xn = xb.name + "#%d" % c2
                self.memset(xb.ap[:, c2, 0:2], 0.0, [xn])
                self.memset(xb.ap[:, c2, L + 2:L + 3], 0.0, [xn])
                self.dma(xb.ap[:, c2, 2:18], xf[ct, 0:128, 2048:2064], ["ksfx%d" % l], [xn])
                self.dma(xb.ap[:, c2, 18:2066], xf[ct, 0:128, 0:2048], ["ksfx%d" % l], [xn])
                self.dma(xb.ap[:, c2, 2066:4114], xf[ct, 128:256, 0:2048], ["ksfx%d" % l], [xn])
                xca = xc.ap[:, c2, :]
                self.act(xca, xb.ap[:, c2, 0:L], AF.Identity, [xn, vec.name], [xc.name + "#%d" % c2],
                         bias=vec.ap[:, 80 + ct:81 + ct], scale=vec.ap[:, 32 + ct:33 + ct])
                for jj in range(1, 4):
                    col = 32 + jj * 12 + ct
                    self.stt(xca, xb.ap[:, c2, jj:jj + L], vec.ap[:, col:col + 1], xca, ALU.mult, ALU.add,
                             [xn, vec.name, xc.name + "#%d" % c2], [xc.name + "#%d" % c2])
                self.copy(xcb.ap[:, c2, :], xca, [xc.name + "#%d" % c2], [xcb.name + "#%d" % c2], eng="dve")
            for c2 in range(2):
                ct = 2 * b + c2
                for d in range(2):
                    ta_ap, ta_n, ti_ap, ti_n, hs = setA if d == 0 else setB
                    wr = self.wload(wgR, wg[((d * 2 + 0) * 6 + b) * 2 + c2], [2, 128])
                    wi = self.wload(wgR, wg[((d * 2 + 1) * 6 + b) * 2 + c2], [2, 128])
                    br = 92 + (d * 2 + 0) * 12 + ct
                    bi = 92 + (d * 2 + 1) * 12 + ct
                    for (c0, n) in KB:
                        for (wt, bcol, dap, dn) in ((wr, br, ta_ap, ta_n), (wi, bi, ti_ap, ti_n)):
                            ps = self.psr.get()
                            for kc in range(2):
                                self.mm(ps.ap[:, 0:n], wt.ap[:, kc, :], xcb.ap[:, kc, c0:c0 + n], kc == 0, kc == 1,
                                        [wt.name, xcb.name], [ps.name])
                            self.act(dap[:, c0:c0 + n], ps.ap[:, 0:n], AF.Sigmoid, [ps.name, vec.name],
                                     [dn], bias=vec.ap[:, bcol:bcol + 1], scale=1.0)
                    ccol = d * 12 + ct
                    self.act(hs.ap, ta_ap, AF.Exp, [ta_n, cn2.name], [hs.name], scale=cn2.ap[:, ccol:ccol + 1])
                    self.act(hs.ap, hs.ap, AF.Sqrt, [hs.name, self.cst.name], [hs.name], bias=self.one, scale=-1.0)
                    self.act(ta_ap, ta_ap, AF.Exp, [ta_n, cn.name], [ta_n], scale=cn.ap[:, ccol:ccol + 1])
                    self.tt(ti_ap, ti_ap, xc.ap[:, c2, :], ALU.mult, [ti_n, xc.name + "#%d" % c2], [ti_n])
                    self.tt(ti_ap, ti_ap, hs.ap, ALU.mult, [ti_n, hs.name], [ti_n])
                    dst = hs
                    if d == 0:
                        o_, a_, u_ = dst.ap, ta_ap, ti_ap
                    else:
                        o_, a_, u_ = dst.ap[:, ::-1], ta_ap[:, ::-1], ti_ap[:, ::-1]
                    self.P.add("dve", lambda e, o_=o_, a_=a_, u_=u_: e.tensor_tensor_scan(o_, a_, u_, 0.0, ALU.mult, ALU.add),
                               [ta_n, ti_n], [dst.name])
                self.ts(yo.ap[:, 0:2048], hsA.ap[:, 16:2064], m0, None, ALU.mult, None, [hsA.name, self.cst.name], [yo.name])
                self.stt(yo.ap[:, 0:2048], hsB.ap[:, 16:2064], m0, yo.ap[:, 0:2048], ALU.mult, ALU.add,
                         [hsB.name, yo.name, self.cst.name], [yo.name])
                self.stt(yo.ap[:, 0:2048], hsA.ap[:, 2064:4112], m1, yo.ap[:, 0:2048], ALU.mult, ALU.add,
                         [hsA.name, yo.name, self.cst.name], [yo.name])
                self.stt(yo.ap[:, 0:2048], hsB.ap[:, 2064:4112], m1, yo.ap[:, 0:2048], ALU.mult, ALU.add,
                         [hsB.name, yo.name, self.cst.name], [yo.name])
                self.tt(yo.ap[:, 2048:2064], hsA.ap[:, 0:16], hsB.ap[:, 0:16], ALU.add, [hsA.name, hsB.name], [yo.name])
                rows = slice(ct * 128, (ct + 1) * 128)
                self.dma(ggt.ap, ggd[rows, :], ["gg%d" % l], [ggt.name])
                self.tt(yb.ap, yo.ap, ggt.ap, ALU.mult, [yo.name, ggt.name], [yb.name], eng="pool")
                self.dma(oall[rows, :], yb.ap, [yb.name], ["oall%d#%d" % (l, ct)])

    def phaseC(self, l, KCo, wo_name):
        hT = self.dr["hT%d" % l]
        hTo = self.dr["hT%d" % (l + 1)]
        oall = self.dr["oall%d" % l]
        wo = self.dr[wo_name]
        w1 = self.dr["ffn_w1"]
        w2 = self.dr["ffn_w2"]
        self.phase(self.base)
        vec = self.load_vecs(l)
        hR = Rot([self.alloc("h%d" % i, [8, SBW], F32) for i in range(2)])
        u1 = self.alloc("u1", [8, SBW], F32)
        sq = self.alloc("sqc", [8, SBW], BF16)
        rstd = self.alloc("rstdc", [SBW], F32)
        u3 = self.alloc("u3", [8, SBW], BF16)
        hid = self.alloc("hid", [22, SBW], BF16)
        oin = self.alloc("oin", [KCo, SBW], BF16)
        sgR = Rot([self.alloc("sg%d" % i, [512], F32) for i in range(2)])
        w1R = Rot([self.alloc("w1_%d" % i, [8, 256], BF16) for i in range(2)])
        w2R = Rot([self.alloc("w2_%d" % i, [22, 128], BF16) for i in range(2)])
        woR = Rot([self.alloc("wo_%d" % i, [KCo, 128], BF16) for i in range(2)])
        src = oall.rearrange("(c p) n -> p c n", p=128)
        dst = hTo.rearrange("(c p) n -> p c n", p=128)

        def load_oin(s0, W):
            for kc in range(KCo):
                self.dma(oin.ap[:, kc, 0:W], src[:, kc, s0:s0 + W], ["oall%d" % l], [oin.name])

        hn = self.load_h(hT, SUPER[0][0], SUPER[0][1], hR.get())
        load_oin(SUPER[0][0], SBW)
        for k, (s0, subs) in enumerate(SUPER):
            h = hn
            for m in range(8):
                wt = self.wload(woR, wo[JIDX[l], m] if wo_name == "mla_wo" else wo[m], [KCo, 128])
                for si, (c0, n) in enumerate(subs):
                    ps = self.psr.get()
                    for kc in range(KCo):
                        self.mm(ps.ap[:, 0:n], wt.ap[:, kc, :], oin.ap[:, kc, c0:c0 + n], kc == 0, kc == KCo - 1,
                                [wt.name, oin.name], [ps.name])
                    self.copy(u1.ap[:, m, c0:c0 + n], ps.ap[:, 0:n], [ps.name], [u1.name + "#%d" % si])
            if k + 1 < len(SUPER):
                hn = self.load_h(hT, SUPER[k + 1][0], SUPER[k + 1][1], hR.get())
                load_oin(SUPER[k + 1][0], SBW)
            self.resid_norm(h, u1, subs, sq, rstd, vec, 8)
            self.rms_stats(h, 8, subs, sq, rstd, 1024)
            for si, (c0, n) in enumerate(subs):
                for c in range(8):
                    self.stt(u3.ap[:, c, c0:c0 + n], h.ap[:, c, c0:c0 + n], vec.ap[:, 16 + c:17 + c],
                             rstd.ap[:, c0:c0 + n], ALU.mult, ALU.mult,
                             [h.name + "#%d" % si, rstd.name + "#%d" % si, vec.name], [u3.name + "#%d" % si])
            for jn in range(22):
                wt = self.wload(w1R, w1[l, jn], [8, 256])
                for si, (c0, n) in enumerate(subs):
                    pg = self.psr.get()
                    for kc in range(8):
                        self.mm(pg.ap[:, 0:n], wt.ap[:, kc, 0:128], u3.ap[:, kc, c0:c0 + n], kc == 0, kc == 7,
                                [wt.name, u3.name + "#%d" % si], [pg.name])
                    pu = self.psr.get()
                    for kc in range(8):
                        self.mm(pu.ap[:, 0:n], wt.ap[:, kc, 128:256], u3.ap[:, kc, c0:c0 + n], kc == 0, kc == 7,
                                [wt.name, u3.name + "#%d" % si], [pu.name])
                    sg = sgR.get()
                    self.act(sg.ap[:, 0:n], pg.ap[:, 0:n], AF.Silu, [pg.name], [sg.name])
                    self.tt(hid.ap[:, jn, c0:c0 + n], sg.ap[:, 0:n], pu.ap[:, 0:n], ALU.mult, [sg.name, pu.name],
                            [hid.name + "#%d" % si])
            for m in range(8):
                wt = self.wload(w2R, w2[l, m], [22, 128])
                for si, (c0, n) in enumerate(subs):
                    ps = self.psr.get()
                    for kc in range(22):
                        self.mm(ps.ap[:, 0:n], wt.ap[:, kc, :], hid.ap[:, kc, c0:c0 + n], kc == 0, kc == 21,
                                [wt.name, hid.name + "#%d" % si], [ps.name])
                    self.copy(u1.ap[:, m, c0:c0 + n], ps.ap[:, 0:n], [ps.name], [u1.name + "#%d" % si])
            self.resid_norm(h, u1, subs, sq, rstd, vec, 24)
            for si, (c0, n) in enumerate(subs):
                self.dma(dst[:, :, s0 + c0:s0 + c0 + n], h.ap[:, :, c0:c0 + n], [h.name + "#%d" % si],
                         ["hT%d#%d" % (l + 1, s0 + c0)])

    def resid_norm(self, h, u1, subs, sq, rstd, vec, gcol):
        self.rms_stats(u1, 8, subs, sq, rstd, 1024)
        for si, (c0, n) in enumerate(subs):
            for c in range(8):
                ua = u1.ap[:, c, c0:c0 + n]
                self.tt(ua, ua, rstd.ap[:, c0:c0 + n], ALU.mult, [u1.name + "#%d" % si, rstd.name + "#%d" % si],
                        [u1.name + "#%d" % si], eng="pool")
                ha = h.ap[:, c, c0:c0 + n]
                self.stt(ha, ua, vec.ap[:, gcol + c:gcol + c + 1], ha, ALU.mult, ALU.add,
                         [u1.name + "#%d" % si, h.name + "#%d" % si, vec.name], [h.name + "#%d" % si])


WEIGHT_SHAPES = {
    "vecs": ([4, 128, NV], F32),
    "cst": ([128, 4], F32),
    "cs": ([128, NOWN], F32),
    "ttr": ([128, 6032], F32),
    "ttm": ([128, 4000], F32),
    "mla_win": ([2, 6, 128, 8, 128], F32),
    "mla_wuq": ([2, 16, 128, 3, 256], F32),
    "mla_wukv": ([2, 16, 128, 2, 256], F32),
    "mla_wo": ([2, 8, 128, 16, 128], F32),
    "diff_wqk": ([16, 128, 8, 128], F32),
    "diff_wv": ([2, 128, 8, 512], F32),
    "diff_lam": ([1, 256], F32),
    "diff_wo": ([8, 128, 8, 128], F32),
    "lru_win": ([24, 128, 8, 128], F32),
    "lru_wg": ([48, 128, 2, 128], F32),
    "lru_wo": ([8, 128, 12, 128], F32),
    "ffn_w1": ([4, 22, 128, 8, 256], F32),
    "ffn_w2": ([4, 8, 128, 22, 128], F32),
}


def act_shapes(l):
    k = KINDS[l]
    d = {"hT%d" % l: ([D, NOWN], F32), "hT%d" % (l + 1): ([D, NOWN], F32)}
    if k == "mla":
        d.update({"ks%d" % l: ([320, NOWN], BF16), "cqn%d" % l: ([384, NOWN], BF16),
                  "ksf%d" % l: ([2, 320, NOWN], BF16), "oall%d" % l: ([2048, NOWN], BF16)})
    elif k == "diff":
        d.update({"ksk%d" % l: ([D, NOWN], BF16), "ksv%d" % l: ([NOWN, D], BF16), "qT%d" % l: ([D, NOWN], BF16),
                  "ksfk%d" % l: ([4, 512, NOWN], BF16), "ksfv%d" % l: ([4, 1056, D], BF16),
                  "oall%d" % l: ([D, NOWN], BF16)})
    else:
        d.update({"ksx%d" % l: ([1536, NOWN], F32), "gg%d" % l: ([1536, NOWN], BF16),
                  "ksfx%d" % l: ([12, 256, NOWN], F32), "oall%d" % l: ([1536, NOWN], BF16)})
    return d


def build_program(stages, ext_in, ext_out):
    nc = bass.Bass("TRN2", target_bir_lowering=False)
    with ExitStack() as st:
        b = Bld(nc, st)
        shapes = {}
        for (_, l) in stages:
            shapes.update(act_shapes(l))
        used_w = set(["vecs", "cst"])
        for (ph, l) in stages:
            k = KINDS[l]
            if ph == "A":
                used_w |= {"mla": {"mla_win", "cs"}, "diff": {"diff_wqk", "diff_wv"}, "lru": {"lru_win"}}[k]
            elif ph == "B":
                used_w |= {"mla": {"mla_wuq", "mla_wukv", "cs"}, "diff": {"diff_lam", "ttr", "ttm"}, "lru": {"lru_wg"}}[k]
            elif ph == "C":
                used_w |= {"ffn_w1", "ffn_w2", {"mla": "mla_wo", "diff": "diff_wo", "lru": "lru_wo"}[k]}
        for name in sorted(used_w):
            shp, dt = WEIGHT_SHAPES[name]
            b.dram(name, shp, dt, "ExternalInput")
        needed = set()
        for (ph, l) in stages:
            k = KINDS[l]
            if ph == "A":
                needed |= {"hT%d" % l} | {"mla": {"ks%d" % l, "cqn%d" % l}, "diff": {"ksk%d" % l, "ksv%d" % l, "qT%d" % l},
                                          "lru": {"ksx%d" % l, "gg%d" % l}}[k]
            elif ph == "B":
                needed |= {"oall%d" % l} | {"mla": {"ksf%d" % l, "cqn%d" % l}, "diff": {"ksfk%d" % l, "ksfv%d" % l, "qT%d" % l},
                                            "lru": {"ksfx%d" % l, "gg%d" % l}}[k]
            elif ph == "C":
                needed |= {"oall%d" % l, "hT%d" % l, "hT%d" % (l + 1)}
            elif ph == "X":
                needed |= {"mla": {"ks%d" % l, "ksf%d" % l}, "diff": {"ksk%d" % l, "ksv%d" % l, "ksfk%d" % l, "ksfv%d" % l},
                           "lru": {"ksx%d" % l, "ksfx%d" % l}}[k]
        for name in sorted(needed):
            shp, dt = shapes[name]
            kind = "ExternalInput" if name in ext_in else ("ExternalOutput" if name in ext_out else "Internal")
            b.dram(name, shp, dt, kind)
        b.setup_consts(b.dr["cst"])
        for (ph, l) in stages:
            k = KINDS[l]
            if ph == "A":
                getattr(b, "phaseA_" + k)(l)
            elif ph == "B":
                getattr(b, "phaseB_" + k)(l)
            elif ph == "X":
                b.exchange(l)
            elif ph == "C":
                b.phaseC(l, {"mla": 16, "diff": 8, "lru": 12}[k], {"mla": "mla_wo", "diff": "diff_wo", "lru": "lru_wo"}[k])
        b.P.barrier()
        b.P.add("sp", lambda e: None, [], [])
        b.P.emit()
    return nc, sorted(used_w)


def tile_w(w, KC, mw):
    K, M = w.shape
    assert K == KC * 128 and M % mw == 0
    return np.ascontiguousarray(w.reshape(KC, 128, M // mw, mw).transpose(2, 1, 0, 3))


def col128(v):
    return v.reshape(-1, 128).T


def prep_weights(inp):
    f = np.float32
    W = {}
    vecs = np.zeros((4, 128, NV), f)
    for l in range(4):
        for k in range(4):
            vecs[l, :, 8 * k:8 * k + 8] = col128(inp["norm_g"][l, k])
        kind, j = KINDS[l], JIDX[l]
        if kind == "mla":
            vecs[l, :, 32:35] = col128(inp["mla_q_norm"][j])
            vecs[l, :, 35:37] = col128(inp["mla_kv_norm"][j])
        elif kind == "diff":
            vecs[l, :, 32:33] = col128(inp["diff_subln"][j])
        else:
            for jj in range(4):
                vecs[l, :, 32 + jj * 12:44 + jj * 12] = col128(inp["lru_conv_w"][j, jj])
            vecs[l, :, 80:92] = col128(inp["lru_conv_b"][j])
            for d in range(2):
                for g in range(2):
                    c0 = 92 + (d * 2 + g) * 12
                    vecs[l, :, c0:c0 + 12] = col128(inp["lru_b_gates"][j, d, g])
                vecs[l, :, 140 + d * 12:152 + d * 12] = col128(inp["lru_lambda"][j, d])
    W["vecs"] = vecs
    sw = np.concatenate([np.arange(32, 64), np.arange(0, 32)])
    win = []
    wuq = []
    for j in range(2):
        w = inp["mla_w_in"][j]
        ext = np.concatenate([w, w[:, 640:704][:, sw]], axis=1)
        win.append(tile_w(ext, 8, 128))
        q = inp["mla_w_uq"][j].reshape(384, 16, 192)
        qe = np.concatenate([q, q[:, :, 128:192][:, :, sw]], axis=2)
        wuq.append(tile_w(qe.reshape(384, 4096), 3, 256))
    W["mla_win"] = np.stack(win)
    W["mla_wuq"] = np.stack(wuq)
    W["mla_wukv"] = np.stack([tile_w(inp["mla_w_ukv"][j], 2, 256) for j in range(2)])
    W["mla_wo"] = np.stack([tile_w(inp["mla_w_o"][j], 16, 128) for j in range(2)])
    dw = inp["diff_w_in"][0]
    W["diff_wqk"] = tile_w(dw[:, 0:2048], 8, 128)
    W["diff_wv"] = tile_w(dw[:, 2048:3072], 8, 512)
    W["diff_lam"] = np.ascontiguousarray(inp["diff_lambda"][0].reshape(1, 256))
    W["diff_wo"] = tile_w(inp["diff_w_o"][0], 8, 128)
    W["lru_win"] = tile_w(inp["lru_w_in"][0], 8, 128)
    wg = inp["lru_w_gates"][0]
    wg = wg.reshape(2, 2, 6, 2, 128, 2, 128)
    W["lru_wg"] = np.ascontiguousarray(wg.transpose(0, 1, 2, 5, 4, 3, 6)).reshape(48, 128, 2, 128)
    W["lru_wo"] = tile_w(inp["lru_w_o"][0], 12, 128)
    w1 = []
    w2 = []
    for l in range(4):
        wi = inp["ffn_w_in"][l]
        g = wi[:, :2816].reshape(1024, 22, 128)
        u = wi[:, 2816:].reshape(1024, 22, 128)
        w1.append(tile_w(np.concatenate([g, u], axis=2).reshape(1024, 22 * 256), 8, 256))
        w2.append(tile_w(inp["ffn_w_out"][l], 22, 128))
    W["ffn_w1"] = np.stack(w1)
    W["ffn_w2"] = np.stack(w2)
    return {k: np.ascontiguousarray(v, dtype=f) for k, v in W.items()}


def core_consts(r):
    f = np.float32
    pos = np.concatenate([NMETA + NREAL * r + np.arange(NREAL), np.arange(NMETA)]).astype(f)
    inv = (10000.0 ** (-np.arange(0, 64, 2, dtype=f) / 64)).astype(f)
    ang = pos[None, :] * inv[:, None]
    c, s = np.cos(ang).astype(f), np.sin(ang).astype(f)
    cs = np.concatenate([c, c, -s, s], axis=0)
    p = np.arange(128, dtype=np.float64)[:, None]
    ttr = np.abs(np.arange(6032, dtype=np.float64)[None, :] - 3968 - p + NREAL * r).astype(f)
    ttm = np.abs(np.arange(4000, dtype=np.float64)[None, :] - 3984 - p).astype(f)
    cst = np.zeros((128, 4), f)
    cst[:, 0] = EPS
    cst[:, 1] = 1.0
    cst[:, 2 + r] = 1.0
    return {"cs": cs, "ttr": ttr, "ttm": ttm, "cst": cst}


LAUNCHES = [
    ([("A", 0)], ["hT0"], ["ks0", "cqn0"]),
    ([("B", 0), ("C", 0), ("A", 1)], ["ksf0", "cqn0", "hT0"], ["hT1", "ksk1", "ksv1", "qT1"]),
    ([("B", 1), ("C", 1), ("A", 2)], ["ksfk1", "ksfv1", "qT1", "hT1"], ["hT2", "ksx2", "gg2"]),
    ([("B", 2), ("C", 2), ("A", 3)], ["ksfx2", "gg2", "hT2"], ["hT3", "ks3", "cqn3"]),
    ([("B", 3), ("C", 3)], ["ksf3", "cqn3", "hT3"], ["hT4"]),
]
FUSED_LAUNCH = ([(ph, l) for l in range(4) for ph in ("A", "X", "B", "C")], ["hT0"], ["hT4"])
FUSED = True
EXCH = {"ks0": "ksf0", "ksk1": "ksfk1", "ksv1": "ksfv1", "ksx2": "ksfx2", "ks3": "ksf3"}

_PROG_CACHE = {}


def run_launch(li, W, consts, state, ncores):
    stages, ext_in, ext_out = FUSED_LAUNCH if li == "fused" else LAUNCHES[li]
    if li not in _PROG_CACHE:
        _PROG_CACHE[li] = build_program(stages, ext_in, ext_out)
    nc, used_w = _PROG_CACHE[li]
    in_maps = []
    for c in range(ncores):
        m = {}
        for name in used_w:
            m[name] = consts[c % 2][name] if name in consts[0] else W[name]
        for name in ext_in:
            m[name] = state[name][c]
        in_maps.append(m)
    res = run_bass_kernel_spmd(nc, in_maps, core_ids=list(range(ncores)))
    for name in ext_out:
        state[name] = [res.results[c][name] for c in range(ncores)]
    for name in ext_out:
        if name in EXCH:
            full = []
            for c in range(ncores):
                p = c // 2
                full.append(np.stack([state[name][2 * p], state[name][2 * p + 1]]))
            state[EXCH[name]] = full


def kernel(**inp):
    inp = {k: np.asarray(v) for k, v in inp.items()}
    x = inp["x"]
    Bn = x.shape[0]
    ncores = 2 * Bn
    W = prep_weights(inp)
    consts = [core_consts(0), core_consts(1)]
    meta = inp["meta_tokens"].astype(np.float32)
    state = {"hT0": []}
    for c in range(ncores):
        b, r = c // 2, c % 2
        own = np.concatenate([x[b, r * NREAL:(r + 1) * NREAL], meta], axis=0)
        state["hT0"].append(np.ascontiguousarray(own.T))
    if FUSED:
        run_launch("fused", W, consts, state, ncores)
    else:
        for li in range(len(LAUNCHES)):
            run_launch(li, W, consts, state, ncores)
    out = np.zeros((Bn, 2 * NREAL, D), np.float32)
    for c in range(ncores):
        b, r = c // 2, c % 2
        out[b, r * NREAL:(r + 1) * NREAL] = state["hT4"][c][:, 0:NREAL].T
    return out
```

```python
import math
from contextlib import ExitStack

import numpy as np
import ml_dtypes
import concourse.bass as bass
import concourse.mybir as mybir
from concourse.bass_utils import run_bass_kernel_spmd

F32 = mybir.dt.float32
BF16 = mybir.dt.bfloat16
ALU = mybir.AluOpType
AF = mybir.ActivationFunctionType
NPBF = ml_dtypes.bfloat16

D = 1024
NREAL = 2048
NMETA = 16
NOWN = NREAL + NMETA
NFULL = 2 * NREAL + NMETA
SBW = 688
SUBS = [(0, 512), (512, 176)]
SUPER = [(0, SUBS), (688, SUBS), (1376, SUBS)]
KT = [(t * 128, 128) for t in range(32)] + [(4096, 16)]
QB = [(0, 512), (512, 512), (1024, 512), (1536, 512), (2048, 16)]
KB = [(i * 512, 512) for i in range(8)] + [(4096, 16)]
NV = 164
AW = 48000
KINDS = ["mla", "diff", "lru", "mla"]
JIDX = [0, 0, 0, 1]
EPS = 1e-6


class Op:
    __slots__ = ("eng", "fn", "deps", "signal", "sigval", "dma", "sem", "cc", "idx")

    def __init__(self, eng, fn, dma, cc=False):
        self.cc = cc
        self.eng = eng
        self.fn = fn
        self.deps = []
        self.signal = dma
        self.sigval = 0
        self.dma = dma
        self.sem = None


class Prog:
    ENGS = ("pe", "act", "dve", "pool", "sp")
    NDMASEM = 8

    def __init__(self, nc):
        self.nc = nc
        self.ops = []
        self.last_w = {}
        self.readers = {}
        self.kids = {}
        self.fence = []
        self.fence_set = set()

    def _keys(self, b):
        if "#" in b:
            par = b.split("#")[0]
            self.kids.setdefault(par, set()).add(b)
            return (b, par)
        return (b,) + tuple(self.kids.get(b, ()))

    def add(self, eng, fn, reads=(), writes=(), dma=False, cc=False):
        op = Op(eng, fn, dma, cc)
        deps = set()
        for b in reads:
            for k in self._keys(b):
                w = self.last_w.get(k)
                if w is not None:
                    deps.add(w)
        for b in writes:
            for k in self._keys(b):
                w = self.last_w.get(k)
                if w is not None:
                    deps.add(w)
                deps.update(self.readers.get(k, ()))
        for b in reads:
            self.readers.setdefault(b, []).append(op)
        for b in writes:
            self.last_w[b] = op
            self.readers[b] = []
            if "#" not in b:
                for k in self.kids.get(b, ()):
                    self.last_w[k] = op
                    self.readers[k] = []
        deps.update(self.fence)
        deps.discard(op)
        latest = {}
        final = []
        for d in deps:
            if d.dma:
                final.append(d)
            else:
                cur = latest.get(d.eng)
                if cur is None or cur.idx < d.idx:
                    latest[d.eng] = d
        final.extend(latest.values())
        for d in final:
            if d.eng == "pe" and eng == "pe" and not d.dma and not dma and d not in self.fence_set:
                continue
            d.signal = True
            op.deps.append(d)
        op.idx = len(self.ops)
        self.ops.append(op)
        return op

    def barrier(self):
        tails = {}
        for op in self.ops:
            tails[(op.eng, op.dma)] = op
        dm = {}
        for op in self.ops:
            if op.dma:
                dm.setdefault(op.eng, []).append(op)
        fence = list(tails.values())
        for e, lst in dm.items():
            fence.extend(lst[-self.NDMASEM:])
        self.fence = fence
        self.fence_set = set(fence)
        for o in fence:
            o.signal = True

    def emit(self):
        nc = self.nc
        cnt = {e: 0 for e in self.ENGS}
        dma_rr = {e: 0 for e in self.ENGS}
        dma_last = {}
        dma_cnt = {}
        for op in self.ops:
            if op.cc:
                k = ("cc", 0)
                dma_cnt[k] = dma_cnt.get(k, 0) + 1
                op.sem = k
                op.sigval = dma_cnt[k]
            elif op.dma:
                k = (op.eng, dma_rr[op.eng] % self.NDMASEM)
                dma_rr[op.eng] += 1
                prev = dma_last.get(k)
                if prev is not None:
                    op.deps.append(prev)
                dma_last[k] = op
                dma_cnt[k] = dma_cnt.get(k, 0) + 16
                op.sem = k
                op.sigval = dma_cnt[k]
            elif op.signal:
                cnt[op.eng] += 1
                op.sem = op.eng
                op.sigval = cnt[op.eng]
        with ExitStack() as st:
            sems = {}
            for e in self.ENGS:
                sems[e] = st.enter_context(nc.semaphore("s_" + e))
            for k in dma_cnt:
                sems[k] = st.enter_context(nc.semaphore("d_%s%d" % k))
            block = st.enter_context(nc.Block())
            engobj = {"pe": block.tensor, "act": block.scalar, "dve": block.vector,
                      "pool": block.gpsimd, "sp": block.sync}
            for e in self.ENGS:
                myops = [op for op in self.ops if op.eng == e]
                if not myops:
                    continue

                def body(eng, myops=myops):
                    waited = {}
                    for op in myops:
                        need = {}
                        for d in op.deps:
                            if need.get(d.sem, 0) < d.sigval:
                                need[d.sem] = d.sigval
                        for k, v in need.items():
                            if waited.get(k, 0) < v:
                                eng.wait_ge(sems[k], v)
                                waited[k] = v
                        ins = op.fn(eng)
                        if op.signal:
                            ins.then_inc(sems[op.sem], 16 if (op.dma and not op.cc) else 1)

                engobj[e](body)


class T:
    def __init__(self, ap, name):
        self.ap = ap
        self.name = name


class Rot:
    def __init__(self, items):
        self.items = items
        self.i = 0

    def get(self):
        t = self.items[self.i % len(self.items)]
        self.i += 1
        return t


class Bld:
    def __init__(self, nc, st):
        self.nc = nc
        self.P = Prog(nc)
        self.arena = st.enter_context(nc.sbuf_tensor("arena", [128, AW], F32))
        self.psum = st.enter_context(nc.psum_tensor("psum", [128, 8, 512], F32))
        self.off = 0
        self.gen = 0
        self.dr = {}
        self.ev = 0
        self.psr = Rot([self.ps(i) for i in range(4, 8)])

    def alloc(self, name, shape, dt):
        n = 1
        for s in shape:
            n *= s
        size = 4 if dt == F32 else 2
        words = (n * size + 3) // 4
        words = (words + 7) // 8 * 8
        assert self.off + words <= AW, (name, self.off, words)
        ap = self.arena[:, self.off:self.off + words]
        self.off += words
        if dt != F32:
            ap = ap.bitcast(dt)
        ap = ap[:, 0:n]
        if len(shape) == 2:
            ap = ap.rearrange("p (a b) -> p a b", b=shape[1])
        elif len(shape) == 3:
            ap = ap.rearrange("p (a b c) -> p a b c", b=shape[1], c=shape[2])
        return T(ap, "%s@%d" % (name, self.gen))

    def phase(self, keep):
        self.P.barrier()
        self.off = keep
        self.gen += 1

    def ps(self, i):
        return T(self.psum[:, i, :], "ps%d" % i)

    def dram(self, name, shape, dt, kind):
        t = self.nc.dram_tensor(name, list(shape), dt, kind=kind).ap()
        self.dr[name] = t
        return t

    def mm(self, out, lhsT, rhs, start, stop, r, w):
        self.P.add("pe", lambda e: e.matmul(out, lhsT, rhs, start=start, stop=stop), r, w)

    def act(self, out, in_, func, r, w, bias=None, scale=None):
        kw = {}
        if bias is not None:
            kw["bias"] = bias
        if scale is not None:
            kw["scale"] = scale
        self.P.add("act", lambda e: e.activation(out, in_, func, **kw), r, w)

    def tt(self, out, in0, in1, op, r, w, eng="dve"):
        self.P.add(eng, lambda e: e.tensor_tensor(out, in0, in1, op), r, w)

    def ts(self, out, in0, s1, s2, op0, op1, r, w, eng="dve"):
        if s2 is None:
            self.P.add(eng, lambda e: e.tensor_scalar(out, in0, s1, None, op0), r, w)
        else:
            self.P.add(eng, lambda e: e.tensor_scalar(out, in0, s1, s2, op0, op1), r, w)

    def stt(self, out, in0, scalar, in1, op0, op1, r, w, eng="dve"):
        self.P.add(eng, lambda e: e.scalar_tensor_tensor(out, in0, scalar, in1, op0, op1), r, w)

    def copy(self, out, in_, r, w, eng=None):
        if eng is None:
            eng = "act" if self.ev % 2 == 0 else "dve"
            self.ev += 1
        if eng == "act":
            self.P.add("act", lambda e: e.copy(out, in_), r, w)
        else:
            self.P.add(eng, lambda e: e.tensor_copy(out, in_), r, w)

    def dma(self, out, in_, r, w, q="sp"):
        self.P.add(q, lambda e: e.dma_start(out=out, in_=in_), r, w, dma=True)

    def memset(self, ap, val, w, eng="dve"):
        self.P.add(eng, lambda e: e.memset(ap, val), (), w)

    def setup_consts(self, cst_dram):
        self.ones = self.alloc("ones", [128], BF16)
        self.cst = self.alloc("cst", [4], F32)
        self.memset(self.ones.ap, 1.0, [self.ones.name])
        self.onesf = self.alloc("onesf", [128], F32)
        self.memset(self.onesf.ap, 1.0, [self.onesf.name])
        self.dma(self.cst.ap, cst_dram, [], [self.cst.name])
        self.eps = self.cst.ap[:, 0:1]
        self.one = self.cst.ap[:, 1:2]
        self.base = self.off

    def load_vecs(self, l):
        v = self.alloc("vec", [NV], F32)
        self.dma(v.ap, self.dr["vecs"][l], [], [v.name])
        return v

    def wload(self, rot, src, shape):
        t = rot.get()
        ap = t.ap
        if len(shape) == 2:
            dst = ap[:, 0:shape[0], 0:shape[1]]
        else:
            dst = ap
        self.dma(dst, src, [], [t.name], q="pool")
        return t

    def rms_stats(self, x, C, subs, sq, rstd, dim):
        for si, (c0, n) in enumerate(subs):
            self.act(sq.ap[:, 0:C, c0:c0 + n], x.ap[:, 0:C, c0:c0 + n], AF.Square,
                     [x.name + "#%d" % si], [sq.name + "#%d" % si])
            ps = self.psr.get()
            for c in range(C):
                self.mm(ps.ap[:, 0:n], self.ones.ap, sq.ap[:, c, c0:c0 + n], c == 0, c == C - 1,
                        [sq.name + "#%d" % si, self.ones.name], [ps.name])
            self.act(rstd.ap[:, c0:c0 + n], ps.ap[:, 0:n], AF.Sqrt, [ps.name, self.cst.name],
                     [rstd.name + "#%d" % si], bias=self.eps, scale=1.0 / dim)
            rs = rstd.ap[:, c0:c0 + n]
            self.P.add("dve", lambda e, rs=rs: e.reciprocal(rs, rs), [rstd.name + "#%d" % si],
                       [rstd.name + "#%d" % si])

    def load_h(self, hT, s0, subs, h):
        src = hT.rearrange("(c p) n -> p c n", p=128)
        for si, (c0, n) in enumerate(subs):
            self.dma(h.ap[:, :, c0:c0 + n], src[:, :, s0 + c0:s0 + c0 + n], [], [h.name + "#%d" % si])
        return h

    def norm_to_bf16(self, h, subs, vec, gcol, C, dim, bufs):
        sq, rstd, u = bufs
        self.rms_stats(h, C, subs, sq, rstd, dim)
        for si, (c0, n) in enumerate(subs):
            for c in range(C):
                self.stt(u.ap[:, c, c0:c0 + n], h.ap[:, c, c0:c0 + n], vec.ap[:, gcol + c:gcol + c + 1],
                         rstd.ap[:, c0:c0 + n], ALU.mult, ALU.mult,
                         [h.name + "#%d" % si, rstd.name + "#%d" % si, vec.name], [u.name + "#%d" % si])
        return u

    def exchange(self, l):
        k = KINDS[l]
        self.phase(self.base)
        pairs = []
        if k == "mla":
            pairs.append((self.dr["ks%d" % l], self.dr["ksf%d" % l].rearrange("r p n -> (r p) n")))
        elif k == "diff":
            for i in range(4):
                pairs.append((self.dr["ksk%d" % l][256 * i:256 * (i + 1), :], self.dr["ksfk%d" % l][i]))
            for i in range(4):
                rows = 512 if i < 3 else 528
                pairs.append((self.dr["ksv%d" % l][512 * i:512 * i + rows, :], self.dr["ksfv%d" % l][i, 0:2 * rows, :]))
        else:
            for i in range(12):
                pairs.append((self.dr["ksx%d" % l][128 * i:128 * (i + 1), :], self.dr["ksfx%d" % l][i]))
        for (src, dst) in pairs:
            self.P.add("pool", lambda e, src=src, dst=dst: e.collective_compute(
                "AllGather", ALU.bypass, replica_groups=[[0, 1], [2, 3], [4, 5], [6, 7]], ins=[src], outs=[dst]),
                [], [], dma=True, cc=True)

    def phaseA_mla(self, l):
        j = JIDX[l]
        hT = self.dr["hT%d" % l]
        ks = self.dr["ks%d" % l]
        cqd = self.dr["cqn%d" % l]
        win = self.dr["mla_win"]
        self.phase(self.base)
        vec = self.load_vecs(l)
        cs = self.alloc("cs", [NOWN], F32)
        self.dma(cs.ap, self.dr["cs"], [], [cs.name])
        hR = Rot([self.alloc("h%d" % i, [8, SBW], F32) for i in range(2)])
        nb = (self.alloc("sq", [8, SBW], BF16), self.alloc("rstd", [SBW], F32), self.alloc("u", [8, SBW], BF16))
        cq = self.alloc("cq", [3, SBW], F32)
        ckv = self.alloc("ckv", [2, SBW], F32)
        nbq = (nb[0], nb[1], self.alloc("cqn", [3, SBW], BF16))
        nbk = (nb[0], nb[1], self.alloc("ckvn", [2, SBW], BF16))
        krb = self.alloc("krb", [SBW], BF16)
        t1 = self.alloc("t1", [512], F32)
        t2 = self.alloc("t2", [512], F32)
        wrot = Rot([self.alloc("w%d" % i, [8, 128], BF16) for i in range(3)])
        hn = self.load_h(hT, SUPER[0][0], SUPER[0][1], hR.get())
        for k, (s0, subs) in enumerate(SUPER):
            W = sum(n for _, n in subs)
            h = hn
            if k + 1 < len(SUPER):
                hn = self.load_h(hT, SUPER[k + 1][0], SUPER[k + 1][1], hR.get())
            u = self.norm_to_bf16(h, subs, vec, 0, 8, 1024, nb)
            for m in range(6):
                wt = self.wload(wrot, win[j, m], [8, 128])
                for si, (c0, n) in enumerate(subs):
                    ps = self.psr.get()
                    for kc in range(8):
                        self.mm(ps.ap[:, 0:n], wt.ap[:, kc, :], u.ap[:, kc, c0:c0 + n], kc == 0, kc == 7,
                                [wt.name, u.name + "#%d" % si], [ps.name])
                    if m < 3:
                        self.copy(cq.ap[:, m, c0:c0 + n], ps.ap[:, 0:n], [ps.name], [cq.name + "#%d" % si])
                    elif m < 5:
                        self.copy(ckv.ap[:, m - 3, c0:c0 + n], ps.ap[:, 0:n], [ps.name], [ckv.name + "#%d" % si])
                    else:
                        g0 = s0 + c0
                        self.tt(t1.ap[0:64, 0:n], ps.ap[0:64, 0:n], cs.ap[0:64, g0:g0 + n], ALU.mult,
                                [ps.name, cs.name], [t1.name])
                        self.tt(t2.ap[0:64, 0:n], ps.ap[64:128, 0:n], cs.ap[64:128, g0:g0 + n], ALU.mult,
                                [ps.name, cs.name], [t2.name])
                        self.tt(krb.ap[0:64, c0:c0 + n], t1.ap[0:64, 0:n], t2.ap[0:64, 0:n], ALU.add,
                                [t1.name, t2.name], [krb.name + "#%d" % si], eng="pool")
            cqn = self.norm_to_bf16(cq, subs, vec, 32, 3, 384, nbq)
            ckvn = self.norm_to_bf16(ckv, subs, vec, 35, 2, 256, nbk)
            self.dma(cqd.rearrange("(c p) n -> p c n", p=128)[:, :, s0:s0 + W], cqn.ap[:, :, 0:W],
                     [cqn.name], ["cqn%d#%d" % (l, s0)])
            self.dma(ks[0:256, :].rearrange("(c p) n -> p c n", p=128)[:, :, s0:s0 + W], ckvn.ap[:, :, 0:W],
                     [ckvn.name], ["ks%d#a%d" % (l, s0)])
            self.dma(ks[256:320, s0:s0 + W], krb.ap[0:64, 0:W], [krb.name], ["ks%d#b%d" % (l, s0)])

    def phaseB_mla(self, l):
        j = JIDX[l]
        ksf = self.dr["ksf%d" % l]
        cqd = self.dr["cqn%d" % l]
        oall = self.dr["oall%d" % l]
        wuq = self.dr["mla_wuq"]
        wukv = self.dr["mla_wukv"]
        SCALE = 192.0 ** -0.5
        self.phase(self.base)
        cs = self.alloc("cs", [NOWN], F32)
        self.dma(cs.ap, self.dr["cs"], [], [cs.name])
        ckv = self.alloc("ckvf", [2, NFULL], BF16)
        kr = self.alloc("krf", [NFULL], BF16)
        cq = self.alloc("cqf", [3, NOWN], BF16)
        for r_ in range(2):
            self.dma(ckv.ap[:, :, r_ * 2048:(r_ + 1) * 2048],
                     ksf[r_, 0:256, 0:2048].rearrange("(c p) n -> p c n", p=128), ["ksf%d" % l], [ckv.name])
            self.dma(kr.ap[0:64, r_ * 2048:(r_ + 1) * 2048], ksf[r_, 256:320, 0:2048], ["ksf%d" % l], [kr.name])
        self.dma(ckv.ap[:, :, 4096:4112], ksf[0, 0:256, 2048:2064].rearrange("(c p) n -> p c n", p=128),
                 ["ksf%d" % l], [ckv.name])
        self.dma(kr.ap[0:64, 4096:4112], ksf[0, 256:320, 2048:2064], ["ksf%d" % l], [kr.name])
        self.dma(cq.ap, cqd.rearrange("(c p) n -> p c n", p=128), ["cqn%d" % l], [cq.name])
        KhR = Rot([self.alloc("Kh%d" % i, [NFULL], BF16) for i in range(2)])
        VhR = Rot([self.alloc("Vh%d" % i, [33, 128], BF16) for i in range(2)])
        QnR = Rot([self.alloc("Qn%d" % i, [NOWN], BF16) for i in range(2)])
        QrR = Rot([self.alloc("Qr%d" % i, [NOWN], BF16) for i in range(2)])
        for qt_ in QrR.items:
            self.memset(qt_.ap[64:128, :], 0.0, [qt_.name])
        self.memset(kr.ap[64:128, :], 0.0, [kr.name])
        OhR = Rot([self.alloc("Oh%d" % i, [NOWN], BF16) for i in range(2)])
        PtR = Rot([self.alloc("Pt%d" % i, [512], BF16) for i in range(6)])
        wqR = Rot([self.alloc("wq%d" % i, [3, 256], BF16) for i in range(2)])
        wkR = Rot([self.alloc("wk%d" % i, [2, 256], BF16) for i in range(2)])
        t1 = self.alloc("t1", [512], F32)
        t2 = self.alloc("t2", [512], F32)
        rsR = Rot([self.alloc("rs%d" % i, [512], F32) for i in range(2)])
        accR = Rot([(self.ps(0), self.ps(1)), (self.ps(2), self.ps(3))])
        saR = Rot([(self.alloc("saP%d" % i, [512], F32), self.alloc("saD%d" % i, [512], F32)) for i in range(2)])
        def gen_list(hh):
            wq = self.wload(wqR, wuq[j, hh], [3, 256])
            wk = self.wload(wkR, wukv[j, hh], [2, 256])
            Kh, Vh, Qn, Qr, Oh = KhR.get(), VhR.get(), QnR.get(), QrR.get(), OhR.get()
            ops = []

            def gK(c0, n):
                ps = self.psr.get()
                for kc in range(2):
                    self.mm(ps.ap[:, 0:n], wk.ap[:, kc, 0:128], ckv.ap[:, kc, c0:c0 + n], kc == 0, kc == 1,
                            [wk.name, ckv.name], [ps.name])
                self.copy(Kh.ap[:, c0:c0 + n], ps.ap[:, 0:n], [ps.name], [Kh.name + "#%d" % (c0 // 512)])

            def gV(g):
                ps = self.psr.get()
                tl = list(range(g, min(g + 4, 33)))
                for t in tl:
                    k0, kn = KT[t]
                    for kc in range(2):
                        self.mm(ps.ap[0:kn, (t - g) * 128:(t - g + 1) * 128], ckv.ap[:, kc, k0:k0 + kn],
                                wk.ap[:, kc, 128:256], kc == 0, kc == 1, [wk.name, ckv.name], [ps.name])
                if len(tl) == 4:
                    self.copy(Vh.ap[:, g:g + 4, :], ps.ap[:, 0:512].rearrange("p (a b) -> p a b", b=128),
                              [ps.name], [Vh.name + "#%d" % (g // 4)])
                else:
                    self.copy(Vh.ap[0:16, 32, :], ps.ap[0:16, 0:128], [ps.name], [Vh.name + "#8"])

            def gQ(qi, c0, n):
                ps = self.psr.get()
                for kc in range(3):
                    self.mm(ps.ap[:, 0:n], wq.ap[:, kc, 0:128], cq.ap[:, kc, c0:c0 + n], kc == 0, kc == 2,
                            [wq.name, cq.name], [ps.name])
                self.copy(Qn.ap[:, c0:c0 + n], ps.ap[:, 0:n], [ps.name], [Qn.name + "#%d" % qi])
                ps = self.psr.get()
                for kc in range(3):
                    self.mm(ps.ap[:, 0:n], wq.ap[:, kc, 128:256], cq.ap[:, kc, c0:c0 + n], kc == 0, kc == 2,
                            [wq.name, cq.name], [ps.name])
                self.tt(t1.ap[0:64, 0:n], ps.ap[0:64, 0:n], cs.ap[0:64, c0:c0 + n], ALU.mult,
                        [ps.name, cs.name], [t1.name])
                self.tt(t2.ap[0:64, 0:n], ps.ap[64:128, 0:n], cs.ap[64:128, c0:c0 + n], ALU.mult,
                        [ps.name, cs.name], [t2.name])
                self.tt(Qr.ap[0:64, c0:c0 + n], t1.ap[0:64, 0:n], t2.ap[0:64, 0:n], ALU.add,
                        [t1.name, t2.name], [Qr.name + "#%d" % qi], eng="pool")

            for (c0, n) in KB:
                ops.append(lambda c0=c0, n=n: gK(c0, n))
            for g in range(0, 33, 4):
                ops.append(lambda g=g: gV(g))
            for qi, (c0, n) in enumerate(QB):
                ops.append(lambda qi=qi, c0=c0, n=n: gQ(qi, c0, n))
            return (Kh, Vh, Qn, Qr, Oh), ops

        nxt_bufs, nxt_ops = gen_list(0)
        for f_ in nxt_ops:
            f_()
        for hh in range(16):
            Kh, Vh, Qn, Qr, Oh = nxt_bufs
            if hh + 1 < 16:
                nxt_bufs, nxt_ops = gen_list(hh + 1)
            else:
                nxt_ops = []
            per = (len(nxt_ops) + 3) // 4
            for qi, (q0, qn) in enumerate(QB):
                psO, psS = accR.get()
                saP, saD = saR.get()
                LAG = 2
                pend = []
                for i in range(33 + LAG):
                    if i < 33:
                        k0, kn = KT[i]
                        pT = self.psr.get()
                        self.mm(pT.ap[0:kn, 0:qn], Kh.ap[:, k0:k0 + kn], Qn.ap[:, q0:q0 + qn], True, False,
                                [Kh.name + "#%d" % (k0 // 512), Qn.name + "#%d" % qi], [pT.name])
                        self.mm(pT.ap[0:kn, 0:qn], kr.ap[:, k0:k0 + kn], Qr.ap[:, q0:q0 + qn], False, True,
                                [kr.name, Qr.name + "#%d" % qi], [pT.name])
                        pt = PtR.get()
                        self.act(pt.ap[0:kn, 0:qn], pT.ap[0:kn, 0:qn], AF.Exp, [pT.name], [pt.name], scale=SCALE)
                        pend.append((i, pt))
                    if i >= LAG:
                        t, pt = pend.pop(0)
                        k0, kn = KT[t]
                        self.mm(psO.ap[:, 0:qn], Vh.ap[0:kn, t, :], pt.ap[0:kn, 0:qn], t == 0, t == 32,
                                [Vh.name + "#%d" % (t // 4), pt.name], [psO.name])
                        if t % 3 == 0:
                            self.mm(psS.ap[:, 0:qn], self.ones.ap[0:kn, :], pt.ap[0:kn, 0:qn], t == 0, False,
                                    [self.ones.name, pt.name], [psS.name])
                        else:
                            sa, se = (saP, "pool") if t % 3 == 2 else (saD, "dve")
                            if t < 3:
                                self.copy(sa.ap[0:kn, 0:qn], pt.ap[0:kn, 0:qn], [pt.name], [sa.name], eng=se)
                            else:
                                self.tt(sa.ap[0:kn, 0:qn], sa.ap[0:kn, 0:qn], pt.ap[0:kn, 0:qn], ALU.add,
                                        [sa.name, pt.name], [sa.name], eng=se)
                self.mm(psS.ap[:, 0:qn], self.onesf.ap, saP.ap[:, 0:qn], False, False, [self.onesf.name, saP.name], [psS.name])
                self.mm(psS.ap[:, 0:qn], self.onesf.ap, saD.ap[:, 0:qn], False, True, [self.onesf.name, saD.name], [psS.name])
                rs = rsR.get()
                rsa = rs.ap[:, 0:qn]
                pss = psS.ap[:, 0:qn]
                self.P.add("dve", lambda e, rsa=rsa, pss=pss: e.reciprocal(rsa, pss), [psS.name], [rs.name])
                self.tt(Oh.ap[:, q0:q0 + qn], psO.ap[:, 0:qn], rsa, ALU.mult, [psO.name, rs.name],
                        [Oh.name + "#%d" % qi])
                if qi < 4:
                    for f_ in nxt_ops[qi * per:(qi + 1) * per]:
                        f_()
            self.dma(oall[hh * 128:(hh + 1) * 128, :], Oh.ap, [Oh.name], ["oall%d#%d" % (l, hh)])

    def phaseA_diff(self, l):
        hT = self.dr["hT%d" % l]
        ksk = self.dr["ksk%d" % l]
        ksv = self.dr["ksv%d" % l]
        qTd = self.dr["qT%d" % l]
        wqk = self.dr["diff_wqk"]
        wv = self.dr["diff_wv"]
        self.phase(self.base)
        vec = self.load_vecs(l)
        hR = Rot([self.alloc("h%d" % i, [8, SBW], F32) for i in range(2)])
        nb = (self.alloc("sq", [8, SBW], BF16), self.alloc("rstd", [SBW], F32), self.alloc("u", [8, SBW], BF16))
        qkR = Rot([self.alloc("qk%d" % i, [8, SBW], BF16) for i in range(2)])
        wrot = Rot([self.alloc("w%d" % i, [8, 128], BF16) for i in range(3)])
        wvR = Rot([self.alloc("wv%d" % i, [8, 512], BF16) for i in range(2)])
        vtR = Rot([self.alloc("vt%d" % i, [1024], BF16) for i in range(3)])
        wvt = [self.wload(wvR, wv[hf], [8, 512]) for hf in range(2)]
        hn = self.load_h(hT, SUPER[0][0], SUPER[0][1], hR.get())
        for k, (s0, subs) in enumerate(SUPER):
            W = sum(n for _, n in subs)
            h = hn
            if k + 1 < len(SUPER):
                hn = self.load_h(hT, SUPER[k + 1][0], SUPER[k + 1][1], hR.get())
            u = self.norm_to_bf16(h, subs, vec, 0, 8, 1024, nb)
            for part, dst in ((0, qTd), (1, ksk)):
                qk = qkR.get()
                for m in range(8):
                    wt = self.wload(wrot, wqk[part * 8 + m], [8, 128])
                    for si, (c0, n) in enumerate(subs):
                        ps = self.psr.get()
                        for kc in range(8):
                            self.mm(ps.ap[:, 0:n], wt.ap[:, kc, :], u.ap[:, kc, c0:c0 + n], kc == 0, kc == 7,
                                    [wt.name, u.name + "#%d" % si], [ps.name])
                        self.copy(qk.ap[:, m, c0:c0 + n], ps.ap[:, 0:n], [ps.name], [qk.name + "#%d" % si])
                self.dma(dst.rearrange("(c p) n -> p c n", p=128)[:, :, s0:s0 + W], qk.ap[:, :, 0:W],
                         [qk.name], ["%s%d#%d" % ("qT" if part == 0 else "ksk", l, s0)])
            toks = []
            for si, (c0, n) in enumerate(subs):
                for a_ in range(0, n, 128):
                    toks.append((si, c0 + a_, min(128, n - a_)))
            for (si, a0, an) in toks:
                vt = vtR.get()
                for hf in range(2):
                    ps = self.psr.get()
                    for kc in range(8):
                        self.mm(ps.ap[0:an, :], u.ap[:, kc, a0:a0 + an], wvt[hf].ap[:, kc, :], kc == 0, kc == 7,
                                [wvt[hf].name, u.name + "#%d" % si], [ps.name])
                    self.copy(vt.ap[0:an, hf * 512:(hf + 1) * 512], ps.ap[0:an, :], [ps.name], [vt.name])
                self.dma(ksv[s0 + a0:s0 + a0 + an, :], vt.ap[0:an, :], [vt.name], ["ksv%d#%d" % (l, s0 + a0)])

    def phaseB_diff(self, l):
        kf = self.dr["ksfk%d" % l]
        vf = self.dr["ksfv%d" % l]
        qTd = self.dr["qT%d" % l]
        oall = self.dr["oall%d" % l]
        lambda_init = 0.8 - 0.6 * math.exp(-0.3 * l)
        self.phase(self.base)
        vec = self.load_vecs(l)
        ttr = self.alloc("ttr", [6032], F32)
        ttm = self.alloc("ttm", [4000], F32)
        lp = self.alloc("lp", [256], F32)
        lt = self.alloc("lt", [8], F32)
        self.dma(ttr.ap, self.dr["ttr"], [], [ttr.name])
        self.dma(ttm.ap, self.dr["ttm"], [], [ttm.name])
        self.dma(lp.ap, self.dr["diff_lam"].partition_broadcast(128), [], [lp.name])
        self.tt(lp.ap[:, 0:64], lp.ap[:, 0:64], lp.ap[:, 64:128], ALU.mult, [lp.name], [lp.name])
        self.tt(lp.ap[:, 128:192], lp.ap[:, 128:192], lp.ap[:, 192:256], ALU.mult, [lp.name], [lp.name])
        a0 = lt.ap[:, 0:1]
        a1 = lt.ap[:, 1:2]
        l0 = lp.ap[:, 0:64]
        l1 = lp.ap[:, 128:192]
        self.P.add("dve", lambda e: e.reduce_sum(a0, l0, mybir.AxisListType.X), [lp.name], [lt.name])
        self.P.add("dve", lambda e: e.reduce_sum(a1, l1, mybir.AxisListType.X), [lp.name], [lt.name])
        self.act(lt.ap[:, 2:4], lt.ap[:, 0:2], AF.Exp, [lt.name], [lt.name])
        self.tt(lt.ap[:, 4:5], lt.ap[:, 3:4], lt.ap[:, 2:3], ALU.subtract, [lt.name], [lt.name])
        self.ts(lt.ap[:, 5:6], lt.ap[:, 4:5], -lambda_init, None, ALU.add, None, [lt.name], [lt.name])
        neglam = lt.ap[:, 5:6]
        KhR = Rot([self.alloc("Kh%d" % i, [NFULL], BF16) for i in range(2)])
        VhR = Rot([self.alloc("Vh%d" % i, [33, 128], BF16) for i in range(2)])
        QhR = Rot([(self.alloc("Qz0_%d" % i, [NOWN], BF16), self.alloc("Qz1_%d" % i, [NOWN], BF16)) for i in range(2)])
        for (qa_, qb_) in QhR.items:
            self.memset(qa_.ap[64:128, :], 0.0, [qa_.name])
            self.memset(qb_.ap[0:64, :], 0.0, [qb_.name])
        OhR = Rot([self.alloc("Oh%d" % i, [NOWN], BF16) for i in range(2)])
        PtR = Rot([self.alloc("Pt%d" % i, [512], BF16) for i in range(10)])
        PeR = Rot([self.alloc("Pe%d" % i, [512], BF16) for i in range(6)])
        EhR = Rot([(self.alloc("Er%d" % i, [6032], BF16), self.alloc("Em%d" % i, [4000], BF16)) for i in range(2)])
        accR = Rot([(self.ps(0), self.ps(1)), (self.ps(2), self.ps(3))])
        saR = Rot([self.alloc("sa%d" % i, [512], F32) for i in range(2)])
        o0 = self.alloc("o0", [512], F32)
        o1 = self.alloc("o1", [512], F32)
        r0 = self.alloc("r0", [512], F32)
        r1 = self.alloc("r1", [512], F32)
        osq = self.alloc("osq", [512], BF16)
        orr = self.alloc("orr", [512], F32)
        for hh in range(8):
            slope = 2.0 ** (-(hh + 1))
            Kh, Vh, Qh, Oh = KhR.get(), VhR.get(), QhR.get(), OhR.get()
            Er, Em = EhR.get()
            self.act(Er.ap, ttr.ap, AF.Exp, [ttr.name], [Er.name], scale=-slope)
            self.act(Em.ap, ttm.ap, AF.Exp, [ttm.name], [Em.name], scale=-slope)
            rows = slice(hh * 128, (hh + 1) * 128)
            kc_, ko_ = hh // 2, (hh % 2) * 128
            for r_ in range(2):
                self.dma(Kh.ap[:, r_ * 2048:(r_ + 1) * 2048], kf[kc_, r_ * 256 + ko_:r_ * 256 + ko_ + 128, 0:2048],
                         ["ksfk%d" % l], [Kh.name])
                for i4 in range(4):
                    rws = 512 if i4 < 3 else 528
                    self.dma(Vh.ap[:, r_ * 16 + 4 * i4:r_ * 16 + 4 * i4 + 4, :],
                             vf[i4, r_ * rws:r_ * rws + 512, rows].rearrange("(t p) d -> p t d", p=128),
                             ["ksfv%d" % l], [Vh.name])
            self.dma(Kh.ap[:, 4096:4112], kf[kc_, ko_:ko_ + 128, 2048:2064], ["ksfk%d" % l], [Kh.name])
            self.dma(Vh.ap[0:16, 32, :], vf[3, 512:528, rows], ["ksfv%d" % l], [Vh.name])
            self.dma(Qh[0].ap[0:64, :], qTd[hh * 128:hh * 128 + 64, :], ["qT%d" % l], [Qh[0].name])
            self.dma(Qh[1].ap[64:128, :], qTd[hh * 128 + 64:hh * 128 + 128, :], ["qT%d" % l], [Qh[1].name])
            for qi, (q0, qn) in enumerate(QB):
                for c, (rr, oo) in enumerate(((r0, o0), (r1, o1))):
                    psO, psS = accR.get()
                    LAG = 5
                    pend = []
                    for i in range(33 + LAG):
                        if i < 33:
                            k0, kn = KT[i]
                            base_k = (NMETA + 128 * i) if i < 32 else 0
                            if qi < 4:
                                m0_ = (NMETA + 512 * qi) - base_k + 3968
                                Dta = Er.ap[0:kn, m0_:m0_ + qn]
                                Dtn = Er.name
                            else:
                                m0_ = 3984 - base_k
                                Dta = Em.ap[0:kn, m0_:m0_ + qn]
                                Dtn = Em.name
                            pT = self.psr.get()
                            self.mm(pT.ap[0:kn, 0:qn], Kh.ap[:, k0:k0 + kn],
                                    Qh[c].ap[:, q0:q0 + qn], True, True, [Kh.name, Qh[c].name], [pT.name])
                            pe = PeR.get()
                            self.act(pe.ap[0:kn, 0:qn], pT.ap[0:kn, 0:qn], AF.Exp, [pT.name], [pe.name], scale=0.125)
                            pt = PtR.get()
                            self.tt(pt.ap[0:kn, 0:qn], pe.ap[0:kn, 0:qn], Dta, ALU.mult, [pe.name, Dtn], [pt.name],
                                    eng=("dve" if i % 2 == 0 else "pool"))
                            pend.append((i, pt))
                        if i >= LAG:
                            t, pt = pend.pop(0)
                            k0, kn = KT[t]
                            self.mm(psO.ap[:, 0:qn], Vh.ap[0:kn, t, :], pt.ap[0:kn, 0:qn], t == 0, t == 32,
                                    [Vh.name, pt.name], [psO.name])
                            self.mm(psS.ap[:, 0:qn], self.ones.ap[0:kn, :], pt.ap[0:kn, 0:qn], t == 0, t == 32,
                                    [self.ones.name, pt.name], [psS.name])
                    rra = rr.ap[:, 0:qn]
                    pss = psS.ap[:, 0:qn]
                    self.P.add("dve", lambda e, rra=rra, pss=pss: e.reciprocal(rra, pss), [psS.name], [rr.name])
                    self.tt(oo.ap[:, 0:qn], psO.ap[:, 0:qn], rra, ALU.mult, [psO.name, rr.name], [oo.name])
                self.stt(o0.ap[:, 0:qn], o1.ap[:, 0:qn], neglam, o0.ap[:, 0:qn], ALU.mult, ALU.add,
                         [o0.name, o1.name, lt.name], [o0.name])
                self.act(osq.ap[:, 0:qn], o0.ap[:, 0:qn], AF.Square, [o0.name], [osq.name])
                ps = self.psr.get()
                self.mm(ps.ap[:, 0:qn], self.ones.ap, osq.ap[:, 0:qn], True, True, [osq.name, self.ones.name], [ps.name])
                self.act(orr.ap[:, 0:qn], ps.ap[:, 0:qn], AF.Sqrt, [ps.name, self.cst.name], [orr.name],
                         bias=self.eps, scale=1.0 / 128)
                ora = orr.ap[:, 0:qn]
                self.P.add("dve", lambda e, ora=ora: e.reciprocal(ora, ora), [orr.name], [orr.name])
                self.tt(o0.ap[:, 0:qn], o0.ap[:, 0:qn], ora, ALU.mult, [o0.name, orr.name], [o0.name])
                self.ts(Oh.ap[:, q0:q0 + qn], o0.ap[:, 0:qn], vec.ap[:, 32:33], 1.0 - lambda_init, ALU.mult, ALU.mult,
                        [o0.name, vec.name], [Oh.name + "#%d" % qi])
            self.dma(oall[rows, :], Oh.ap, [Oh.name], ["oall%d#%d" % (l, hh)])

    def phaseA_lru(self, l):
        hT = self.dr["hT%d" % l]
        ksx = self.dr["ksx%d" % l]
        ggd = self.dr["gg%d" % l]
        win = self.dr["lru_win"]
        self.phase(self.base)
        vec = self.load_vecs(l)
        hR = Rot([self.alloc("h%d" % i, [8, SBW], F32) for i in range(2)])
        nb = (self.alloc("sq", [8, SBW], BF16), self.alloc("rstd", [SBW], F32), self.alloc("u", [8, SBW], BF16))
        wrot = Rot([self.alloc("w%d" % i, [8, 128], BF16) for i in range(3)])
        xoR = Rot([self.alloc("xo%d" % i, [SBW], F32) for i in range(2)])
        goR = Rot([self.alloc("go%d" % i, [SBW], BF16) for i in range(2)])
        gx = self.alloc("gx", [512], F32)
        g2 = self.alloc("g2", [512], F32)
        hn = self.load_h(hT, SUPER[0][0], SUPER[0][1], hR.get())
        for k, (s0, subs) in enumerate(SUPER):
            W = sum(n for _, n in subs)
            h = hn
            if k + 1 < len(SUPER):
                hn = self.load_h(hT, SUPER[k + 1][0], SUPER[k + 1][1], hR.get())
            u = self.norm_to_bf16(h, subs, vec, 0, 8, 1024, nb)
            for m in range(24):
                wt = self.wload(wrot, win[m], [8, 128])
                xo = xoR.get() if m < 12 else goR.get()
                for si, (c0, n) in enumerate(subs):
                    ps = self.psr.get()
                    for kc in range(8):
                        self.mm(ps.ap[:, 0:n], wt.ap[:, kc, :], u.ap[:, kc, c0:c0 + n], kc == 0, kc == 7,
                                [wt.name, u.name + "#%d" % si], [ps.name])
                    if m < 12:
                        self.copy(xo.ap[:, c0:c0 + n], ps.ap[:, 0:n], [ps.name], [xo.name + "#%d" % si])
                    else:
                        gxa = gx.ap[:, 0:n]
                        g2a = g2.ap[:, 0:n]
                        self.copy(gxa, ps.ap[:, 0:n], [ps.name], [gx.name], eng="act")
                        self.act(g2a, ps.ap[:, 0:n], AF.Square, [ps.name], [g2.name], scale=math.sqrt(0.044715))
                        self.stt(g2a, g2a, 1.0, gxa, ALU.add, ALU.mult, [g2.name, gx.name], [g2.name])
                        self.act(g2a, g2a, AF.Sigmoid, [g2.name], [g2.name], scale=1.5957691216057308)
                        self.tt(xo.ap[:, c0:c0 + n], g2a, gxa, ALU.mult, [g2.name, gx.name], [xo.name + "#%d" % si],
                                eng="pool")
                if m < 12:
                    self.dma(ksx[m * 128:(m + 1) * 128, s0:s0 + W], xo.ap[:, 0:W], [xo.name], ["ksx%d#%d_%d" % (l, m, s0)])
                else:
                    mm_ = m - 12
                    self.dma(ggd[mm_ * 128:(mm_ + 1) * 128, s0:s0 + W], xo.ap[:, 0:W], [xo.name],
                             ["gg%d#%d_%d" % (l, mm_, s0)])

    def phaseB_lru(self, l):
        xf = self.dr["ksfx%d" % l]
        ggd = self.dr["gg%d" % l]
        oall = self.dr["oall%d" % l]
        wg = self.dr["lru_wg"]
        self.phase(self.base)
        vec = self.load_vecs(l)
        cn = self.alloc("cneg", [24], F32)
        cn2 = self.alloc("cneg2", [24], F32)
        self.act(cn.ap, vec.ap[:, 140:164], AF.Exp, [vec.name], [cn.name], scale=-1.0)
        self.act(cn.ap, cn.ap, AF.Ln, [cn.name, self.cst.name], [cn.name], bias=self.one, scale=1.0)
        self.ts(cn.ap, cn.ap, -8.0, None, ALU.mult, None, [cn.name], [cn.name])
        self.ts(cn2.ap, cn.ap, 2.0, None, ALU.mult, None, [cn.name], [cn2.name])
        L = NFULL
        xb = self.alloc("xb", [2, L + 3], F32)
        xc = self.alloc("xc", [2, L], F32)
        xcb = self.alloc("xcb", [2, L], BF16)
        taA = self.alloc("ta", [L], F32)
        tiA = self.alloc("ti", [L], F32)
        hsA = self.alloc("hsA", [L], F32)
        hsB = self.alloc("hsB", [L], F32)
        yo = self.alloc("yo", [NOWN], F32)
        ggt = self.alloc("ggt", [NOWN], BF16)
        yb = self.alloc("yb", [NOWN], BF16)
        wgR = Rot([self.alloc("wg%d" % i, [2, 128], BF16) for i in range(16)])

        def load_gate_w(b_):
            out = {}
            for c2_ in range(2):
                for d_ in range(2):
                    for g_ in range(2):
                        out[(c2_, d_, g_)] = self.wload(wgR, wg[((d_ * 2 + g_) * 6 + b_) * 2 + c2_], [2, 128])
            return out

        gw_next = load_gate_w(0)
        m0 = self.cst.ap[:, 2:3]
        m1 = self.cst.ap[:, 3:4]
        setA = (taA.ap, taA.name, tiA.ap, tiA.name, hsA)
        setB = (xb.ap[:, 0, 0:L], xb.name + "#0", xb.ap[:, 1, 0:L], xb.name + "#1", hsB)
        for b in range(6):
            for c2 in range(2):
                ct = 2 * b + c2
                xn = xb.name + "#%d" % c2
                self.memset(xb.ap[:, c2, 0:2], 0.0, [xn])
                self.memset(xb.ap[:, c2, L + 2:L + 3], 0.0, [xn])
                self.dma(xb.ap[:, c2, 2:18], xf[ct, 0:128, 2048:2064], ["ksfx%d" % l], [xn])
                self.dma(xb.ap[:, c2, 18:2066], xf[ct, 0:128, 0:2048], ["ksfx%d" % l], [xn])
                self.dma(xb.ap[:, c2, 2066:4114], xf[ct, 128:256, 0:2048], ["ksfx%d" % l], [xn])
                xca = xc.ap[:, c2, :]
                self.act(xca, xb.ap[:, c2, 0:L], AF.Identity, [xn, vec.name], [xc.name + "#%d" % c2],
                         bias=vec.ap[:, 80 + ct:81 + ct], scale=vec.ap[:, 32 + ct:33 + ct])
                for jj in range(1, 4):
                    col = 32 + jj * 12 + ct
                    self.stt(xca, xb.ap[:, c2, jj:jj + L], vec.ap[:, col:col + 1], xca, ALU.mult, ALU.add,
                             [xn, vec.name, xc.name + "#%d" % c2], [xc.name + "#%d" % c2])
                self.copy(xcb.ap[:, c2, :], xca, [xc.name + "#%d" % c2], [xcb.name + "#%d" % c2], eng="dve")
            gw = gw_next
            if b + 1 < 6:
                gw_next = load_gate_w(b + 1)
            for c2 in range(2):
                ct = 2 * b + c2
                for d in range(2):
                    ta_ap, ta_n, ti_ap, ti_n, hs = setA if d == 0 else setB
                    wr = gw[(c2, d, 0)]
                    wi = gw[(c2, d, 1)]
                    br = 92 + (d * 2 + 0) * 12 + ct
                    bi = 92 + (d * 2 + 1) * 12 + ct
                    for (c0, n) in KB:
                        for (wt, bcol, dap, dn) in ((wr, br, ta_ap, ta_n), (wi, bi, ti_ap, ti_n)):
                            ps = self.psr.get()
                            for kc in range(2):
                                self.mm(ps.ap[:, 0:n], wt.ap[:, kc, :], xcb.ap[:, kc, c0:c0 + n], kc == 0, kc == 1,
                                        [wt.name, xcb.name], [ps.name])
                            self.act(dap[:, c0:c0 + n], ps.ap[:, 0:n], AF.Sigmoid, [ps.name, vec.name],
                                     [dn], bias=vec.ap[:, bcol:bcol + 1], scale=1.0)
                    ccol = d * 12 + ct
                    self.act(hs.ap, ta_ap, AF.Exp, [ta_n, cn2.name], [hs.name], scale=cn2.ap[:, ccol:ccol + 1])
                    self.act(hs.ap, hs.ap, AF.Sqrt, [hs.name, self.cst.name], [hs.name], bias=self.one, scale=-1.0)
                    self.act(ta_ap, ta_ap, AF.Exp, [ta_n, cn.name], [ta_n], scale=cn.ap[:, ccol:ccol + 1])
                    self.tt(ti_ap, ti_ap, xc.ap[:, c2, :], ALU.mult, [ti_n, xc.name + "#%d" % c2], [ti_n])
                    self.tt(ti_ap, ti_ap, hs.ap, ALU.mult, [ti_n, hs.name], [ti_n])
                    dst = hs
                    if d == 0:
                        o_, a_, u_ = dst.ap, ta_ap, ti_ap
                    else:
                        o_, a_, u_ = dst.ap[:, ::-1], ta_ap[:, ::-1], ti_ap[:, ::-1]
                    self.P.add("dve", lambda e, o_=o_, a_=a_, u_=u_: e.tensor_tensor_scan(o_, a_, u_, 0.0, ALU.mult, ALU.add),
                               [ta_n, ti_n], [dst.name])
                self.ts(yo.ap[:, 0:2048], hsA.ap[:, 16:2064], m0, None, ALU.mult, None, [hsA.name, self.cst.name], [yo.name])
                self.stt(yo.ap[:, 0:2048], hsB.ap[:, 16:2064], m0, yo.ap[:, 0:2048], ALU.mult, ALU.add,
                         [hsB.name, yo.name, self.cst.name], [yo.name])
                self.stt(yo.ap[:, 0:2048], hsA.ap[:, 2064:4112], m1, yo.ap[:, 0:2048], ALU.mult, ALU.add,
                         [hsA.name, yo.name, self.cst.name], [yo.name])
                self.stt(yo.ap[:, 0:2048], hsB.ap[:, 2064:4112], m1, yo.ap[:, 0:2048], ALU.mult, ALU.add,
                         [hsB.name, yo.name, self.cst.name], [yo.name])
                self.tt(yo.ap[:, 2048:2064], hsA.ap[:, 0:16], hsB.ap[:, 0:16], ALU.add, [hsA.name, hsB.name], [yo.name])
                rows = slice(ct * 128, (ct + 1) * 128)
                self.dma(ggt.ap, ggd[rows, :], ["gg%d" % l], [ggt.name])
                self.tt(yb.ap, yo.ap, ggt.ap, ALU.mult, [yo.name, ggt.name], [yb.name], eng="pool")
                self.dma(oall[rows, :], yb.ap, [yb.name], ["oall%d#%d" % (l, ct)])

    def phaseC(self, l, KCo, wo_name):
        hT = self.dr["hT%d" % l]
        hTo = self.dr["hT%d" % (l + 1)]
        oall = self.dr["oall%d" % l]
        wo = self.dr[wo_name]
        w1 = self.dr["ffn_w1"]
        w2 = self.dr["ffn_w2"]
        self.phase(self.base)
        vec = self.load_vecs(l)
        hR = Rot([self.alloc("h%d" % i, [8, SBW], F32) for i in range(2)])
        u1 = self.alloc("u1", [8, SBW], F32)
        sq = self.alloc("sqc", [8, SBW], BF16)
        rstd = self.alloc("rstdc", [SBW], F32)
        u3 = self.alloc("u3", [8, SBW], BF16)
        hid = self.alloc("hid", [22, SBW], BF16)
        oin = self.alloc("oin", [KCo, SBW], BF16)
        sgR = Rot([self.alloc("sg%d" % i, [512], F32) for i in range(2)])
        w1R = Rot([self.alloc("w1_%d" % i, [8, 256], BF16) for i in range(2)])
        w2R = Rot([self.alloc("w2_%d" % i, [22, 128], BF16) for i in range(2)])
        woR = Rot([self.alloc("wo_%d" % i, [KCo, 128], BF16) for i in range(2)])
        src = oall.rearrange("(c p) n -> p c n", p=128)
        dst = hTo.rearrange("(c p) n -> p c n", p=128)

        def load_oin(s0, W):
            for kc in range(KCo):
                self.dma(oin.ap[:, kc, 0:W], src[:, kc, s0:s0 + W], ["oall%d" % l], [oin.name])

        hn = self.load_h(hT, SUPER[0][0], SUPER[0][1], hR.get())
        load_oin(SUPER[0][0], SBW)
        for k, (s0, subs) in enumerate(SUPER):
            h = hn
            for m in range(8):
                wt = self.wload(woR, wo[JIDX[l], m] if wo_name == "mla_wo" else wo[m], [KCo, 128])
                for si, (c0, n) in enumerate(subs):
                    ps = self.psr.get()
                    for kc in range(KCo):
                        self.mm(ps.ap[:, 0:n], wt.ap[:, kc, :], oin.ap[:, kc, c0:c0 + n], kc == 0, kc == KCo - 1,
                                [wt.name, oin.name], [ps.name])
                    self.copy(u1.ap[:, m, c0:c0 + n], ps.ap[:, 0:n], [ps.name], [u1.name + "#%d" % si])
            if k + 1 < len(SUPER):
                hn = self.load_h(hT, SUPER[k + 1][0], SUPER[k + 1][1], hR.get())
                load_oin(SUPER[k + 1][0], SBW)
            self.resid_norm(h, u1, subs, sq, rstd, vec, 8)
            self.rms_stats(h, 8, subs, sq, rstd, 1024)
            for si, (c0, n) in enumerate(subs):
                for c in range(8):
                    self.stt(u3.ap[:, c, c0:c0 + n], h.ap[:, c, c0:c0 + n], vec.ap[:, 16 + c:17 + c],
                             rstd.ap[:, c0:c0 + n], ALU.mult, ALU.mult,
                             [h.name + "#%d" % si, rstd.name + "#%d" % si, vec.name], [u3.name + "#%d" % si])
            for jn in range(22):
                wt = self.wload(w1R, w1[l, jn], [8, 256])
                for si, (c0, n) in enumerate(subs):
                    pg = self.psr.get()
                    for kc in range(8):
                        self.mm(pg.ap[:, 0:n], wt.ap[:, kc, 0:128], u3.ap[:, kc, c0:c0 + n], kc == 0, kc == 7,
                                [wt.name, u3.name + "#%d" % si], [pg.name])
                    pu = self.psr.get()
                    for kc in range(8):
                        self.mm(pu.ap[:, 0:n], wt.ap[:, kc, 128:256], u3.ap[:, kc, c0:c0 + n], kc == 0, kc == 7,
                                [wt.name, u3.name + "#%d" % si], [pu.name])
                    sg = sgR.get()
                    self.act(sg.ap[:, 0:n], pg.ap[:, 0:n], AF.Silu, [pg.name], [sg.name])
                    self.tt(hid.ap[:, jn, c0:c0 + n], sg.ap[:, 0:n], pu.ap[:, 0:n], ALU.mult, [sg.name, pu.name],
                            [hid.name + "#%d" % si])
            for m in range(8):
                wt = self.wload(w2R, w2[l, m], [22, 128])
                for si, (c0, n) in enumerate(subs):
                    ps = self.psr.get()
                    for kc in range(22):
                        self.mm(ps.ap[:, 0:n], wt.ap[:, kc, :], hid.ap[:, kc, c0:c0 + n], kc == 0, kc == 21,
                                [wt.name, hid.name + "#%d" % si], [ps.name])
                    self.copy(u1.ap[:, m, c0:c0 + n], ps.ap[:, 0:n], [ps.name], [u1.name + "#%d" % si])
            self.resid_norm(h, u1, subs, sq, rstd, vec, 24)
            for si, (c0, n) in enumerate(subs):
                self.dma(dst[:, :, s0 + c0:s0 + c0 + n], h.ap[:, :, c0:c0 + n], [h.name + "#%d" % si],
                         ["hT%d#%d" % (l + 1, s0 + c0)])

    def resid_norm(self, h, u1, subs, sq, rstd, vec, gcol):
        self.rms_stats(u1, 8, subs, sq, rstd, 1024)
        for si, (c0, n) in enumerate(subs):
            for c in range(8):
                ua = u1.ap[:, c, c0:c0 + n]
                self.tt(ua, ua, rstd.ap[:, c0:c0 + n], ALU.mult, [u1.name + "#%d" % si, rstd.name + "#%d" % si],
                        [u1.name + "#%d" % si], eng="pool")
                ha = h.ap[:, c, c0:c0 + n]
                self.stt(ha, ua, vec.ap[:, gcol + c:gcol + c + 1], ha, ALU.mult, ALU.add,
                         [u1.name + "#%d" % si, h.name + "#%d" % si, vec.name], [h.name + "#%d" % si])


WEIGHT_SHAPES = {
    "vecs": ([4, 128, NV], F32),
    "cst": ([128, 4], F32),
    "cs": ([128, NOWN], F32),
    "ttr": ([128, 6032], F32),
    "ttm": ([128, 4000], F32),
    "mla_win": ([2, 6, 128, 8, 128], F32),
    "mla_wuq": ([2, 16, 128, 3, 256], F32),
    "mla_wukv": ([2, 16, 128, 2, 256], F32),
    "mla_wo": ([2, 8, 128, 16, 128], F32),
    "diff_wqk": ([16, 128, 8, 128], F32),
    "diff_wv": ([2, 128, 8, 512], F32),
    "diff_lam": ([1, 256], F32),
    "diff_wo": ([8, 128, 8, 128], F32),
    "lru_win": ([24, 128, 8, 128], F32),
    "lru_wg": ([48, 128, 2, 128], F32),
    "lru_wo": ([8, 128, 12, 128], F32),
    "ffn_w1": ([4, 22, 128, 8, 256], F32),
    "ffn_w2": ([4, 8, 128, 22, 128], F32),
}


def act_shapes(l):
    k = KINDS[l]
    d = {"hT%d" % l: ([D, NOWN], F32), "hT%d" % (l + 1): ([D, NOWN], F32)}
    if k == "mla":
        d.update({"ks%d" % l: ([320, NOWN], BF16), "cqn%d" % l: ([384, NOWN], BF16),
                  "ksf%d" % l: ([2, 320, NOWN], BF16), "oall%d" % l: ([2048, NOWN], BF16)})
    elif k == "diff":
        d.update({"ksk%d" % l: ([D, NOWN], BF16), "ksv%d" % l: ([NOWN, D], BF16), "qT%d" % l: ([D, NOWN], BF16),
                  "ksfk%d" % l: ([4, 512, NOWN], BF16), "ksfv%d" % l: ([4, 1056, D], BF16),
                  "oall%d" % l: ([D, NOWN], BF16)})
    else:
        d.update({"ksx%d" % l: ([1536, NOWN], F32), "gg%d" % l: ([1536, NOWN], BF16),
                  "ksfx%d" % l: ([12, 256, NOWN], F32), "oall%d" % l: ([1536, NOWN], BF16)})
    return d


def build_program(stages, ext_in, ext_out):
    nc = bass.Bass("TRN2", target_bir_lowering=False)
    with ExitStack() as st:
        b = Bld(nc, st)
        shapes = {}
        for (_, l) in stages:
            shapes.update(act_shapes(l))
        used_w = set(["vecs", "cst"])
        for (ph, l) in stages:
            k = KINDS[l]
            if ph == "A":
                used_w |= {"mla": {"mla_win", "cs"}, "diff": {"diff_wqk", "diff_wv"}, "lru": {"lru_win"}}[k]
            elif ph == "B":
                used_w |= {"mla": {"mla_wuq", "mla_wukv", "cs"}, "diff": {"diff_lam", "ttr", "ttm"}, "lru": {"lru_wg"}}[k]
            elif ph == "C":
                used_w |= {"ffn_w1", "ffn_w2", {"mla": "mla_wo", "diff": "diff_wo", "lru": "lru_wo"}[k]}
        for name in sorted(used_w):
            shp, dt = WEIGHT_SHAPES[name]
            b.dram(name, shp, dt, "ExternalInput")
        needed = set()
        for (ph, l) in stages:
            k = KINDS[l]
            if ph == "A":
                needed |= {"hT%d" % l} | {"mla": {"ks%d" % l, "cqn%d" % l}, "diff": {"ksk%d" % l, "ksv%d" % l, "qT%d" % l},
                                          "lru": {"ksx%d" % l, "gg%d" % l}}[k]
            elif ph == "B":
                needed |= {"oall%d" % l} | {"mla": {"ksf%d" % l, "cqn%d" % l}, "diff": {"ksfk%d" % l, "ksfv%d" % l, "qT%d" % l},
                                            "lru": {"ksfx%d" % l, "gg%d" % l}}[k]
            elif ph == "C":
                needed |= {"oall%d" % l, "hT%d" % l, "hT%d" % (l + 1)}
            elif ph == "X":
                needed |= {"mla": {"ks%d" % l, "ksf%d" % l}, "diff": {"ksk%d" % l, "ksv%d" % l, "ksfk%d" % l, "ksfv%d" % l},
                           "lru": {"ksx%d" % l, "ksfx%d" % l}}[k]
        for name in sorted(needed):
            shp, dt = shapes[name]
            kind = "ExternalInput" if name in ext_in else ("ExternalOutput" if name in ext_out else "Internal")
            b.dram(name, shp, dt, kind)
        b.setup_consts(b.dr["cst"])
        for (ph, l) in stages:
            k = KINDS[l]
            if ph == "A":
                getattr(b, "phaseA_" + k)(l)
            elif ph == "B":
                getattr(b, "phaseB_" + k)(l)
            elif ph == "X":
                b.exchange(l)
            elif ph == "C":
                b.phaseC(l, {"mla": 16, "diff": 8, "lru": 12}[k], {"mla": "mla_wo", "diff": "diff_wo", "lru": "lru_wo"}[k])
        b.P.barrier()
        b.P.add("sp", lambda e: None, [], [])
        b.P.emit()
    return nc, sorted(used_w)


def tile_w(w, KC, mw):
    K, M = w.shape
    assert K == KC * 128 and M % mw == 0
    return np.ascontiguousarray(w.reshape(KC, 128, M // mw, mw).transpose(2, 1, 0, 3))


def col128(v):
    return v.reshape(-1, 128).T


def prep_weights(inp):
    f = np.float32
    W = {}
    vecs = np.zeros((4, 128, NV), f)
    for l in range(4):
        for k in range(4):
            vecs[l, :, 8 * k:8 * k + 8] = col128(inp["norm_g"][l, k])
        kind, j = KINDS[l], JIDX[l]
        if kind == "mla":
            vecs[l, :, 32:35] = col128(inp["mla_q_norm"][j])
            vecs[l, :, 35:37] = col128(inp["mla_kv_norm"][j])
        elif kind == "diff":
            vecs[l, :, 32:33] = col128(inp["diff_subln"][j])
        else:
            for jj in range(4):
                vecs[l, :, 32 + jj * 12:44 + jj * 12] = col128(inp["lru_conv_w"][j, jj])
            vecs[l, :, 80:92] = col128(inp["lru_conv_b"][j])
            for d in range(2):
                for g in range(2):
                    c0 = 92 + (d * 2 + g) * 12
                    vecs[l, :, c0:c0 + 12] = col128(inp["lru_b_gates"][j, d, g])
                vecs[l, :, 140 + d * 12:152 + d * 12] = col128(inp["lru_lambda"][j, d])
    W["vecs"] = vecs
    sw = np.concatenate([np.arange(32, 64), np.arange(0, 32)])
    win = []
    wuq = []
    for j in range(2):
        w = inp["mla_w_in"][j]
        ext = np.concatenate([w, w[:, 640:704][:, sw]], axis=1)
        win.append(tile_w(ext, 8, 128))
        q = inp["mla_w_uq"][j].reshape(384, 16, 192)
        qe = np.concatenate([q, q[:, :, 128:192][:, :, sw]], axis=2)
        wuq.append(tile_w(qe.reshape(384, 4096), 3, 256))
    W["mla_win"] = np.stack(win)
    W["mla_wuq"] = np.stack(wuq)
    W["mla_wukv"] = np.stack([tile_w(inp["mla_w_ukv"][j], 2, 256) for j in range(2)])
    W["mla_wo"] = np.stack([tile_w(inp["mla_w_o"][j], 16, 128) for j in range(2)])
    dw = inp["diff_w_in"][0]
    W["diff_wqk"] = tile_w(dw[:, 0:2048], 8, 128)
    W["diff_wv"] = tile_w(dw[:, 2048:3072], 8, 512)
    W["diff_lam"] = np.ascontiguousarray(inp["diff_lambda"][0].reshape(1, 256))
    W["diff_wo"] = tile_w(inp["diff_w_o"][0], 8, 128)
    W["lru_win"] = tile_w(inp["lru_w_in"][0], 8, 128)
    wg = inp["lru_w_gates"][0]
    wg = wg.reshape(2, 2, 6, 2, 128, 2, 128)
    W["lru_wg"] = np.ascontiguousarray(wg.transpose(0, 1, 2, 5, 4, 3, 6)).reshape(48, 128, 2, 128)
    W["lru_wo"] = tile_w(inp["lru_w_o"][0], 12, 128)
    w1 = []
    w2 = []
    for l in range(4):
        wi = inp["ffn_w_in"][l]
        g = wi[:, :2816].reshape(1024, 22, 128)
        u = wi[:, 2816:].reshape(1024, 22, 128)
        w1.append(tile_w(np.concatenate([g, u], axis=2).reshape(1024, 22 * 256), 8, 256))
        w2.append(tile_w(inp["ffn_w_out"][l], 22, 128))
    W["ffn_w1"] = np.stack(w1)
    W["ffn_w2"] = np.stack(w2)
    return {k: np.ascontiguousarray(v, dtype=f) for k, v in W.items()}


def core_consts(r):
    f = np.float32
    pos = np.concatenate([NMETA + NREAL * r + np.arange(NREAL), np.arange(NMETA)]).astype(f)
    inv = (10000.0 ** (-np.arange(0, 64, 2, dtype=f) / 64)).astype(f)
    ang = pos[None, :] * inv[:, None]
    c, s = np.cos(ang).astype(f), np.sin(ang).astype(f)
    cs = np.concatenate([c, c, -s, s], axis=0)
    p = np.arange(128, dtype=np.float64)[:, None]
    ttr = np.abs(np.arange(6032, dtype=np.float64)[None, :] - 3968 - p + NREAL * r).astype(f)
    ttm = np.abs(np.arange(4000, dtype=np.float64)[None, :] - 3984 - p).astype(f)
    cst = np.zeros((128, 4), f)
    cst[:, 0] = EPS
    cst[:, 1] = 1.0
    cst[:, 2 + r] = 1.0
    return {"cs": cs, "ttr": ttr, "ttm": ttm, "cst": cst}


LAUNCHES = [
    ([("A", 0)], ["hT0"], ["ks0", "cqn0"]),
    ([("B", 0), ("C", 0), ("A", 1)], ["ksf0", "cqn0", "hT0"], ["hT1", "ksk1", "ksv1", "qT1"]),
    ([("B", 1), ("C", 1), ("A", 2)], ["ksfk1", "ksfv1", "qT1", "hT1"], ["hT2", "ksx2", "gg2"]),
    ([("B", 2), ("C", 2), ("A", 3)], ["ksfx2", "gg2", "hT2"], ["hT3", "ks3", "cqn3"]),
    ([("B", 3), ("C", 3)], ["ksf3", "cqn3", "hT3"], ["hT4"]),
]
FUSED_LAUNCH = ([(ph, l) for l in range(4) for ph in ("A", "X", "B", "C")], ["hT0"], ["hT4"])
FUSED = True
EXCH = {"ks0": "ksf0", "ksk1": "ksfk1", "ksv1": "ksfv1", "ksx2": "ksfx2", "ks3": "ksf3"}

_PROG_CACHE = {}


def run_launch(li, W, consts, state, ncores):
    stages, ext_in, ext_out = FUSED_LAUNCH if li == "fused" else LAUNCHES[li]
    if li not in _PROG_CACHE:
        _PROG_CACHE[li] = build_program(stages, ext_in, ext_out)
    nc, used_w = _PROG_CACHE[li]
    in_maps = []
    for c in range(ncores):
        m = {}
        for name in used_w:
            m[name] = consts[c % 2][name] if name in consts[0] else W[name]
        for name in ext_in:
            m[name] = state[name][c]
        in_maps.append(m)
    res = run_bass_kernel_spmd(nc, in_maps, core_ids=list(range(ncores)))
    for name in ext_out:
        state[name] = [res.results[c][name] for c in range(ncores)]
    for name in ext_out:
        if name in EXCH:
            full = []
            for c in range(ncores):
                p = c // 2
                full.append(np.stack([state[name][2 * p], state[name][2 * p + 1]]))
            state[EXCH[name]] = full


def kernel(**inp):
    inp = {k: np.asarray(v) for k, v in inp.items()}
    x = inp["x"]
    Bn = x.shape[0]
    ncores = 2 * Bn
    W = prep_weights(inp)
    consts = [core_consts(0), core_consts(1)]
    meta = inp["meta_tokens"].astype(np.float32)
    state = {"hT0": []}
    for c in range(ncores):
        b, r = c // 2, c % 2
        own = np.concatenate([x[b, r * NREAL:(r + 1) * NREAL], meta], axis=0)
        state["hT0"].append(np.ascontiguousarray(own.T))
    if FUSED:
        run_launch("fused", W, consts, state, ncores)
    else:
        for li in range(len(LAUNCHES)):
            run_launch(li, W, consts, state, ncores)
    out = np.zeros((Bn, 2 * NREAL, D), np.float32)
    for c in range(ncores):
        b, r = c // 2, c % 2
        out[b, r * NREAL:(r + 1) * NREAL] = state["hT4"][c][:, 0:NREAL].T
    return out
```

```python
import math
from contextlib import ExitStack

import numpy as np
import ml_dtypes
import concourse.bass as bass
import concourse.mybir as mybir
from concourse.bass_utils import run_bass_kernel_spmd

F32 = mybir.dt.float32
BF16 = mybir.dt.bfloat16
ALU = mybir.AluOpType
AF = mybir.ActivationFunctionType
NPBF = ml_dtypes.bfloat16

D = 1024
NREAL = 2048
NMETA = 16
NOWN = NREAL + NMETA
NFULL = 2 * NREAL + NMETA
SBW = 688
SUBS = [(0, 512), (512, 176)]
SUPER = [(0, SUBS), (688, SUBS), (1376, SUBS)]
KT = [(t * 128, 128) for t in range(32)] + [(4096, 16)]
QB = [(0, 512), (512, 512), (1024, 512), (1536, 512), (2048, 16)]
KB = [(i * 512, 512) for i in range(8)] + [(4096, 16)]
NV = 164
AW = 48000
KINDS = ["mla", "diff", "lru", "mla"]
JIDX = [0, 0, 0, 1]
EPS = 1e-6


class Op:
    __slots__ = ("eng", "fn", "deps", "signal", "sigval", "dma", "sem", "cc", "idx")

    def __init__(self, eng, fn, dma, cc=False):
        self.cc = cc
        self.eng = eng
        self.fn = fn
        self.deps = []
        self.signal = dma
        self.sigval = 0
        self.dma = dma
        self.sem = None


class Prog:
    ENGS = ("pe", "act", "dve", "pool", "sp")
    NDMASEM = 8

    def __init__(self, nc):
        self.nc = nc
        self.ops = []
        self.last_w = {}
        self.readers = {}
        self.kids = {}
        self.fence = []
        self.fence_set = set()

    def _keys(self, b):
        if "#" in b:
            par = b.split("#")[0]
            self.kids.setdefault(par, set()).add(b)
            return (b, par)
        return (b,) + tuple(self.kids.get(b, ()))

    def add(self, eng, fn, reads=(), writes=(), dma=False, cc=False):
        op = Op(eng, fn, dma, cc)
        deps = set()
        for b in reads:
            for k in self._keys(b):
                w = self.last_w.get(k)
                if w is not None:
                    deps.add(w)
        for b in writes:
            for k in self._keys(b):
                w = self.last_w.get(k)
                if w is not None:
                    deps.add(w)
                deps.update(self.readers.get(k, ()))
        for b in reads:
            self.readers.setdefault(b, []).append(op)
        for b in writes:
            self.last_w[b] = op
            self.readers[b] = []
            if "#" not in b:
                for k in self.kids.get(b, ()):
                    self.last_w[k] = op
                    self.readers[k] = []
        deps.update(self.fence)
        deps.discard(op)
        latest = {}
        final = []
        for d in deps:
            if d.dma:
                final.append(d)
            else:
                cur = latest.get(d.eng)
                if cur is None or cur.idx < d.idx:
                    latest[d.eng] = d
        final.extend(latest.values())
        for d in final:
            if d.eng == "pe" and eng == "pe" and not d.dma and not dma and d not in self.fence_set:
                continue
            d.signal = True
            op.deps.append(d)
        op.idx = len(self.ops)
        self.ops.append(op)
        return op

    def barrier(self):
        tails = {}
        for op in self.ops:
            tails[(op.eng, op.dma)] = op
        dm = {}
        for op in self.ops:
            if op.dma:
                dm.setdefault(op.eng, []).append(op)
        fence = list(tails.values())
        for e, lst in dm.items():
            fence.extend(lst[-self.NDMASEM:])
        self.fence = fence
        self.fence_set = set(fence)
        for o in fence:
            o.signal = True

    def emit(self):
        nc = self.nc
        cnt = {e: 0 for e in self.ENGS}
        dma_rr = {e: 0 for e in self.ENGS}
        dma_last = {}
        dma_cnt = {}
        for op in self.ops:
            if op.cc:
                k = ("cc", 0)
                dma_cnt[k] = dma_cnt.get(k, 0) + 1
                op.sem = k
                op.sigval = dma_cnt[k]
            elif op.dma:
                k = (op.eng, dma_rr[op.eng] % self.NDMASEM)
                dma_rr[op.eng] += 1
                prev = dma_last.get(k)
                if prev is not None:
                    op.deps.append(prev)
                dma_last[k] = op
                dma_cnt[k] = dma_cnt.get(k, 0) + 16
                op.sem = k
                op.sigval = dma_cnt[k]
            elif op.signal:
                cnt[op.eng] += 1
                op.sem = op.eng
                op.sigval = cnt[op.eng]
        with ExitStack() as st:
            sems = {}
            for e in self.ENGS:
                sems[e] = st.enter_context(nc.semaphore("s_" + e))
            for k in dma_cnt:
                sems[k] = st.enter_context(nc.semaphore("d_%s%d" % k))
            block = st.enter_context(nc.Block())
            engobj = {"pe": block.tensor, "act": block.scalar, "dve": block.vector,
                      "pool": block.gpsimd, "sp": block.sync}
            for e in self.ENGS:
                myops = [op for op in self.ops if op.eng == e]
                if not myops:
                    continue

                def body(eng, myops=myops):
                    waited = {}
                    for op in myops:
                        need = {}
                        for d in op.deps:
                            if need.get(d.sem, 0) < d.sigval:
                                need[d.sem] = d.sigval
                        for k, v in need.items():
                            if waited.get(k, 0) < v:
                                eng.wait_ge(sems[k], v)
                                waited[k] = v
                        ins = op.fn(eng)
                        if op.signal:
                            ins.then_inc(sems[op.sem], 16 if (op.dma and not op.cc) else 1)

                engobj[e](body)


class T:
    def __init__(self, ap, name):
        self.ap = ap
        self.name = name


class Rot:
    def __init__(self, items):
        self.items = items
        self.i = 0

    def get(self):
        t = self.items[self.i % len(self.items)]
        self.i += 1
        return t


class Bld:
    def __init__(self, nc, st):
        self.nc = nc
        self.P = Prog(nc)
        self.arena = st.enter_context(nc.sbuf_tensor("arena", [128, AW], F32))
        self.psum = st.enter_context(nc.psum_tensor("psum", [128, 8, 512], F32))
        self.off = 0
        self.gen = 0
        self.dr = {}
        self.ev = 0
        self.psr = Rot([self.ps(i) for i in range(4, 8)])

    def alloc(self, name, shape, dt):
        n = 1
        for s in shape:
            n *= s
        size = 4 if dt == F32 else 2
        words = (n * size + 3) // 4
        words = (words + 7) // 8 * 8
        assert self.off + words <= AW, (name, self.off, words)
        ap = self.arena[:, self.off:self.off + words]
        self.off += words
        if dt != F32:
            ap = ap.bitcast(dt)
        ap = ap[:, 0:n]
        if len(shape) == 2:
            ap = ap.rearrange("p (a b) -> p a b", b=shape[1])
        elif len(shape) == 3:
            ap = ap.rearrange("p (a b c) -> p a b c", b=shape[1], c=shape[2])
        return T(ap, "%s@%d" % (name, self.gen))

    def phase(self, keep):
        self.P.barrier()
        self.off = keep
        self.gen += 1

    def ps(self, i):
        return T(self.psum[:, i, :], "ps%d" % i)

    def dram(self, name, shape, dt, kind):
        t = self.nc.dram_tensor(name, list(shape), dt, kind=kind).ap()
        self.dr[name] = t
        return t

    def mm(self, out, lhsT, rhs, start, stop, r, w):
        self.P.add("pe", lambda e: e.matmul(out, lhsT, rhs, start=start, stop=stop), r, w)

    def act(self, out, in_, func, r, w, bias=None, scale=None):
        kw = {}
        if bias is not None:
            kw["bias"] = bias
        if scale is not None:
            kw["scale"] = scale
        self.P.add("act", lambda e: e.activation(out, in_, func, **kw), r, w)

    def tt(self, out, in0, in1, op, r, w, eng="dve"):
        self.P.add(eng, lambda e: e.tensor_tensor(out, in0, in1, op), r, w)

    def ts(self, out, in0, s1, s2, op0, op1, r, w, eng="dve"):
        if s2 is None:
            self.P.add(eng, lambda e: e.tensor_scalar(out, in0, s1, None, op0), r, w)
        else:
            self.P.add(eng, lambda e: e.tensor_scalar(out, in0, s1, s2, op0, op1), r, w)

    def stt(self, out, in0, scalar, in1, op0, op1, r, w, eng="dve"):
        self.P.add(eng, lambda e: e.scalar_tensor_tensor(out, in0, scalar, in1, op0, op1), r, w)

    def copy(self, out, in_, r, w, eng=None):
        if eng is None:
            eng = "act" if self.ev % 2 == 0 else "dve"
            self.ev += 1
        if eng == "act":
            self.P.add("act", lambda e: e.copy(out, in_), r, w)
        else:
            self.P.add(eng, lambda e: e.tensor_copy(out, in_), r, w)

    def dma(self, out, in_, r, w, q="sp"):
        self.P.add(q, lambda e: e.dma_start(out=out, in_=in_), r, w, dma=True)

    def memset(self, ap, val, w, eng="dve"):
        self.P.add(eng, lambda e: e.memset(ap, val), (), w)

    def setup_consts(self, cst_dram):
        self.ones = self.alloc("ones", [128], BF16)
        self.cst = self.alloc("cst", [4], F32)
        self.memset(self.ones.ap, 1.0, [self.ones.name])
        self.onesf = self.alloc("onesf", [128], F32)
        self.memset(self.onesf.ap, 1.0, [self.onesf.name])
        self.dma(self.cst.ap, cst_dram, [], [self.cst.name])
        self.eps = self.cst.ap[:, 0:1]
        self.one = self.cst.ap[:, 1:2]
        self.base = self.off

    def load_vecs(self, l):
        v = self.alloc("vec", [NV], F32)
        self.dma(v.ap, self.dr["vecs"][l], [], [v.name])
        return v

    def wload(self, rot, src, shape):
        t = rot.get()
        ap = t.ap
        if len(shape) == 2:
            dst = ap[:, 0:shape[0], 0:shape[1]]
        else:
            dst = ap
        self.dma(dst, src, [], [t.name], q="pool")
        return t

    def rms_stats(self, x, C, subs, sq, rstd, dim):
        for si, (c0, n) in enumerate(subs):
            self.act(sq.ap[:, 0:C, c0:c0 + n], x.ap[:, 0:C, c0:c0 + n], AF.Square,
                     [x.name + "#%d" % si], [sq.name + "#%d" % si])
            ps = self.psr.get()
            for c in range(C):
                self.mm(ps.ap[:, 0:n], self.ones.ap, sq.ap[:, c, c0:c0 + n], c == 0, c == C - 1,
                        [sq.name + "#%d" % si, self.ones.name], [ps.name])
            self.act(rstd.ap[:, c0:c0 + n], ps.ap[:, 0:n], AF.Sqrt, [ps.name, self.cst.name],
                     [rstd.name + "#%d" % si], bias=self.eps, scale=1.0 / dim)
            rs = rstd.ap[:, c0:c0 + n]
            self.P.add("dve", lambda e, rs=rs: e.reciprocal(rs, rs), [rstd.name + "#%d" % si],
                       [rstd.name + "#%d" % si])

    def load_h(self, hT, s0, subs, h):
        src = hT.rearrange("(c p) n -> p c n", p=128)
        for si, (c0, n) in enumerate(subs):
            self.dma(h.ap[:, :, c0:c0 + n], src[:, :, s0 + c0:s0 + c0 + n], [], [h.name + "#%d" % si])
        return h

    def norm_to_bf16(self, h, subs, vec, gcol, C, dim, bufs):
        sq, rstd, u = bufs
        self.rms_stats(h, C, subs, sq, rstd, dim)
        for si, (c0, n) in enumerate(subs):
            for c in range(C):
                self.stt(u.ap[:, c, c0:c0 + n], h.ap[:, c, c0:c0 + n], vec.ap[:, gcol + c:gcol + c + 1],
                         rstd.ap[:, c0:c0 + n], ALU.mult, ALU.mult,
                         [h.name + "#%d" % si, rstd.name + "#%d" % si, vec.name], [u.name + "#%d" % si])
        return u

    def exchange(self, l):
        k = KINDS[l]
        self.phase(self.base)
        pairs = []
        if k == "mla":
            pairs.append((self.dr["ks%d" % l], self.dr["ksf%d" % l].rearrange("r p n -> (r p) n")))
        elif k == "diff":
            for i in range(4):
                pairs.append((self.dr["ksk%d" % l][256 * i:256 * (i + 1), :], self.dr["ksfk%d" % l][i]))
            for i in range(4):
                rows = 512 if i < 3 else 528
                pairs.append((self.dr["ksv%d" % l][512 * i:512 * i + rows, :], self.dr["ksfv%d" % l][i, 0:2 * rows, :]))
        else:
            for i in range(12):
                pairs.append((self.dr["ksx%d" % l][128 * i:128 * (i + 1), :], self.dr["ksfx%d" % l][i]))
        for (src, dst) in pairs:
            self.P.add("pool", lambda e, src=src, dst=dst: e.collective_compute(
                "AllGather", ALU.bypass, replica_groups=[[0, 1], [2, 3], [4, 5], [6, 7]], ins=[src], outs=[dst]),
                [], [], dma=True, cc=True)

    def phaseA_mla(self, l):
        j = JIDX[l]
        hT = self.dr["hT%d" % l]
        ks = self.dr["ks%d" % l]
        cqd = self.dr["cqn%d" % l]
        win = self.dr["mla_win"]
        self.phase(self.base)
        vec = self.load_vecs(l)
        cs = self.alloc("cs", [NOWN], F32)
        self.dma(cs.ap, self.dr["cs"], [], [cs.name])
        hR = Rot([self.alloc("h%d" % i, [8, SBW], F32) for i in range(2)])
        nb = (self.alloc("sq", [8, SBW], BF16), self.alloc("rstd", [SBW], F32), self.alloc("u", [8, SBW], BF16))
        cq = self.alloc("cq", [3, SBW], F32)
        ckv = self.alloc("ckv", [2, SBW], F32)
        nbq = (nb[0], nb[1], self.alloc("cqn", [3, SBW], BF16))
        nbk = (nb[0], nb[1], self.alloc("ckvn", [2, SBW], BF16))
        krb = self.alloc("krb", [SBW], BF16)
        t1 = self.alloc("t1", [512], F32)
        t2 = self.alloc("t2", [512], F32)
        wrot = Rot([self.alloc("w%d" % i, [8, 128], BF16) for i in range(3)])
        hn = self.load_h(hT, SUPER[0][0], SUPER[0][1], hR.get())
        for k, (s0, subs) in enumerate(SUPER):
            W = sum(n for _, n in subs)
            h = hn
            if k + 1 < len(SUPER):
                hn = self.load_h(hT, SUPER[k + 1][0], SUPER[k + 1][1], hR.get())
            u = self.norm_to_bf16(h, subs, vec, 0, 8, 1024, nb)
            for m in range(6):
                wt = self.wload(wrot, win[j, m], [8, 128])
                for si, (c0, n) in enumerate(subs):
                    ps = self.psr.get()
                    for kc in range(8):
                        self.mm(ps.ap[:, 0:n], wt.ap[:, kc, :], u.ap[:, kc, c0:c0 + n], kc == 0, kc == 7,
                                [wt.name, u.name + "#%d" % si], [ps.name])
                    if m < 3:
                        self.copy(cq.ap[:, m, c0:c0 + n], ps.ap[:, 0:n], [ps.name], [cq.name + "#%d" % si])
                    elif m < 5:
                        self.copy(ckv.ap[:, m - 3, c0:c0 + n], ps.ap[:, 0:n], [ps.name], [ckv.name + "#%d" % si])
                    else:
                        g0 = s0 + c0
                        self.tt(t1.ap[0:64, 0:n], ps.ap[0:64, 0:n], cs.ap[0:64, g0:g0 + n], ALU.mult,
                                [ps.name, cs.name], [t1.name])
                        self.tt(t2.ap[0:64, 0:n], ps.ap[64:128, 0:n], cs.ap[64:128, g0:g0 + n], ALU.mult,
                                [ps.name, cs.name], [t2.name])
                        self.tt(krb.ap[0:64, c0:c0 + n], t1.ap[0:64, 0:n], t2.ap[0:64, 0:n], ALU.add,
                                [t1.name, t2.name], [krb.name + "#%d" % si], eng="pool")
            cqn = self.norm_to_bf16(cq, subs, vec, 32, 3, 384, nbq)
            ckvn = self.norm_to_bf16(ckv, subs, vec, 35, 2, 256, nbk)
            self.dma(cqd.rearrange("(c p) n -> p c n", p=128)[:, :, s0:s0 + W], cqn.ap[:, :, 0:W],
                     [cqn.name], ["cqn%d#%d" % (l, s0)])
            self.dma(ks[0:256, :].rearrange("(c p) n -> p c n", p=128)[:, :, s0:s0 + W], ckvn.ap[:, :, 0:W],
                     [ckvn.name], ["ks%d#a%d" % (l, s0)])
            self.dma(ks[256:320, s0:s0 + W], krb.ap[0:64, 0:W], [krb.name], ["ks%d#b%d" % (l, s0)])

    def phaseB_mla(self, l):
        j = JIDX[l]
        ksf = self.dr["ksf%d" % l]
        cqd = self.dr["cqn%d" % l]
        oall = self.dr["oall%d" % l]
        wuq = self.dr["mla_wuq"]
        wukv = self.dr["mla_wukv"]
        SCALE = 192.0 ** -0.5
        self.phase(self.base)
        cs = self.alloc("cs", [NOWN], F32)
        self.dma(cs.ap, self.dr["cs"], [], [cs.name])
        ckv = self.alloc("ckvf", [2, NFULL], BF16)
        kr = self.alloc("krf", [NFULL], BF16)
        cq = self.alloc("cqf", [3, NOWN], BF16)
        for r_ in range(2):
            self.dma(ckv.ap[:, :, r_ * 2048:(r_ + 1) * 2048],
                     ksf[r_, 0:256, 0:2048].rearrange("(c p) n -> p c n", p=128), ["ksf%d" % l], [ckv.name])
            self.dma(kr.ap[0:64, r_ * 2048:(r_ + 1) * 2048], ksf[r_, 256:320, 0:2048], ["ksf%d" % l], [kr.name])
        self.dma(ckv.ap[:, :, 4096:4112], ksf[0, 0:256, 2048:2064].rearrange("(c p) n -> p c n", p=128),
                 ["ksf%d" % l], [ckv.name])
        self.dma(kr.ap[0:64, 4096:4112], ksf[0, 256:320, 2048:2064], ["ksf%d" % l], [kr.name])
        self.dma(cq.ap, cqd.rearrange("(c p) n -> p c n", p=128), ["cqn%d" % l], [cq.name])
        KhR = Rot([self.alloc("Kh%d" % i, [NFULL], BF16) for i in range(2)])
        VhR = Rot([self.alloc("Vh%d" % i, [33, 128], BF16) for i in range(2)])
        QnR = Rot([self.alloc("Qn%d" % i, [NOWN], BF16) for i in range(2)])
        QrR = Rot([self.alloc("Qr%d" % i, [NOWN], BF16) for i in range(2)])
        for qt_ in QrR.items:
            self.memset(qt_.ap[64:128, :], 0.0, [qt_.name])
        self.memset(kr.ap[64:128, :], 0.0, [kr.name])
        OhR = Rot([self.alloc("Oh%d" % i, [NOWN], BF16) for i in range(2)])
        PtR = Rot([self.alloc("Pt%d" % i, [512], BF16) for i in range(6)])
        wqR = Rot([self.alloc("wq%d" % i, [3, 256], BF16) for i in range(2)])
        wkR = Rot([self.alloc("wk%d" % i, [2, 256], BF16) for i in range(2)])
        t1 = self.alloc("t1", [512], F32)
        t2 = self.alloc("t2", [512], F32)
        rsR = Rot([self.alloc("rs%d" % i, [512], F32) for i in range(2)])
        accR = Rot([(self.ps(0), self.ps(1)), (self.ps(2), self.ps(3))])
        saR = Rot([(self.alloc("saP%d" % i, [512], F32), self.alloc("saD%d" % i, [512], F32)) for i in range(2)])
        def gen_list(hh):
            wq = self.wload(wqR, wuq[j, hh], [3, 256])
            wk = self.wload(wkR, wukv[j, hh], [2, 256])
            Kh, Vh, Qn, Qr, Oh = KhR.get(), VhR.get(), QnR.get(), QrR.get(), OhR.get()
            ops = []

            def gK(c0, n):
                ps = self.psr.get()
                for kc in range(2):
                    self.mm(ps.ap[:, 0:n], wk.ap[:, kc, 0:128], ckv.ap[:, kc, c0:c0 + n], kc == 0, kc == 1,
                            [wk.name, ckv.name], [ps.name])
                self.copy(Kh.ap[:, c0:c0 + n], ps.ap[:, 0:n], [ps.name], [Kh.name + "#%d" % (c0 // 512)])

            def gV(g):
                ps = self.psr.get()
                tl = list(range(g, min(g + 4, 33)))
                for t in tl:
                    k0, kn = KT[t]
                    for kc in range(2):
                        self.mm(ps.ap[0:kn, (t - g) * 128:(t - g + 1) * 128], ckv.ap[:, kc, k0:k0 + kn],
                                wk.ap[:, kc, 128:256], kc == 0, kc == 1, [wk.name, ckv.name], [ps.name])
                if len(tl) == 4:
                    self.copy(Vh.ap[:, g:g + 4, :], ps.ap[:, 0:512].rearrange("p (a b) -> p a b", b=128),
                              [ps.name], [Vh.name + "#%d" % (g // 4)])
                else:
                    self.copy(Vh.ap[0:16, 32, :], ps.ap[0:16, 0:128], [ps.name], [Vh.name + "#8"])

            def gQ(qi, c0, n):
                ps = self.psr.get()
                for kc in range(3):
                    self.mm(ps.ap[:, 0:n], wq.ap[:, kc, 0:128], cq.ap[:, kc, c0:c0 + n], kc == 0, kc == 2,
                            [wq.name, cq.name], [ps.name])
                self.copy(Qn.ap[:, c0:c0 + n], ps.ap[:, 0:n], [ps.name], [Qn.name + "#%d" % qi])
                ps = self.psr.get()
                for kc in range(3):
                    self.mm(ps.ap[:, 0:n], wq.ap[:, kc, 128:256], cq.ap[:, kc, c0:c0 + n], kc == 0, kc == 2,
                            [wq.name, cq.name], [ps.name])
                self.tt(t1.ap[0:64, 0:n], ps.ap[0:64, 0:n], cs.ap[0:64, c0:c0 + n], ALU.mult,
                        [ps.name, cs.name], [t1.name])
                self.tt(t2.ap[0:64, 0:n], ps.ap[64:128, 0:n], cs.ap[64:128, c0:c0 + n], ALU.mult,
                        [ps.name, cs.name], [t2.name])
                self.tt(Qr.ap[0:64, c0:c0 + n], t1.ap[0:64, 0:n], t2.ap[0:64, 0:n], ALU.add,
                        [t1.name, t2.name], [Qr.name + "#%d" % qi], eng="pool")

            for (c0, n) in KB:
                ops.append(lambda c0=c0, n=n: gK(c0, n))
            for g in range(0, 33, 4):
                ops.append(lambda g=g: gV(g))
            for qi, (c0, n) in enumerate(QB):
                ops.append(lambda qi=qi, c0=c0, n=n: gQ(qi, c0, n))
            return (Kh, Vh, Qn, Qr, Oh), ops

        nxt_bufs, nxt_ops = gen_list(0)
        for f_ in nxt_ops:
            f_()
        for hh in range(16):
            Kh, Vh, Qn, Qr, Oh = nxt_bufs
            if hh + 1 < 16:
                nxt_bufs, nxt_ops = gen_list(hh + 1)
            else:
                nxt_ops = []
            per = (len(nxt_ops) + 3) // 4
            for qi, (q0, qn) in enumerate(QB):
                psO, psS = accR.get()
                saP, saD = saR.get()
                LAG = 2
                pend = []
                for i in range(33 + LAG):
                    if i < 33:
                        k0, kn = KT[i]
                        pT = self.psr.get()
                        self.mm(pT.ap[0:kn, 0:qn], Kh.ap[:, k0:k0 + kn], Qn.ap[:, q0:q0 + qn], True, False,
                                [Kh.name + "#%d" % (k0 // 512), Qn.name + "#%d" % qi], [pT.name])
                        self.mm(pT.ap[0:kn, 0:qn], kr.ap[:, k0:k0 + kn], Qr.ap[:, q0:q0 + qn], False, True,
                                [kr.name, Qr.name + "#%d" % qi], [pT.name])
                        pt = PtR.get()
                        self.act(pt.ap[0:kn, 0:qn], pT.ap[0:kn, 0:qn], AF.Exp, [pT.name], [pt.name], scale=SCALE)
                        pend.append((i, pt))
                    if i >= LAG:
                        t, pt = pend.pop(0)
                        k0, kn = KT[t]
                        self.mm(psO.ap[:, 0:qn], Vh.ap[0:kn, t, :], pt.ap[0:kn, 0:qn], t == 0, t == 32,
                                [Vh.name + "#%d" % (t // 4), pt.name], [psO.name])
                        if t % 3 == 0:
                            self.mm(psS.ap[:, 0:qn], self.ones.ap[0:kn, :], pt.ap[0:kn, 0:qn], t == 0, False,
                                    [self.ones.name, pt.name], [psS.name])
                        else:
                            sa, se = (saP, "pool") if t % 3 == 2 else (saD, "dve")
                            if t < 3:
                                self.copy(sa.ap[0:kn, 0:qn], pt.ap[0:kn, 0:qn], [pt.name], [sa.name], eng=se)
                            else:
                                self.tt(sa.ap[0:kn, 0:qn], sa.ap[0:kn, 0:qn], pt.ap[0:kn, 0:qn], ALU.add,
                                        [sa.name, pt.name], [sa.name], eng=se)
                self.mm(psS.ap[:, 0:qn], self.onesf.ap, saP.ap[:, 0:qn], False, False, [self.onesf.name, saP.name], [psS.name])
                self.mm(psS.ap[:, 0:qn], self.onesf.ap, saD.ap[:, 0:qn], False, True, [self.onesf.name, saD.name], [psS.name])
                rs = rsR.get()
                rsa = rs.ap[:, 0:qn]
                pss = psS.ap[:, 0:qn]
                self.P.add("dve", lambda e, rsa=rsa, pss=pss: e.reciprocal(rsa, pss), [psS.name], [rs.name])
                self.tt(Oh.ap[:, q0:q0 + qn], psO.ap[:, 0:qn], rsa, ALU.mult, [psO.name, rs.name],
                        [Oh.name + "#%d" % qi])
                if qi < 4:
                    for f_ in nxt_ops[qi * per:(qi + 1) * per]:
                        f_()
            self.dma(oall[hh * 128:(hh + 1) * 128, :], Oh.ap, [Oh.name], ["oall%d#%d" % (l, hh)])

    def phaseA_diff(self, l):
        hT = self.dr["hT%d" % l]
        ksk = self.dr["ksk%d" % l]
        ksv = self.dr["ksv%d" % l]
        qTd = self.dr["qT%d" % l]
        wqk = self.dr["diff_wqk"]
        wv = self.dr["diff_wv"]
        self.phase(self.base)
        vec = self.load_vecs(l)
        hR = Rot([self.alloc("h%d" % i, [8, SBW], F32) for i in range(2)])
        nb = (self.alloc("sq", [8, SBW], BF16), self.alloc("rstd", [SBW], F32), self.alloc("u", [8, SBW], BF16))
        qkR = Rot([self.alloc("qk%d" % i, [8, SBW], BF16) for i in range(2)])
        wrot = Rot([self.alloc("w%d" % i, [8, 128], BF16) for i in range(3)])
        wvR = Rot([self.alloc("wv%d" % i, [8, 512], BF16) for i in range(2)])
        vtR = Rot([self.alloc("vt%d" % i, [1024], BF16) for i in range(3)])
        wvt = [self.wload(wvR, wv[hf], [8, 512]) for hf in range(2)]
        hn = self.load_h(hT, SUPER[0][0], SUPER[0][1], hR.get())
        for k, (s0, subs) in enumerate(SUPER):
            W = sum(n for _, n in subs)
            h = hn
            if k + 1 < len(SUPER):
                hn = self.load_h(hT, SUPER[k + 1][0], SUPER[k + 1][1], hR.get())
            u = self.norm_to_bf16(h, subs, vec, 0, 8, 1024, nb)
            for part, dst in ((0, qTd), (1, ksk)):
                qk = qkR.get()
                for m in range(8):
                    wt = self.wload(wrot, wqk[part * 8 + m], [8, 128])
                    for si, (c0, n) in enumerate(subs):
                        ps = self.psr.get()
                        for kc in range(8):
                            self.mm(ps.ap[:, 0:n], wt.ap[:, kc, :], u.ap[:, kc, c0:c0 + n], kc == 0, kc == 7,
                                    [wt.name, u.name + "#%d" % si], [ps.name])
                        self.copy(qk.ap[:, m, c0:c0 + n], ps.ap[:, 0:n], [ps.name], [qk.name + "#%d" % si])
                self.dma(dst.rearrange("(c p) n -> p c n", p=128)[:, :, s0:s0 + W], qk.ap[:, :, 0:W],
                         [qk.name], ["%s%d#%d" % ("qT" if part == 0 else "ksk", l, s0)])
            toks = []
            for si, (c0, n) in enumerate(subs):
                for a_ in range(0, n, 128):
                    toks.append((si, c0 + a_, min(128, n - a_)))
            for (si, a0, an) in toks:
                vt = vtR.get()
                for hf in range(2):
                    ps = self.psr.get()
                    for kc in range(8):
                        self.mm(ps.ap[0:an, :], u.ap[:, kc, a0:a0 + an], wvt[hf].ap[:, kc, :], kc == 0, kc == 7,
                                [wvt[hf].name, u.name + "#%d" % si], [ps.name])
                    self.copy(vt.ap[0:an, hf * 512:(hf + 1) * 512], ps.ap[0:an, :], [ps.name], [vt.name])
                self.dma(ksv[s0 + a0:s0 + a0 + an, :], vt.ap[0:an, :], [vt.name], ["ksv%d#%d" % (l, s0 + a0)])

    def phaseB_diff(self, l):
        kf = self.dr["ksfk%d" % l]
        vf = self.dr["ksfv%d" % l]
        qTd = self.dr["qT%d" % l]
        oall = self.dr["oall%d" % l]
        lambda_init = 0.8 - 0.6 * math.exp(-0.3 * l)
        self.phase(self.base)
        vec = self.load_vecs(l)
        ttr = self.alloc("ttr", [6032], F32)
        ttm = self.alloc("ttm", [4000], F32)
        lp = self.alloc("lp", [256], F32)
        lt = self.alloc("lt", [8], F32)
        self.dma(ttr.ap, self.dr["ttr"], [], [ttr.name])
        self.dma(ttm.ap, self.dr["ttm"], [], [ttm.name])
        self.dma(lp.ap, self.dr["diff_lam"].partition_broadcast(128), [], [lp.name])
        self.tt(lp.ap[:, 0:64], lp.ap[:, 0:64], lp.ap[:, 64:128], ALU.mult, [lp.name], [lp.name])
        self.tt(lp.ap[:, 128:192], lp.ap[:, 128:192], lp.ap[:, 192:256], ALU.mult, [lp.name], [lp.name])
        a0 = lt.ap[:, 0:1]
        a1 = lt.ap[:, 1:2]
        l0 = lp.ap[:, 0:64]
        l1 = lp.ap[:, 128:192]
        self.P.add("dve", lambda e: e.reduce_sum(a0, l0, mybir.AxisListType.X), [lp.name], [lt.name])
        self.P.add("dve", lambda e: e.reduce_sum(a1, l1, mybir.AxisListType.X), [lp.name], [lt.name])
        self.act(lt.ap[:, 2:4], lt.ap[:, 0:2], AF.Exp, [lt.name], [lt.name])
        self.tt(lt.ap[:, 4:5], lt.ap[:, 3:4], lt.ap[:, 2:3], ALU.subtract, [lt.name], [lt.name])
        self.ts(lt.ap[:, 5:6], lt.ap[:, 4:5], -lambda_init, None, ALU.add, None, [lt.name], [lt.name])
        neglam = lt.ap[:, 5:6]
        KhR = Rot([self.alloc("Kh%d" % i, [NFULL], BF16) for i in range(2)])
        VhR = Rot([self.alloc("Vh%d" % i, [33, 128], BF16) for i in range(2)])
        QhR = Rot([(self.alloc("Qz0_%d" % i, [NOWN], BF16), self.alloc("Qz1_%d" % i, [NOWN], BF16)) for i in range(2)])
        for (qa_, qb_) in QhR.items:
            self.memset(qa_.ap[64:128, :], 0.0, [qa_.name])
            self.memset(qb_.ap[0:64, :], 0.0, [qb_.name])
        OhR = Rot([self.alloc("Oh%d" % i, [NOWN], BF16) for i in range(2)])
        PtR = Rot([self.alloc("Pt%d" % i, [512], BF16) for i in range(10)])
        PeR = Rot([self.alloc("Pe%d" % i, [512], BF16) for i in range(6)])
        EhR = Rot([(self.alloc("Er%d" % i, [6032], BF16), self.alloc("Em%d" % i, [4000], BF16)) for i in range(2)])
        accR = Rot([(self.ps(0), self.ps(1)), (self.ps(2), self.ps(3))])
        saR = Rot([self.alloc("sa%d" % i, [512], F32) for i in range(2)])
        o0 = self.alloc("o0", [512], F32)
        o1 = self.alloc("o1", [512], F32)
        r0 = self.alloc("r0", [512], F32)
        r1 = self.alloc("r1", [512], F32)
        osq = self.alloc("osq", [512], BF16)
        orr = self.alloc("orr", [512], F32)
        def load_head(hh):
            Kh, Vh, Qh, Oh = KhR.get(), VhR.get(), QhR.get(), OhR.get()
            rows = slice(hh * 128, (hh + 1) * 128)
            kc_, ko_ = hh // 2, (hh % 2) * 128
            for r_ in range(2):
                self.dma(Kh.ap[:, r_ * 2048:(r_ + 1) * 2048], kf[kc_, r_ * 256 + ko_:r_ * 256 + ko_ + 128, 0:2048],
                         ["ksfk%d" % l], [Kh.name])
                for i4 in range(4):
                    rws = 512 if i4 < 3 else 528
                    self.dma(Vh.ap[:, r_ * 16 + 4 * i4:r_ * 16 + 4 * i4 + 4, :],
                             vf[i4, r_ * rws:r_ * rws + 512, rows].rearrange("(t p) d -> p t d", p=128),
                             ["ksfv%d" % l], [Vh.name])
            self.dma(Kh.ap[:, 4096:4112], kf[kc_, ko_:ko_ + 128, 2048:2064], ["ksfk%d" % l], [Kh.name])
            self.dma(Vh.ap[0:16, 32, :], vf[3, 512:528, rows], ["ksfv%d" % l], [Vh.name])
            self.dma(Qh[0].ap[0:64, :], qTd[hh * 128:hh * 128 + 64, :], ["qT%d" % l], [Qh[0].name])
            self.dma(Qh[1].ap[64:128, :], qTd[hh * 128 + 64:hh * 128 + 128, :], ["qT%d" % l], [Qh[1].name])
            return Kh, Vh, Qh, Oh

        nxt = load_head(0)
        for hh in range(8):
            slope = 2.0 ** (-(hh + 1))
            Kh, Vh, Qh, Oh = nxt
            rows = slice(hh * 128, (hh + 1) * 128)
            if hh + 1 < 8:
                nxt = load_head(hh + 1)
            Er, Em = EhR.get()
            self.act(Er.ap, ttr.ap, AF.Exp, [ttr.name], [Er.name], scale=-slope)
            self.act(Em.ap, ttm.ap, AF.Exp, [ttm.name], [Em.name], scale=-slope)
            for qi, (q0, qn) in enumerate(QB):
                for c, (rr, oo) in enumerate(((r0, o0), (r1, o1))):
                    psO, psS = accR.get()
                    LAG = 5
                    pend = []
                    for i in range(33 + LAG):
                        if i < 33:
                            k0, kn = KT[i]
                            base_k = (NMETA + 128 * i) if i < 32 else 0
                            if qi < 4:
                                m0_ = (NMETA + 512 * qi) - base_k + 3968
                                Dta = Er.ap[0:kn, m0_:m0_ + qn]
                                Dtn = Er.name
                            else:
                                m0_ = 3984 - base_k
                                Dta = Em.ap[0:kn, m0_:m0_ + qn]
                                Dtn = Em.name
                            pT = self.psr.get()
                            self.mm(pT.ap[0:kn, 0:qn], Kh.ap[:, k0:k0 + kn],
                                    Qh[c].ap[:, q0:q0 + qn], True, True, [Kh.name, Qh[c].name], [pT.name])
                            pe = PeR.get()
                            self.act(pe.ap[0:kn, 0:qn], pT.ap[0:kn, 0:qn], AF.Exp, [pT.name], [pe.name], scale=0.125)
                            pt = PtR.get()
                            self.tt(pt.ap[0:kn, 0:qn], pe.ap[0:kn, 0:qn], Dta, ALU.mult, [pe.name, Dtn], [pt.name],
                                    eng=("dve" if i % 2 == 0 else "pool"))
                            pend.append((i, pt))
                        if i >= LAG:
                            t, pt = pend.pop(0)
                            k0, kn = KT[t]
                            self.mm(psO.ap[:, 0:qn], Vh.ap[0:kn, t, :], pt.ap[0:kn, 0:qn], t == 0, t == 32,
                                    [Vh.name, pt.name], [psO.name])
                            self.mm(psS.ap[:, 0:qn], self.ones.ap[0:kn, :], pt.ap[0:kn, 0:qn], t == 0, t == 32,
                                    [self.ones.name, pt.name], [psS.name])
                    rra = rr.ap[:, 0:qn]
                    pss = psS.ap[:, 0:qn]
                    self.P.add("dve", lambda e, rra=rra, pss=pss: e.reciprocal(rra, pss), [psS.name], [rr.name])
                    self.tt(oo.ap[:, 0:qn], psO.ap[:, 0:qn], rra, ALU.mult, [psO.name, rr.name], [oo.name])
                self.stt(o0.ap[:, 0:qn], o1.ap[:, 0:qn], neglam, o0.ap[:, 0:qn], ALU.mult, ALU.add,
                         [o0.name, o1.name, lt.name], [o0.name])
                self.act(osq.ap[:, 0:qn], o0.ap[:, 0:qn], AF.Square, [o0.name], [osq.name])
                ps = self.psr.get()
                self.mm(ps.ap[:, 0:qn], self.ones.ap, osq.ap[:, 0:qn], True, True, [osq.name, self.ones.name], [ps.name])
                self.act(orr.ap[:, 0:qn], ps.ap[:, 0:qn], AF.Sqrt, [ps.name, self.cst.name], [orr.name],
                         bias=self.eps, scale=1.0 / 128)
                ora = orr.ap[:, 0:qn]
                self.P.add("dve", lambda e, ora=ora: e.reciprocal(ora, ora), [orr.name], [orr.name])
                self.tt(o0.ap[:, 0:qn], o0.ap[:, 0:qn], ora, ALU.mult, [o0.name, orr.name], [o0.name])
                self.ts(Oh.ap[:, q0:q0 + qn], o0.ap[:, 0:qn], vec.ap[:, 32:33], 1.0 - lambda_init, ALU.mult, ALU.mult,
                        [o0.name, vec.name], [Oh.name + "#%d" % qi])
            self.dma(oall[rows, :], Oh.ap, [Oh.name], ["oall%d#%d" % (l, hh)])

    def phaseA_lru(self, l):
        hT = self.dr["hT%d" % l]
        ksx = self.dr["ksx%d" % l]
        ggd = self.dr["gg%d" % l]
        win = self.dr["lru_win"]
        self.phase(self.base)
        vec = self.load_vecs(l)
        hR = Rot([self.alloc("h%d" % i, [8, SBW], F32) for i in range(2)])
        nb = (self.alloc("sq", [8, SBW], BF16), self.alloc("rstd", [SBW], F32), self.alloc("u", [8, SBW], BF16))
        wrot = Rot([self.alloc("w%d" % i, [8, 128], BF16) for i in range(3)])
        xoR = Rot([self.alloc("xo%d" % i, [SBW], F32) for i in range(2)])
        goR = Rot([self.alloc("go%d" % i, [SBW], BF16) for i in range(2)])
        gx = self.alloc("gx", [512], F32)
        g2 = self.alloc("g2", [512], F32)
        hn = self.load_h(hT, SUPER[0][0], SUPER[0][1], hR.get())
        for k, (s0, subs) in enumerate(SUPER):
            W = sum(n for _, n in subs)
            h = hn
            if k + 1 < len(SUPER):
                hn = self.load_h(hT, SUPER[k + 1][0], SUPER[k + 1][1], hR.get())
            u = self.norm_to_bf16(h, subs, vec, 0, 8, 1024, nb)
            for m in range(24):
                wt = self.wload(wrot, win[m], [8, 128])
                xo = xoR.get() if m < 12 else goR.get()
                for si, (c0, n) in enumerate(subs):
                    ps = self.psr.get()
                    for kc in range(8):
                        self.mm(ps.ap[:, 0:n], wt.ap[:, kc, :], u.ap[:, kc, c0:c0 + n], kc == 0, kc == 7,
                                [wt.name, u.name + "#%d" % si], [ps.name])
                    if m < 12:
                        self.copy(xo.ap[:, c0:c0 + n], ps.ap[:, 0:n], [ps.name], [xo.name + "#%d" % si])
                    else:
                        gxa = gx.ap[:, 0:n]
                        g2a = g2.ap[:, 0:n]
                        self.copy(gxa, ps.ap[:, 0:n], [ps.name], [gx.name], eng="act")
                        self.act(g2a, ps.ap[:, 0:n], AF.Square, [ps.name], [g2.name], scale=math.sqrt(0.044715))
                        self.stt(g2a, g2a, 1.0, gxa, ALU.add, ALU.mult, [g2.name, gx.name], [g2.name])
                        self.act(g2a, g2a, AF.Sigmoid, [g2.name], [g2.name], scale=1.5957691216057308)
                        self.tt(xo.ap[:, c0:c0 + n], g2a, gxa, ALU.mult, [g2.name, gx.name], [xo.name + "#%d" % si],
                                eng="pool")
                if m < 12:
                    self.dma(ksx[m * 128:(m + 1) * 128, s0:s0 + W], xo.ap[:, 0:W], [xo.name], ["ksx%d#%d_%d" % (l, m, s0)])
                else:
                    mm_ = m - 12
                    self.dma(ggd[mm_ * 128:(mm_ + 1) * 128, s0:s0 + W], xo.ap[:, 0:W], [xo.name],
                             ["gg%d#%d_%d" % (l, mm_, s0)])

    def phaseB_lru(self, l):
        xf = self.dr["ksfx%d" % l]
        ggd = self.dr["gg%d" % l]
        oall = self.dr["oall%d" % l]
        wg = self.dr["lru_wg"]
        self.phase(self.base)
        vec = self.load_vecs(l)
        cn = self.alloc("cneg", [24], F32)
        cn2 = self.alloc("cneg2", [24], F32)
        self.act(cn.ap, vec.ap[:, 140:164], AF.Exp, [vec.name], [cn.name], scale=-1.0)
        self.act(cn.ap, cn.ap, AF.Ln, [cn.name, self.cst.name], [cn.name], bias=self.one, scale=1.0)
        self.ts(cn.ap, cn.ap, -8.0, None, ALU.mult, None, [cn.name], [cn.name])
        self.ts(cn2.ap, cn.ap, 2.0, None, ALU.mult, None, [cn.name], [cn2.name])
        L = NFULL
        xb = self.alloc("xb", [2, L + 3], F32)
        xc = self.alloc("xc", [2, L], F32)
        xcb = self.alloc("xcb", [2, L], BF16)
        taA = self.alloc("ta", [L], F32)
        tiA = self.alloc("ti", [L], F32)
        hsA = self.alloc("hsA", [L], F32)
        hsB = self.alloc("hsB", [L], F32)
        yo = self.alloc("yo", [NOWN], F32)
        ggt = self.alloc("ggt", [NOWN], BF16)
        yb = self.alloc("yb", [NOWN], BF16)
        wgR = Rot([self.alloc("wg%d" % i, [2, 128], BF16) for i in range(16)])

        def load_gate_w(b_):
            out = {}
            for c2_ in range(2):
                for d_ in range(2):
                    for g_ in range(2):
                        out[(c2_, d_, g_)] = self.wload(wgR, wg[((d_ * 2 + g_) * 6 + b_) * 2 + c2_], [2, 128])
            return out

        gw_next = load_gate_w(0)
        m0 = self.cst.ap[:, 2:3]
        m1 = self.cst.ap[:, 3:4]
        setA = (taA.ap, taA.name, tiA.ap, tiA.name, hsA)
        setB = (xb.ap[:, 0, 0:L], xb.name + "#0", xb.ap[:, 1, 0:L], xb.name + "#1", hsB)
        for b in range(6):
            for c2 in range(2):
                ct = 2 * b + c2
                xn = xb.name + "#%d" % c2
                self.memset(xb.ap[:, c2, 0:2], 0.0, [xn])
                self.memset(xb.ap[:, c2, L + 2:L + 3], 0.0, [xn])
                self.dma(xb.ap[:, c2, 2:18], xf[ct, 0:128, 2048:2064], ["ksfx%d" % l], [xn])
                self.dma(xb.ap[:, c2, 18:2066], xf[ct, 0:128, 0:2048], ["ksfx%d" % l], [xn])
                self.dma(xb.ap[:, c2, 2066:4114], xf[ct, 128:256, 0:2048], ["ksfx%d" % l], [xn])
                xca = xc.ap[:, c2, :]
                self.act(xca, xb.ap[:, c2, 0:L], AF.Identity, [xn, vec.name], [xc.name + "#%d" % c2],
                         bias=vec.ap[:, 80 + ct:81 + ct], scale=vec.ap[:, 32 + ct:33 + ct])
                for jj in range(1, 4):
                    col = 32 + jj * 12 + ct
                    self.stt(xca, xb.ap[:, c2, jj:jj + L], vec.ap[:, col:col + 1], xca, ALU.mult, ALU.add,
                             [xn, vec.name, xc.name + "#%d" % c2], [xc.name + "#%d" % c2])
                self.copy(xcb.ap[:, c2, :], xca, [xc.name + "#%d" % c2], [xcb.name + "#%d" % c2], eng="dve")
            gw = gw_next
            if b + 1 < 6:
                gw_next = load_gate_w(b + 1)
            for c2 in range(2):
                ct = 2 * b + c2
                for d in range(2):
                    ta_ap, ta_n, ti_ap, ti_n, hs = setA if d == 0 else setB
                    wr = gw[(c2, d, 0)]
                    wi = gw[(c2, d, 1)]
                    br = 92 + (d * 2 + 0) * 12 + ct
                    bi = 92 + (d * 2 + 1) * 12 + ct
                    for (c0, n) in KB:
                        for (wt, bcol, dap, dn) in ((wr, br, ta_ap, ta_n), (wi, bi, ti_ap, ti_n)):
                            ps = self.psr.get()
                            for kc in range(2):
                                self.mm(ps.ap[:, 0:n], wt.ap[:, kc, :], xcb.ap[:, kc, c0:c0 + n], kc == 0, kc == 1,
                                        [wt.name, xcb.name], [ps.name])
                            self.act(dap[:, c0:c0 + n], ps.ap[:, 0:n], AF.Sigmoid, [ps.name, vec.name],
                                     [dn], bias=vec.ap[:, bcol:bcol + 1], scale=1.0)
                    ccol = d * 12 + ct
                    self.act(hs.ap, ta_ap, AF.Exp, [ta_n, cn2.name], [hs.name], scale=cn2.ap[:, ccol:ccol + 1])
                    self.act(hs.ap, hs.ap, AF.Sqrt, [hs.name, self.cst.name], [hs.name], bias=self.one, scale=-1.0)
                    self.act(ta_ap, ta_ap, AF.Exp, [ta_n, cn.name], [ta_n], scale=cn.ap[:, ccol:ccol + 1])
                    self.tt(ti_ap, ti_ap, xc.ap[:, c2, :], ALU.mult, [ti_n, xc.name + "#%d" % c2], [ti_n])
                    self.tt(ti_ap, ti_ap, hs.ap, ALU.mult, [ti_n, hs.name], [ti_n])
                    dst = hs
                    if d == 0:
                        o_, a_, u_ = dst.ap, ta_ap, ti_ap
                    else:
                        o_, a_, u_ = dst.ap[:, ::-1], ta_ap[:, ::-1], ti_ap[:, ::-1]
                    self.P.add("dve", lambda e, o_=o_, a_=a_, u_=u_: e.tensor_tensor_scan(o_, a_, u_, 0.0, ALU.mult, ALU.add),
                               [ta_n, ti_n], [dst.name])
                self.ts(yo.ap[:, 0:2048], hsA.ap[:, 16:2064], m0, None, ALU.mult, None, [hsA.name, self.cst.name], [yo.name])
                self.stt(yo.ap[:, 0:2048], hsB.ap[:, 16:2064], m0, yo.ap[:, 0:2048], ALU.mult, ALU.add,
                         [hsB.name, yo.name, self.cst.name], [yo.name])
                self.stt(yo.ap[:, 0:2048], hsA.ap[:, 2064:4112], m1, yo.ap[:, 0:2048], ALU.mult, ALU.add,
                         [hsA.name, yo.name, self.cst.name], [yo.name])
                self.stt(yo.ap[:, 0:2048], hsB.ap[:, 2064:4112], m1, yo.ap[:, 0:2048], ALU.mult, ALU.add,
                         [hsB.name, yo.name, self.cst.name], [yo.name])
                self.tt(yo.ap[:, 2048:2064], hsA.ap[:, 0:16], hsB.ap[:, 0:16], ALU.add, [hsA.name, hsB.name], [yo.name])
                rows = slice(ct * 128, (ct + 1) * 128)
                self.dma(ggt.ap, ggd[rows, :], ["gg%d" % l], [ggt.name])
                self.tt(yb.ap, yo.ap, ggt.ap, ALU.mult, [yo.name, ggt.name], [yb.name], eng="pool")
                self.dma(oall[rows, :], yb.ap, [yb.name], ["oall%d#%d" % (l, ct)])

    def phaseC(self, l, KCo, wo_name):
        hT = self.dr["hT%d" % l]
        hTo = self.dr["hT%d" % (l + 1)]
        oall = self.dr["oall%d" % l]
        wo = self.dr[wo_name]
        w1 = self.dr["ffn_w1"]
        w2 = self.dr["ffn_w2"]
        self.phase(self.base)
        vec = self.load_vecs(l)
        hR = Rot([self.alloc("h%d" % i, [8, SBW], F32) for i in range(2)])
        u1 = self.alloc("u1", [8, SBW], F32)
        sq = self.alloc("sqc", [8, SBW], BF16)
        rstd = self.alloc("rstdc", [SBW], F32)
        u3 = self.alloc("u3", [8, SBW], BF16)
        hid = self.alloc("hid", [22, SBW], BF16)
        oin = self.alloc("oin", [KCo, SBW], BF16)
        sgR = Rot([self.alloc("sg%d" % i, [512], F32) for i in range(2)])
        w1R = Rot([self.alloc("w1_%d" % i, [8, 256], BF16) for i in range(2)])
        w2R = Rot([self.alloc("w2_%d" % i, [22, 128], BF16) for i in range(2)])
        woR = Rot([self.alloc("wo_%d" % i, [KCo, 128], BF16) for i in range(2)])
        src = oall.rearrange("(c p) n -> p c n", p=128)
        dst = hTo.rearrange("(c p) n -> p c n", p=128)

        def load_oin(s0, W):
            for kc in range(KCo):
                self.dma(oin.ap[:, kc, 0:W], src[:, kc, s0:s0 + W], ["oall%d" % l], [oin.name])

        hn = self.load_h(hT, SUPER[0][0], SUPER[0][1], hR.get())
        load_oin(SUPER[0][0], SBW)
        for k, (s0, subs) in enumerate(SUPER):
            h = hn
            for m in range(8):
                wt = self.wload(woR, wo[JIDX[l], m] if wo_name == "mla_wo" else wo[m], [KCo, 128])
                for si, (c0, n) in enumerate(subs):
                    ps = self.psr.get()
                    for kc in range(KCo):
                        self.mm(ps.ap[:, 0:n], wt.ap[:, kc, :], oin.ap[:, kc, c0:c0 + n], kc == 0, kc == KCo - 1,
                                [wt.name, oin.name], [ps.name])
                    self.copy(u1.ap[:, m, c0:c0 + n], ps.ap[:, 0:n], [ps.name], [u1.name + "#%d" % si])
            if k + 1 < len(SUPER):
                hn = self.load_h(hT, SUPER[k + 1][0], SUPER[k + 1][1], hR.get())
                load_oin(SUPER[k + 1][0], SBW)
            self.resid_norm(h, u1, subs, sq, rstd, vec, 8)
            self.rms_stats(h, 8, subs, sq, rstd, 1024)
            for si, (c0, n) in enumerate(subs):
                for c in range(8):
                    self.stt(u3.ap[:, c, c0:c0 + n], h.ap[:, c, c0:c0 + n], vec.ap[:, 16 + c:17 + c],
                             rstd.ap[:, c0:c0 + n], ALU.mult, ALU.mult,
                             [h.name + "#%d" % si, rstd.name + "#%d" % si, vec.name], [u3.name + "#%d" % si])
            for jn in range(22):
                wt = self.wload(w1R, w1[l, jn], [8, 256])
                for si, (c0, n) in enumerate(subs):
                    pg = self.psr.get()
                    for kc in range(8):
                        self.mm(pg.ap[:, 0:n], wt.ap[:, kc, 0:128], u3.ap[:, kc, c0:c0 + n], kc == 0, kc == 7,
                                [wt.name, u3.name + "#%d" % si], [pg.name])
                    pu = self.psr.get()
                    for kc in range(8):
                        self.mm(pu.ap[:, 0:n], wt.ap[:, kc, 128:256], u3.ap[:, kc, c0:c0 + n], kc == 0, kc == 7,
                                [wt.name, u3.name + "#%d" % si], [pu.name])
                    sg = sgR.get()
                    self.act(sg.ap[:, 0:n], pg.ap[:, 0:n], AF.Silu, [pg.name], [sg.name])
                    self.tt(hid.ap[:, jn, c0:c0 + n], sg.ap[:, 0:n], pu.ap[:, 0:n], ALU.mult, [sg.name, pu.name],
                            [hid.name + "#%d" % si])
            for m in range(8):
                wt = self.wload(w2R, w2[l, m], [22, 128])
                for si, (c0, n) in enumerate(subs):
                    ps = self.psr.get()
                    for kc in range(22):
                        self.mm(ps.ap[:, 0:n], wt.ap[:, kc, :], hid.ap[:, kc, c0:c0 + n], kc == 0, kc == 21,
                                [wt.name, hid.name + "#%d" % si], [ps.name])
                    self.copy(u1.ap[:, m, c0:c0 + n], ps.ap[:, 0:n], [ps.name], [u1.name + "#%d" % si])
            self.resid_norm(h, u1, subs, sq, rstd, vec, 24)
            for si, (c0, n) in enumerate(subs):
                self.dma(dst[:, :, s0 + c0:s0 + c0 + n], h.ap[:, :, c0:c0 + n], [h.name + "#%d" % si],
                         ["hT%d#%d" % (l + 1, s0 + c0)])

    def resid_norm(self, h, u1, subs, sq, rstd, vec, gcol):
        self.rms_stats(u1, 8, subs, sq, rstd, 1024)
        for si, (c0, n) in enumerate(subs):
            for c in range(8):
                ua = u1.ap[:, c, c0:c0 + n]
                self.tt(ua, ua, rstd.ap[:, c0:c0 + n], ALU.mult, [u1.name + "#%d" % si, rstd.name + "#%d" % si],
                        [u1.name + "#%d" % si], eng="pool")
                ha = h.ap[:, c, c0:c0 + n]
                self.stt(ha, ua, vec.ap[:, gcol + c:gcol + c + 1], ha, ALU.mult, ALU.add,
                         [u1.name + "#%d" % si, h.name + "#%d" % si, vec.name], [h.name + "#%d" % si])


WEIGHT_SHAPES = {
    "vecs": ([4, 128, NV], F32),
    "cst": ([128, 4], F32),
    "cs": ([128, NOWN], F32),
    "ttr": ([128, 6032], F32),
    "ttm": ([128, 4000], F32),
    "mla_win": ([2, 6, 128, 8, 128], F32),
    "mla_wuq": ([2, 16, 128, 3, 256], F32),
    "mla_wukv": ([2, 16, 128, 2, 256], F32),
    "mla_wo": ([2, 8, 128, 16, 128], F32),
    "diff_wqk": ([16, 128, 8, 128], F32),
    "diff_wv": ([2, 128, 8, 512], F32),
    "diff_lam": ([1, 256], F32),
    "diff_wo": ([8, 128, 8, 128], F32),
    "lru_win": ([24, 128, 8, 128], F32),
    "lru_wg": ([48, 128, 2, 128], F32),
    "lru_wo": ([8, 128, 12, 128], F32),
    "ffn_w1": ([4, 22, 128, 8, 256], F32),
    "ffn_w2": ([4, 8, 128, 22, 128], F32),
}


def act_shapes(l):
    k = KINDS[l]
    d = {"hT%d" % l: ([D, NOWN], F32), "hT%d" % (l + 1): ([D, NOWN], F32)}
    if k == "mla":
        d.update({"ks%d" % l: ([320, NOWN], BF16), "cqn%d" % l: ([384, NOWN], BF16),
                  "ksf%d" % l: ([2, 320, NOWN], BF16), "oall%d" % l: ([2048, NOWN], BF16)})
    elif k == "diff":
        d.update({"ksk%d" % l: ([D, NOWN], BF16), "ksv%d" % l: ([NOWN, D], BF16), "qT%d" % l: ([D, NOWN], BF16),
                  "ksfk%d" % l: ([4, 512, NOWN], BF16), "ksfv%d" % l: ([4, 1056, D], BF16),
                  "oall%d" % l: ([D, NOWN], BF16)})
    else:
        d.update({"ksx%d" % l: ([1536, NOWN], F32), "gg%d" % l: ([1536, NOWN], BF16),
                  "ksfx%d" % l: ([12, 256, NOWN], F32), "oall%d" % l: ([1536, NOWN], BF16)})
    return d


def build_program(stages, ext_in, ext_out):
    nc = bass.Bass("TRN2", target_bir_lowering=False)
    with ExitStack() as st:
        b = Bld(nc, st)
        shapes = {}
        for (_, l) in stages:
            shapes.update(act_shapes(l))
        used_w = set(["vecs", "cst"])
        for (ph, l) in stages:
            k = KINDS[l]
            if ph == "A":
                used_w |= {"mla": {"mla_win", "cs"}, "diff": {"diff_wqk", "diff_wv"}, "lru": {"lru_win"}}[k]
            elif ph == "B":
                used_w |= {"mla": {"mla_wuq", "mla_wukv", "cs"}, "diff": {"diff_lam", "ttr", "ttm"}, "lru": {"lru_wg"}}[k]
            elif ph == "C":
                used_w |= {"ffn_w1", "ffn_w2", {"mla": "mla_wo", "diff": "diff_wo", "lru": "lru_wo"}[k]}
        for name in sorted(used_w):
            shp, dt = WEIGHT_SHAPES[name]
            b.dram(name, shp, dt, "ExternalInput")
        needed = set()
        for (ph, l) in stages:
            k = KINDS[l]
            if ph == "A":
                needed |= {"hT%d" % l} | {"mla": {"ks%d" % l, "cqn%d" % l}, "diff": {"ksk%d" % l, "ksv%d" % l, "qT%d" % l},
                                          "lru": {"ksx%d" % l, "gg%d" % l}}[k]
            elif ph == "B":
                needed |= {"oall%d" % l} | {"mla": {"ksf%d" % l, "cqn%d" % l}, "diff": {"ksfk%d" % l, "ksfv%d" % l, "qT%d" % l},
                                            "lru": {"ksfx%d" % l, "gg%d" % l}}[k]
            elif ph == "C":
                needed |= {"oall%d" % l, "hT%d" % l, "hT%d" % (l + 1)}
            elif ph == "X":
                needed |= {"mla": {"ks%d" % l, "ksf%d" % l}, "diff": {"ksk%d" % l, "ksv%d" % l, "ksfk%d" % l, "ksfv%d" % l},
                           "lru": {"ksx%d" % l, "ksfx%d" % l}}[k]
        for name in sorted(needed):
            shp, dt = shapes[name]
            kind = "ExternalInput" if name in ext_in else ("ExternalOutput" if name in ext_out else "Internal")
            b.dram(name, shp, dt, kind)
        b.setup_consts(b.dr["cst"])
        for (ph, l) in stages:
            k = KINDS[l]
            if ph == "A":
                getattr(b, "phaseA_" + k)(l)
            elif ph == "B":
                getattr(b, "phaseB_" + k)(l)
            elif ph == "X":
                b.exchange(l)
            elif ph == "C":
                b.phaseC(l, {"mla": 16, "diff": 8, "lru": 12}[k], {"mla": "mla_wo", "diff": "diff_wo", "lru": "lru_wo"}[k])
        b.P.barrier()
        b.P.add("sp", lambda e: None, [], [])
        b.P.emit()
    return nc, sorted(used_w)


def tile_w(w, KC, mw):
    K, M = w.shape
    assert K == KC * 128 and M % mw == 0
    return np.ascontiguousarray(w.reshape(KC, 128, M // mw, mw).transpose(2, 1, 0, 3))


def col128(v):
    return v.reshape(-1, 128).T


def prep_weights(inp):
    f = np.float32
    W = {}
    vecs = np.zeros((4, 128, NV), f)
    for l in range(4):
        for k in range(4):
            vecs[l, :, 8 * k:8 * k + 8] = col128(inp["norm_g"][l, k])
        kind, j = KINDS[l], JIDX[l]
        if kind == "mla":
            vecs[l, :, 32:35] = col128(inp["mla_q_norm"][j])
            vecs[l, :, 35:37] = col128(inp["mla_kv_norm"][j])
        elif kind == "diff":
            vecs[l, :, 32:33] = col128(inp["diff_subln"][j])
        else:
            for jj in range(4):
                vecs[l, :, 32 + jj * 12:44 + jj * 12] = col128(inp["lru_conv_w"][j, jj])
            vecs[l, :, 80:92] = col128(inp["lru_conv_b"][j])
            for d in range(2):
                for g in range(2):
                    c0 = 92 + (d * 2 + g) * 12
                    vecs[l, :, c0:c0 + 12] = col128(inp["lru_b_gates"][j, d, g])
                vecs[l, :, 140 + d * 12:152 + d * 12] = col128(inp["lru_lambda"][j, d])
    W["vecs"] = vecs
    sw = np.concatenate([np.arange(32, 64), np.arange(0, 32)])
    win = []
    wuq = []
    for j in range(2):
        w = inp["mla_w_in"][j]
        ext = np.concatenate([w, w[:, 640:704][:, sw]], axis=1)
        win.append(tile_w(ext, 8, 128))
        q = inp["mla_w_uq"][j].reshape(384, 16, 192)
        qe = np.concatenate([q, q[:, :, 128:192][:, :, sw]], axis=2)
        wuq.append(tile_w(qe.reshape(384, 4096), 3, 256))
    W["mla_win"] = np.stack(win)
    W["mla_wuq"] = np.stack(wuq)
    W["mla_wukv"] = np.stack([tile_w(inp["mla_w_ukv"][j], 2, 256) for j in range(2)])
    W["mla_wo"] = np.stack([tile_w(inp["mla_w_o"][j], 16, 128) for j in range(2)])
    dw = inp["diff_w_in"][0]
    W["diff_wqk"] = tile_w(dw[:, 0:2048], 8, 128)
    W["diff_wv"] = tile_w(dw[:, 2048:3072], 8, 512)
    W["diff_lam"] = np.ascontiguousarray(inp["diff_lambda"][0].reshape(1, 256))
    W["diff_wo"] = tile_w(inp["diff_w_o"][0], 8, 128)
    W["lru_win"] = tile_w(inp["lru_w_in"][0], 8, 128)
    wg = inp["lru_w_gates"][0]
    wg = wg.reshape(2, 2, 6, 2, 128, 2, 128)
    W["lru_wg"] = np.ascontiguousarray(wg.transpose(0, 1, 2, 5, 4, 3, 6)).reshape(48, 128, 2, 128)
    W["lru_wo"] = tile_w(inp["lru_w_o"][0], 12, 128)
    w1 = []
    w2 = []
    for l in range(4):
        wi = inp["ffn_w_in"][l]
        g = wi[:, :2816].reshape(1024, 22, 128)
        u = wi[:, 2816:].reshape(1024, 22, 128)
        w1.append(tile_w(np.concatenate([g, u], axis=2).reshape(1024, 22 * 256), 8, 256))
        w2.append(tile_w(inp["ffn_w_out"][l], 22, 128))
    W["ffn_w1"] = np.stack(w1)
    W["ffn_w2"] = np.stack(w2)
    return {k: np.ascontiguousarray(v, dtype=f) for k, v in W.items()}


def core_consts(r):
    f = np.float32
    pos = np.concatenate([NMETA + NREAL * r + np.arange(NREAL), np.arange(NMETA)]).astype(f)
    inv = (10000.0 ** (-np.arange(0, 64, 2, dtype=f) / 64)).astype(f)
    ang = pos[None, :] * inv[:, None]
    c, s = np.cos(ang).astype(f), np.sin(ang).astype(f)
    cs = np.concatenate([c, c, -s, s], axis=0)
    p = np.arange(128, dtype=np.float64)[:, None]
    ttr = np.abs(np.arange(6032, dtype=np.float64)[None, :] - 3968 - p + NREAL * r).astype(f)
    ttm = np.abs(np.arange(4000, dtype=np.float64)[None, :] - 3984 - p).astype(f)
    cst = np.zeros((128, 4), f)
    cst[:, 0] = EPS
    cst[:, 1] = 1.0
    cst[:, 2 + r] = 1.0
    return {"cs": cs, "ttr": ttr, "ttm": ttm, "cst": cst}


LAUNCHES = [
    ([("A", 0)], ["hT0"], ["ks0", "cqn0"]),
    ([("B", 0), ("C", 0), ("A", 1)], ["ksf0", "cqn0", "hT0"], ["hT1", "ksk1", "ksv1", "qT1"]),
    ([("B", 1), ("C", 1), ("A", 2)], ["ksfk1", "ksfv1", "qT1", "hT1"], ["hT2", "ksx2", "gg2"]),
    ([("B", 2), ("C", 2), ("A", 3)], ["ksfx2", "gg2", "hT2"], ["hT3", "ks3", "cqn3"]),
    ([("B", 3), ("C", 3)], ["ksf3", "cqn3", "hT3"], ["hT4"]),
]
FUSED_LAUNCH = ([(ph, l) for l in range(4) for ph in ("A", "X", "B", "C")], ["hT0"], ["hT4"])
FUSED = True
EXCH = {"ks0": "ksf0", "ksk1": "ksfk1", "ksv1": "ksfv1", "ksx2": "ksfx2", "ks3": "ksf3"}

_PROG_CACHE = {}


def run_launch(li, W, consts, state, ncores):
    stages, ext_in, ext_out = FUSED_LAUNCH if li == "fused" else LAUNCHES[li]
    if li not in _PROG_CACHE:
        _PROG_CACHE[li] = build_program(stages, ext_in, ext_out)
    nc, used_w = _PROG_CACHE[li]
    in_maps = []
    for c in range(ncores):
        m = {}
        for name in used_w:
            m[name] = consts[c % 2][name] if name in consts[0] else W[name]
        for name in ext_in:
            m[name] = state[name][c]
        in_maps.append(m)
    res = run_bass_kernel_spmd(nc, in_maps, core_ids=list(range(ncores)))
    for name in ext_out:
        state[name] = [res.results[c][name] for c in range(ncores)]
    for name in ext_out:
        if name in EXCH:
            full = []
            for c in range(ncores):
                p = c // 2
                full.append(np.stack([state[name][2 * p], state[name][2 * p + 1]]))
            state[EXCH[name]] = full


def kernel(**inp):
    inp = {k: np.asarray(v) for k, v in inp.items()}
    x = inp["x"]
    Bn = x.shape[0]
    ncores = 2 * Bn
    W = prep_weights(inp)
    consts = [core_consts(0), core_consts(1)]
    meta = inp["meta_tokens"].astype(np.float32)
    state = {"hT0": []}
    for c in range(ncores):
        b, r = c // 2, c % 2
        own = np.concatenate([x[b, r * NREAL:(r + 1) * NREAL], meta], axis=0)
        state["hT0"].append(np.ascontiguousarray(own.T))
    if FUSED:
        run_launch("fused", W, consts, state, ncores)
    else:
        for li in range(len(LAUNCHES)):
            run_launch(li, W, consts, state, ncores)
    out = np.zeros((Bn, 2 * NREAL, D), np.float32)
    for c in range(ncores):
        b, r = c // 2, c % 2
        out[b, r * NREAL:(r + 1) * NREAL] = state["hT4"][c][:, 0:NREAL].T
    return out
```

```python
import math
from contextlib import ExitStack

import numpy as np
import ml_dtypes
import concourse.bass as bass
import concourse.mybir as mybir
from concourse.bass_utils import run_bass_kernel_spmd

F32 = mybir.dt.float32
BF16 = mybir.dt.bfloat16
ALU = mybir.AluOpType
AF = mybir.ActivationFunctionType
NPBF = ml_dtypes.bfloat16

D = 1024
NREAL = 2048
NMETA = 16
NOWN = NREAL + NMETA
NFULL = 2 * NREAL + NMETA
SBW = 688
SUBS = [(0, 512), (512, 176)]
SUPER = [(0, SUBS), (688, SUBS), (1376, SUBS)]
KT = [(t * 128, 128) for t in range(32)] + [(4096, 16)]
QB = [(0, 512), (512, 512), (1024, 512), (1536, 512), (2048, 16)]
KB = [(i * 512, 512) for i in range(8)] + [(4096, 16)]
NV = 164
AW = 48000
KINDS = ["mla", "diff", "lru", "mla"]
JIDX = [0, 0, 0, 1]
EPS = 1e-6


class Op:
    __slots__ = ("eng", "fn", "deps", "signal", "sigval", "dma", "sem", "cc", "idx")

    def __init__(self, eng, fn, dma, cc=False):
        self.cc = cc
        self.eng = eng
        self.fn = fn
        self.deps = []
        self.signal = dma
        self.sigval = 0
        self.dma = dma
        self.sem = None


class Prog:
    ENGS = ("pe", "act", "dve", "pool", "sp")
    NDMASEM = 8

    def __init__(self, nc):
        self.nc = nc
        self.ops = []
        self.last_w = {}
        self.readers = {}
        self.kids = {}
        self.fence = []
        self.fence_set = set()

    def _keys(self, b):
        if "#" in b:
            par = b.split("#")[0]
            self.kids.setdefault(par, set()).add(b)
            return (b, par)
        return (b,) + tuple(self.kids.get(b, ()))

    def add(self, eng, fn, reads=(), writes=(), dma=False, cc=False):
        op = Op(eng, fn, dma, cc)
        deps = set()
        for b in reads:
            for k in self._keys(b):
                w = self.last_w.get(k)
                if w is not None:
                    deps.add(w)
        for b in writes:
            for k in self._keys(b):
                w = self.last_w.get(k)
                if w is not None:
                    deps.add(w)
                deps.update(self.readers.get(k, ()))
        for b in reads:
            self.readers.setdefault(b, []).append(op)
        for b in writes:
            self.last_w[b] = op
            self.readers[b] = []
            if "#" not in b:
                for k in self.kids.get(b, ()):
                    self.last_w[k] = op
                    self.readers[k] = []
        deps.update(self.fence)
        deps.discard(op)
        latest = {}
        final = []
        for d in deps:
            if d.dma:
                final.append(d)
            else:
                cur = latest.get(d.eng)
                if cur is None or cur.idx < d.idx:
                    latest[d.eng] = d
        final.extend(latest.values())
        for d in final:
            if d.eng == "pe" and eng == "pe" and not d.dma and not dma and d not in self.fence_set:
                continue
            d.signal = True
            op.deps.append(d)
        op.idx = len(self.ops)
        self.ops.append(op)
        return op

    def barrier(self):
        tails = {}
        for op in self.ops:
            tails[(op.eng, op.dma)] = op
        dm = {}
        for op in self.ops:
            if op.dma:
                dm.setdefault(op.eng, []).append(op)
        fence = list(tails.values())
        for e, lst in dm.items():
            fence.extend(lst[-self.NDMASEM:])
        self.fence = fence
        self.fence_set = set(fence)
        for o in fence:
            o.signal = True

    def emit(self):
        nc = self.nc
        cnt = {e: 0 for e in self.ENGS}
        dma_rr = {e: 0 for e in self.ENGS}
        dma_last = {}
        dma_cnt = {}
        for op in self.ops:
            if op.cc:
                k = ("cc", 0)
                dma_cnt[k] = dma_cnt.get(k, 0) + 1
                op.sem = k
                op.sigval = dma_cnt[k]
            elif op.dma:
                k = (op.eng, dma_rr[op.eng] % self.NDMASEM)
                dma_rr[op.eng] += 1
                prev = dma_last.get(k)
                if prev is not None:
                    op.deps.append(prev)
                dma_last[k] = op
                dma_cnt[k] = dma_cnt.get(k, 0) + 16
                op.sem = k
                op.sigval = dma_cnt[k]
            elif op.signal:
                cnt[op.eng] += 1
                op.sem = op.eng
                op.sigval = cnt[op.eng]
        with ExitStack() as st:
            sems = {}
            for e in self.ENGS:
                sems[e] = st.enter_context(nc.semaphore("s_" + e))
            for k in dma_cnt:
                sems[k] = st.enter_context(nc.semaphore("d_%s%d" % k))
            block = st.enter_context(nc.Block())
            engobj = {"pe": block.tensor, "act": block.scalar, "dve": block.vector,
                      "pool": block.gpsimd, "sp": block.sync}
            for e in self.ENGS:
                myops = [op for op in self.ops if op.eng == e]
                if not myops:
                    continue

                def body(eng, myops=myops):
                    waited = {}
                    for op in myops:
                        need = {}
                        for d in op.deps:
                            if need.get(d.sem, 0) < d.sigval:
                                need[d.sem] = d.sigval
                        for k, v in need.items():
                            if waited.get(k, 0) < v:
                                eng.wait_ge(sems[k], v)
                                waited[k] = v
                        ins = op.fn(eng)
                        if op.signal:
                            ins.then_inc(sems[op.sem], 16 if (op.dma and not op.cc) else 1)

                engobj[e](body)


class T:
    def __init__(self, ap, name):
        self.ap = ap
        self.name = name


class Rot:
    def __init__(self, items):
        self.items = items
        self.i = 0

    def get(self):
        t = self.items[self.i % len(self.items)]
        self.i += 1
        return t


class Bld:
    def __init__(self, nc, st):
        self.nc = nc
        self.P = Prog(nc)
        self.arena = st.enter_context(nc.sbuf_tensor("arena", [128, AW], F32))
        self.psum = st.enter_context(nc.psum_tensor("psum", [128, 8, 512], F32))
        self.off = 0
        self.gen = 0
        self.dr = {}
        self.ev = 0
        self.psr4 = Rot([self.ps(i) for i in range(4, 8)])
        self.psr8 = Rot([self.ps(i) for i in range(8)])
        self.psr = self.psr4

    def alloc(self, name, shape, dt):
        n = 1
        for s in shape:
            n *= s
        size = 4 if dt == F32 else 2
        words = (n * size + 3) // 4
        words = (words + 7) // 8 * 8
        assert self.off + words <= AW, (name, self.off, words)
        ap = self.arena[:, self.off:self.off + words]
        self.off += words
        if dt != F32:
            ap = ap.bitcast(dt)
        ap = ap[:, 0:n]
        if len(shape) == 2:
            ap = ap.rearrange("p (a b) -> p a b", b=shape[1])
        elif len(shape) == 3:
            ap = ap.rearrange("p (a b c) -> p a b c", b=shape[1], c=shape[2])
        return T(ap, "%s@%d" % (name, self.gen))

    def phase(self, keep):
        self.P.barrier()
        self.off = keep
        self.gen += 1

    def ps(self, i):
        return T(self.psum[:, i, :], "ps%d" % i)

    def dram(self, name, shape, dt, kind):
        t = self.nc.dram_tensor(name, list(shape), dt, kind=kind).ap()
        self.dr[name] = t
        return t

    def mm(self, out, lhsT, rhs, start, stop, r, w):
        self.P.add("pe", lambda e: e.matmul(out, lhsT, rhs, start=start, stop=stop), r, w)

    def act(self, out, in_, func, r, w, bias=None, scale=None):
        kw = {}
        if bias is not None:
            kw["bias"] = bias
        if scale is not None:
            kw["scale"] = scale
        self.P.add("act", lambda e: e.activation(out, in_, func, **kw), r, w)

    def tt(self, out, in0, in1, op, r, w, eng="dve"):
        self.P.add(eng, lambda e: e.tensor_tensor(out, in0, in1, op), r, w)

    def ts(self, out, in0, s1, s2, op0, op1, r, w, eng="dve"):
        if s2 is None:
            self.P.add(eng, lambda e: e.tensor_scalar(out, in0, s1, None, op0), r, w)
        else:
            self.P.add(eng, lambda e: e.tensor_scalar(out, in0, s1, s2, op0, op1), r, w)

    def stt(self, out, in0, scalar, in1, op0, op1, r, w, eng="dve"):
        self.P.add(eng, lambda e: e.scalar_tensor_tensor(out, in0, scalar, in1, op0, op1), r, w)

    def copy(self, out, in_, r, w, eng=None):
        if eng is None:
            eng = "act" if self.ev % 2 == 0 else "dve"
            self.ev += 1
        if eng == "act":
            self.P.add("act", lambda e: e.copy(out, in_), r, w)
        else:
            self.P.add(eng, lambda e: e.tensor_copy(out, in_), r, w)

    def dma(self, out, in_, r, w, q="sp"):
        self.P.add(q, lambda e: e.dma_start(out=out, in_=in_), r, w, dma=True)

    def memset(self, ap, val, w, eng="dve"):
        self.P.add(eng, lambda e: e.memset(ap, val), (), w)

    def setup_consts(self, cst_dram):
        self.ones = self.alloc("ones", [128], BF16)
        self.cst = self.alloc("cst", [4], F32)
        self.memset(self.ones.ap, 1.0, [self.ones.name])
        self.onesf = self.alloc("onesf", [128], F32)
        self.memset(self.onesf.ap, 1.0, [self.onesf.name])
        self.dma(self.cst.ap, cst_dram, [], [self.cst.name])
        self.eps = self.cst.ap[:, 0:1]
        self.one = self.cst.ap[:, 1:2]
        self.base = self.off

    def load_vecs(self, l):
        v = self.alloc("vec", [NV], F32)
        self.dma(v.ap, self.dr["vecs"][l], [], [v.name])
        return v

    def wload(self, rot, src, shape):
        t = rot.get()
        ap = t.ap
        if len(shape) == 2:
            dst = ap[:, 0:shape[0], 0:shape[1]]
        else:
            dst = ap
        self.dma(dst, src, [], [t.name], q="pool")
        return t

    def rms_stats(self, x, C, subs, sq, rstd, dim):
        for si, (c0, n) in enumerate(subs):
            self.act(sq.ap[:, 0:C, c0:c0 + n], x.ap[:, 0:C, c0:c0 + n], AF.Square,
                     [x.name + "#%d" % si], [sq.name + "#%d" % si])
            ps = self.psr.get()
            for c in range(C):
                self.mm(ps.ap[:, 0:n], self.ones.ap, sq.ap[:, c, c0:c0 + n], c == 0, c == C - 1,
                        [sq.name + "#%d" % si, self.ones.name], [ps.name])
            self.act(rstd.ap[:, c0:c0 + n], ps.ap[:, 0:n], AF.Sqrt, [ps.name, self.cst.name],
                     [rstd.name + "#%d" % si], bias=self.eps, scale=1.0 / dim)
            rs = rstd.ap[:, c0:c0 + n]
            self.P.add("dve", lambda e, rs=rs: e.reciprocal(rs, rs), [rstd.name + "#%d" % si],
                       [rstd.name + "#%d" % si])

    def load_h(self, hT, s0, subs, h):
        src = hT.rearrange("(c p) n -> p c n", p=128)
        for si, (c0, n) in enumerate(subs):
            self.dma(h.ap[:, :, c0:c0 + n], src[:, :, s0 + c0:s0 + c0 + n], [], [h.name + "#%d" % si])
        return h

    def norm_to_bf16(self, h, subs, vec, gcol, C, dim, bufs):
        sq, rstd, u = bufs
        self.rms_stats(h, C, subs, sq, rstd, dim)
        for si, (c0, n) in enumerate(subs):
            for c in range(C):
                self.stt(u.ap[:, c, c0:c0 + n], h.ap[:, c, c0:c0 + n], vec.ap[:, gcol + c:gcol + c + 1],
                         rstd.ap[:, c0:c0 + n], ALU.mult, ALU.mult,
                         [h.name + "#%d" % si, rstd.name + "#%d" % si, vec.name], [u.name + "#%d" % si])
        return u

    def exchange(self, l):
        k = KINDS[l]
        self.phase(self.base)
        self.psr = self.psr4
        pairs = []
        if k == "mla":
            pairs.append((self.dr["ks%d" % l], self.dr["ksf%d" % l].rearrange("r p n -> (r p) n")))
        elif k == "diff":
            for i in range(4):
                pairs.append((self.dr["ksk%d" % l][256 * i:256 * (i + 1), :], self.dr["ksfk%d" % l][i]))
            for i in range(4):
                rows = 512 if i < 3 else 528
                pairs.append((self.dr["ksv%d" % l][512 * i:512 * i + rows, :], self.dr["ksfv%d" % l][i, 0:2 * rows, :]))
        else:
            for i in range(12):
                pairs.append((self.dr["ksx%d" % l][128 * i:128 * (i + 1), :], self.dr["ksfx%d" % l][i]))
        for (src, dst) in pairs:
            self.P.add("pool", lambda e, src=src, dst=dst: e.collective_compute(
                "AllGather", ALU.bypass, replica_groups=[[0, 1], [2, 3], [4, 5], [6, 7]], ins=[src], outs=[dst]),
                [], [], dma=True, cc=True)

    def phaseA_mla(self, l):
        j = JIDX[l]
        hT = self.dr["hT%d" % l]
        ks = self.dr["ks%d" % l]
        cqd = self.dr["cqn%d" % l]
        win = self.dr["mla_win"]
        self.phase(self.base)
        self.psr = self.psr8
        vec = self.load_vecs(l)
        cs = self.alloc("cs", [NOWN], F32)
        self.dma(cs.ap, self.dr["cs"], [], [cs.name])
        hR = Rot([self.alloc("h%d" % i, [8, SBW], F32) for i in range(2)])
        nb = (self.alloc("sq", [8, SBW], BF16), self.alloc("rstd", [SBW], F32), self.alloc("u", [8, SBW], BF16))
        cq = self.alloc("cq", [3, SBW], F32)
        ckv = self.alloc("ckv", [2, SBW], F32)
        nbq = (nb[0], nb[1], self.alloc("cqn", [3, SBW], BF16))
        nbk = (nb[0], nb[1], self.alloc("ckvn", [2, SBW], BF16))
        krb = self.alloc("krb", [SBW], BF16)
        t1 = self.alloc("t1", [512], F32)
        t2 = self.alloc("t2", [512], F32)
        wrot = Rot([self.alloc("w%d" % i, [8, 128], BF16) for i in range(3)])
        hn = self.load_h(hT, SUPER[0][0], SUPER[0][1], hR.get())
        for k, (s0, subs) in enumerate(SUPER):
            W = sum(n for _, n in subs)
            h = hn
            if k + 1 < len(SUPER):
                hn = self.load_h(hT, SUPER[k + 1][0], SUPER[k + 1][1], hR.get())
            u = self.norm_to_bf16(h, subs, vec, 0, 8, 1024, nb)
            for m in range(6):
                wt = self.wload(wrot, win[j, m], [8, 128])
                for si, (c0, n) in enumerate(subs):
                    ps = self.psr.get()
                    for kc in range(8):
                        self.mm(ps.ap[:, 0:n], wt.ap[:, kc, :], u.ap[:, kc, c0:c0 + n], kc == 0, kc == 7,
                                [wt.name, u.name + "#%d" % si], [ps.name])
                    if m < 3:
                        self.copy(cq.ap[:, m, c0:c0 + n], ps.ap[:, 0:n], [ps.name], [cq.name + "#%d" % si])
                    elif m < 5:
                        self.copy(ckv.ap[:, m - 3, c0:c0 + n], ps.ap[:, 0:n], [ps.name], [ckv.name + "#%d" % si])
                    else:
                        g0 = s0 + c0
                        self.tt(t1.ap[0:64, 0:n], ps.ap[0:64, 0:n], cs.ap[0:64, g0:g0 + n], ALU.mult,
                                [ps.name, cs.name], [t1.name])
                        self.tt(t2.ap[0:64, 0:n], ps.ap[64:128, 0:n], cs.ap[64:128, g0:g0 + n], ALU.mult,
                                [ps.name, cs.name], [t2.name])
                        self.tt(krb.ap[0:64, c0:c0 + n], t1.ap[0:64, 0:n], t2.ap[0:64, 0:n], ALU.add,
                                [t1.name, t2.name], [krb.name + "#%d" % si], eng="pool")
            cqn = self.norm_to_bf16(cq, subs, vec, 32, 3, 384, nbq)
            ckvn = self.norm_to_bf16(ckv, subs, vec, 35, 2, 256, nbk)
            self.dma(cqd.rearrange("(c p) n -> p c n", p=128)[:, :, s0:s0 + W], cqn.ap[:, :, 0:W],
                     [cqn.name], ["cqn%d#%d" % (l, s0)])
            self.dma(ks[0:256, :].rearrange("(c p) n -> p c n", p=128)[:, :, s0:s0 + W], ckvn.ap[:, :, 0:W],
                     [ckvn.name], ["ks%d#a%d" % (l, s0)])
            self.dma(ks[256:320, s0:s0 + W], krb.ap[0:64, 0:W], [krb.name], ["ks%d#b%d" % (l, s0)])

    def phaseB_mla(self, l):
        j = JIDX[l]
        ksf = self.dr["ksf%d" % l]
        cqd = self.dr["cqn%d" % l]
        oall = self.dr["oall%d" % l]
        wuq = self.dr["mla_wuq"]
        wukv = self.dr["mla_wukv"]
        SCALE = 192.0 ** -0.5
        self.phase(self.base)
        self.psr = self.psr4
        cs = self.alloc("cs", [NOWN], F32)
        self.dma(cs.ap, self.dr["cs"], [], [cs.name])
        ckv = self.alloc("ckvf", [2, NFULL], BF16)
        kr = self.alloc("krf", [NFULL], BF16)
        cq = self.alloc("cqf", [3, NOWN], BF16)
        for r_ in range(2):
            self.dma(ckv.ap[:, :, r_ * 2048:(r_ + 1) * 2048],
                     ksf[r_, 0:256, 0:2048].rearrange("(c p) n -> p c n", p=128), ["ksf%d" % l], [ckv.name])
            self.dma(kr.ap[0:64, r_ * 2048:(r_ + 1) * 2048], ksf[r_, 256:320, 0:2048], ["ksf%d" % l], [kr.name])
        self.dma(ckv.ap[:, :, 4096:4112], ksf[0, 0:256, 2048:2064].rearrange("(c p) n -> p c n", p=128),
                 ["ksf%d" % l], [ckv.name])
        self.dma(kr.ap[0:64, 4096:4112], ksf[0, 256:320, 2048:2064], ["ksf%d" % l], [kr.name])
        self.dma(cq.ap, cqd.rearrange("(c p) n -> p c n", p=128), ["cqn%d" % l], [cq.name])
        KhR = Rot([self.alloc("Kh%d" % i, [NFULL], BF16) for i in range(2)])
        VhR = Rot([self.alloc("Vh%d" % i, [33, 128], BF16) for i in range(2)])
        QnR = Rot([self.alloc("Qn%d" % i, [NOWN], BF16) for i in range(2)])
        QrR = Rot([self.alloc("Qr%d" % i, [NOWN], BF16) for i in range(2)])
        for qt_ in QrR.items:
            self.memset(qt_.ap[64:128, :], 0.0, [qt_.name])
        self.memset(kr.ap[64:128, :], 0.0, [kr.name])
        OhR = Rot([self.alloc("Oh%d" % i, [NOWN], BF16) for i in range(2)])
        PtR = Rot([self.alloc("Pt%d" % i, [512], BF16) for i in range(6)])
        wqR = Rot([self.alloc("wq%d" % i, [3, 256], BF16) for i in range(2)])
        wkR = Rot([self.alloc("wk%d" % i, [2, 256], BF16) for i in range(2)])
        t1 = self.alloc("t1", [512], F32)
        t2 = self.alloc("t2", [512], F32)
        rsR = Rot([self.alloc("rs%d" % i, [512], F32) for i in range(2)])
        accR = Rot([(self.ps(0), self.ps(1)), (self.ps(2), self.ps(3))])
        saR = Rot([(self.alloc("saP%d" % i, [512], F32), self.alloc("saD%d" % i, [512], F32)) for i in range(2)])
        def gen_list(hh):
            wq = self.wload(wqR, wuq[j, hh], [3, 256])
            wk = self.wload(wkR, wukv[j, hh], [2, 256])
            Kh, Vh, Qn, Qr, Oh = KhR.get(), VhR.get(), QnR.get(), QrR.get(), OhR.get()
            ops = []

            def gK(c0, n):
                ps = self.psr.get()
                for kc in range(2):
                    self.mm(ps.ap[:, 0:n], wk.ap[:, kc, 0:128], ckv.ap[:, kc, c0:c0 + n], kc == 0, kc == 1,
                            [wk.name, ckv.name], [ps.name])
                self.copy(Kh.ap[:, c0:c0 + n], ps.ap[:, 0:n], [ps.name], [Kh.name + "#%d" % (c0 // 512)])

            def gV(g):
                ps = self.psr.get()
                tl = list(range(g, min(g + 4, 33)))
                for t in tl:
                    k0, kn = KT[t]
                    for kc in range(2):
                        self.mm(ps.ap[0:kn, (t - g) * 128:(t - g + 1) * 128], ckv.ap[:, kc, k0:k0 + kn],
                                wk.ap[:, kc, 128:256], kc == 0, kc == 1, [wk.name, ckv.name], [ps.name])
                if len(tl) == 4:
                    self.copy(Vh.ap[:, g:g + 4, :], ps.ap[:, 0:512].rearrange("p (a b) -> p a b", b=128),
                              [ps.name], [Vh.name + "#%d" % (g // 4)])
                else:
                    self.copy(Vh.ap[0:16, 32, :], ps.ap[0:16, 0:128], [ps.name], [Vh.name + "#8"])

            def gQ(qi, c0, n):
                ps = self.psr.get()
                for kc in range(3):
                    self.mm(ps.ap[:, 0:n], wq.ap[:, kc, 0:128], cq.ap[:, kc, c0:c0 + n], kc == 0, kc == 2,
                            [wq.name, cq.name], [ps.name])
                self.copy(Qn.ap[:, c0:c0 + n], ps.ap[:, 0:n], [ps.name], [Qn.name + "#%d" % qi])
                ps = self.psr.get()
                for kc in range(3):
                    self.mm(ps.ap[:, 0:n], wq.ap[:, kc, 128:256], cq.ap[:, kc, c0:c0 + n], kc == 0, kc == 2,
                            [wq.name, cq.name], [ps.name])
                self.tt(t1.ap[0:64, 0:n], ps.ap[0:64, 0:n], cs.ap[0:64, c0:c0 + n], ALU.mult,
                        [ps.name, cs.name], [t1.name])
                self.tt(t2.ap[0:64, 0:n], ps.ap[64:128, 0:n], cs.ap[64:128, c0:c0 + n], ALU.mult,
                        [ps.name, cs.name], [t2.name])
                self.tt(Qr.ap[0:64, c0:c0 + n], t1.ap[0:64, 0:n], t2.ap[0:64, 0:n], ALU.add,
                        [t1.name, t2.name], [Qr.name + "#%d" % qi], eng="pool")

            for (c0, n) in KB:
                ops.append(lambda c0=c0, n=n: gK(c0, n))
            for g in range(0, 33, 4):
                ops.append(lambda g=g: gV(g))
            for qi, (c0, n) in enumerate(QB):
                ops.append(lambda qi=qi, c0=c0, n=n: gQ(qi, c0, n))
            return (Kh, Vh, Qn, Qr, Oh), ops

        nxt_bufs, nxt_ops = gen_list(0)
        for f_ in nxt_ops:
            f_()
        for hh in range(16):
            Kh, Vh, Qn, Qr, Oh = nxt_bufs
            if hh + 1 < 16:
                nxt_bufs, nxt_ops = gen_list(hh + 1)
            else:
                nxt_ops = []
            per = (len(nxt_ops) + 3) // 4
            for qi, (q0, qn) in enumerate(QB):
                psO, psS = accR.get()
                saP, saD = saR.get()
                LAG = 2
                pend = []
                for i in range(33 + LAG):
                    if i < 33:
                        k0, kn = KT[i]
                        pT = self.psr.get()
                        self.mm(pT.ap[0:kn, 0:qn], Kh.ap[:, k0:k0 + kn], Qn.ap[:, q0:q0 + qn], True, False,
                                [Kh.name + "#%d" % (k0 // 512), Qn.name + "#%d" % qi], [pT.name])
                        self.mm(pT.ap[0:kn, 0:qn], kr.ap[:, k0:k0 + kn], Qr.ap[:, q0:q0 + qn], False, True,
                                [kr.name, Qr.name + "#%d" % qi], [pT.name])
                        pt = PtR.get()
                        self.act(pt.ap[0:kn, 0:qn], pT.ap[0:kn, 0:qn], AF.Exp, [pT.name], [pt.name], scale=SCALE)
                        pend.append((i, pt))
                    if i >= LAG:
                        t, pt = pend.pop(0)
                        k0, kn = KT[t]
                        self.mm(psO.ap[:, 0:qn], Vh.ap[0:kn, t, :], pt.ap[0:kn, 0:qn], t == 0, t == 32,
                                [Vh.name + "#%d" % (t // 4), pt.name], [psO.name])
                        if t % 3 == 0:
                            self.mm(psS.ap[:, 0:qn], self.ones.ap[0:kn, :], pt.ap[0:kn, 0:qn], t == 0, False,
                                    [self.ones.name, pt.name], [psS.name])
                        else:
                            sa, se = (saP, "pool") if t % 3 == 2 else (saD, "dve")
                            if t < 3:
                                self.copy(sa.ap[0:kn, 0:qn], pt.ap[0:kn, 0:qn], [pt.name], [sa.name], eng=se)
                            else:
                                self.tt(sa.ap[0:kn, 0:qn], sa.ap[0:kn, 0:qn], pt.ap[0:kn, 0:qn], ALU.add,
                                        [sa.name, pt.name], [sa.name], eng=se)
                self.mm(psS.ap[:, 0:qn], self.onesf.ap, saP.ap[:, 0:qn], False, False, [self.onesf.name, saP.name], [psS.name])
                self.mm(psS.ap[:, 0:qn], self.onesf.ap, saD.ap[:, 0:qn], False, True, [self.onesf.name, saD.name], [psS.name])
                rs = rsR.get()
                rsa = rs.ap[:, 0:qn]
                pss = psS.ap[:, 0:qn]
                self.P.add("dve", lambda e, rsa=rsa, pss=pss: e.reciprocal(rsa, pss), [psS.name], [rs.name])
                self.tt(Oh.ap[:, q0:q0 + qn], psO.ap[:, 0:qn], rsa, ALU.mult, [psO.name, rs.name],
                        [Oh.name + "#%d" % qi])
                if qi < 4:
                    for f_ in nxt_ops[qi * per:(qi + 1) * per]:
                        f_()
            self.dma(oall[hh * 128:(hh + 1) * 128, :], Oh.ap, [Oh.name], ["oall%d#%d" % (l, hh)])

    def phaseA_diff(self, l):
        hT = self.dr["hT%d" % l]
        ksk = self.dr["ksk%d" % l]
        ksv = self.dr["ksv%d" % l]
        qTd = self.dr["qT%d" % l]
        wqk = self.dr["diff_wqk"]
        wv = self.dr["diff_wv"]
        self.phase(self.base)
        self.psr = self.psr8
        vec = self.load_vecs(l)
        hR = Rot([self.alloc("h%d" % i, [8, SBW], F32) for i in range(2)])
        nb = (self.alloc("sq", [8, SBW], BF16), self.alloc("rstd", [SBW], F32), self.alloc("u", [8, SBW], BF16))
        qkR = Rot([self.alloc("qk%d" % i, [8, SBW], BF16) for i in range(2)])
        wrot = Rot([self.alloc("w%d" % i, [8, 128], BF16) for i in range(3)])
        wvR = Rot([self.alloc("wv%d" % i, [8, 512], BF16) for i in range(2)])
        vtR = Rot([self.alloc("vt%d" % i, [1024], BF16) for i in range(3)])
        wvt = [self.wload(wvR, wv[hf], [8, 512]) for hf in range(2)]
        hn = self.load_h(hT, SUPER[0][0], SUPER[0][1], hR.get())
        for k, (s0, subs) in enumerate(SUPER):
            W = sum(n for _, n in subs)
            h = hn
            if k + 1 < len(SUPER):
                hn = self.load_h(hT, SUPER[k + 1][0], SUPER[k + 1][1], hR.get())
            u = self.norm_to_bf16(h, subs, vec, 0, 8, 1024, nb)
            for part, dst in ((0, qTd), (1, ksk)):
                qk = qkR.get()
                for m in range(8):
                    wt = self.wload(wrot, wqk[part * 8 + m], [8, 128])
                    for si, (c0, n) in enumerate(subs):
                        ps = self.psr.get()
                        for kc in range(8):
                            self.mm(ps.ap[:, 0:n], wt.ap[:, kc, :], u.ap[:, kc, c0:c0 + n], kc == 0, kc == 7,
                                    [wt.name, u.name + "#%d" % si], [ps.name])
                        self.copy(qk.ap[:, m, c0:c0 + n], ps.ap[:, 0:n], [ps.name], [qk.name + "#%d" % si])
                self.dma(dst.rearrange("(c p) n -> p c n", p=128)[:, :, s0:s0 + W], qk.ap[:, :, 0:W],
                         [qk.name], ["%s%d#%d" % ("qT" if part == 0 else "ksk", l, s0)])
            toks = []
            for si, (c0, n) in enumerate(subs):
                for a_ in range(0, n, 128):
                    toks.append((si, c0 + a_, min(128, n - a_)))
            for (si, a0, an) in toks:
                vt = vtR.get()
                for hf in range(2):
                    ps = self.psr.get()
                    for kc in range(8):
                        self.mm(ps.ap[0:an, :], u.ap[:, kc, a0:a0 + an], wvt[hf].ap[:, kc, :], kc == 0, kc == 7,
                                [wvt[hf].name, u.name + "#%d" % si], [ps.name])
                    self.copy(vt.ap[0:an, hf * 512:(hf + 1) * 512], ps.ap[0:an, :], [ps.name], [vt.name])
                self.dma(ksv[s0 + a0:s0 + a0 + an, :], vt.ap[0:an, :], [vt.name], ["ksv%d#%d" % (l, s0 + a0)])

    def phaseB_diff(self, l):
        kf = self.dr["ksfk%d" % l]
        vf = self.dr["ksfv%d" % l]
        qTd = self.dr["qT%d" % l]
        oall = self.dr["oall%d" % l]
        lambda_init = 0.8 - 0.6 * math.exp(-0.3 * l)
        self.phase(self.base)
        self.psr = self.psr4
        vec = self.load_vecs(l)
        ttr = self.alloc("ttr", [6032], F32)
        ttm = self.alloc("ttm", [4000], F32)
        lp = self.alloc("lp", [256], F32)
        lt = self.alloc("lt", [8], F32)
        self.dma(ttr.ap, self.dr["ttr"], [], [ttr.name])
        self.dma(ttm.ap, self.dr["ttm"], [], [ttm.name])
        self.dma(lp.ap, self.dr["diff_lam"].partition_broadcast(128), [], [lp.name])
        self.tt(lp.ap[:, 0:64], lp.ap[:, 0:64], lp.ap[:, 64:128], ALU.mult, [lp.name], [lp.name])
        self.tt(lp.ap[:, 128:192], lp.ap[:, 128:192], lp.ap[:, 192:256], ALU.mult, [lp.name], [lp.name])
        a0 = lt.ap[:, 0:1]
        a1 = lt.ap[:, 1:2]
        l0 = lp.ap[:, 0:64]
        l1 = lp.ap[:, 128:192]
        self.P.add("dve", lambda e: e.reduce_sum(a0, l0, mybir.AxisListType.X), [lp.name], [lt.name])
        self.P.add("dve", lambda e: e.reduce_sum(a1, l1, mybir.AxisListType.X), [lp.name], [lt.name])
        self.act(lt.ap[:, 2:4], lt.ap[:, 0:2], AF.Exp, [lt.name], [lt.name])
        self.tt(lt.ap[:, 4:5], lt.ap[:, 3:4], lt.ap[:, 2:3], ALU.subtract, [lt.name], [lt.name])
        self.ts(lt.ap[:, 5:6], lt.ap[:, 4:5], -lambda_init, None, ALU.add, None, [lt.name], [lt.name])
        neglam = lt.ap[:, 5:6]
        KhR = Rot([self.alloc("Kh%d" % i, [NFULL], BF16) for i in range(2)])
        VhR = Rot([self.alloc("Vh%d" % i, [33, 128], BF16) for i in range(2)])
        QhR = Rot([(self.alloc("Qz0_%d" % i, [NOWN], BF16), self.alloc("Qz1_%d" % i, [NOWN], BF16)) for i in range(2)])
        for (qa_, qb_) in QhR.items:
            self.memset(qa_.ap[64:128, :], 0.0, [qa_.name])
            self.memset(qb_.ap[0:64, :], 0.0, [qb_.name])
        OhR = Rot([self.alloc("Oh%d" % i, [NOWN], BF16) for i in range(2)])
        PtR = Rot([self.alloc("Pt%d" % i, [512], BF16) for i in range(10)])
        PeR = Rot([self.alloc("Pe%d" % i, [512], BF16) for i in range(6)])
        EhR = Rot([(self.alloc("Er%d" % i, [6032], BF16), self.alloc("Em%d" % i, [4000], BF16)) for i in range(2)])
        accR = Rot([(self.ps(0), self.ps(1)), (self.ps(2), self.ps(3))])
        saR = Rot([self.alloc("sa%d" % i, [512], F32) for i in range(2)])
        o0 = self.alloc("o0", [512], F32)
        o1 = self.alloc("o1", [512], F32)
        r0 = self.alloc("r0", [512], F32)
        r1 = self.alloc("r1", [512], F32)
        osq = self.alloc("osq", [512], BF16)
        orr = self.alloc("orr", [512], F32)
        for hh in range(8):
            slope = 2.0 ** (-(hh + 1))
            Kh, Vh, Qh, Oh = KhR.get(), VhR.get(), QhR.get(), OhR.get()
            Er, Em = EhR.get()
            self.act(Er.ap, ttr.ap, AF.Exp, [ttr.name], [Er.name], scale=-slope)
            self.act(Em.ap, ttm.ap, AF.Exp, [ttm.name], [Em.name], scale=-slope)
            rows = slice(hh * 128, (hh + 1) * 128)
            kc_, ko_ = hh // 2, (hh % 2) * 128
            for r_ in range(2):
                self.dma(Kh.ap[:, r_ * 2048:(r_ + 1) * 2048], kf[kc_, r_ * 256 + ko_:r_ * 256 + ko_ + 128, 0:2048],
                         ["ksfk%d" % l], [Kh.name])
                for i4 in range(4):
                    rws = 512 if i4 < 3 else 528
                    self.dma(Vh.ap[:, r_ * 16 + 4 * i4:r_ * 16 + 4 * i4 + 4, :],
                             vf[i4, r_ * rws:r_ * rws + 512, rows].rearrange("(t p) d -> p t d", p=128),
                             ["ksfv%d" % l], [Vh.name])
            self.dma(Kh.ap[:, 4096:4112], kf[kc_, ko_:ko_ + 128, 2048:2064], ["ksfk%d" % l], [Kh.name])
            self.dma(Vh.ap[0:16, 32, :], vf[3, 512:528, rows], ["ksfv%d" % l], [Vh.name])
            self.dma(Qh[0].ap[0:64, :], qTd[hh * 128:hh * 128 + 64, :], ["qT%d" % l], [Qh[0].name])
            self.dma(Qh[1].ap[64:128, :], qTd[hh * 128 + 64:hh * 128 + 128, :], ["qT%d" % l], [Qh[1].name])
            for qi, (q0, qn) in enumerate(QB):
                for c, (rr, oo) in enumerate(((r0, o0), (r1, o1))):
                    psO, psS = accR.get()
                    LAG = 5
                    pend = []
                    for i in range(33 + LAG):
                        if i < 33:
                            k0, kn = KT[i]
                            base_k = (NMETA + 128 * i) if i < 32 else 0
                            if qi < 4:
                                m0_ = (NMETA + 512 * qi) - base_k + 3968
                                Dta = Er.ap[0:kn, m0_:m0_ + qn]
                                Dtn = Er.name
                            else:
                                m0_ = 3984 - base_k
                                Dta = Em.ap[0:kn, m0_:m0_ + qn]
                                Dtn = Em.name
                            pT = self.psr.get()
                            self.mm(pT.ap[0:kn, 0:qn], Kh.ap[:, k0:k0 + kn],
                                    Qh[c].ap[:, q0:q0 + qn], True, True, [Kh.name, Qh[c].name], [pT.name])
                            pe = PeR.get()
                            self.act(pe.ap[0:kn, 0:qn], pT.ap[0:kn, 0:qn], AF.Exp, [pT.name], [pe.name], scale=0.125)
                            pt = PtR.get()
                            self.tt(pt.ap[0:kn, 0:qn], pe.ap[0:kn, 0:qn], Dta, ALU.mult, [pe.name, Dtn], [pt.name],
                                    eng=("dve" if i % 2 == 0 else "pool"))
                            pend.append((i, pt))
                        if i >= LAG:
                            t, pt = pend.pop(0)
                            k0, kn = KT[t]
                            self.mm(psO.ap[:, 0:qn], Vh.ap[0:kn, t, :], pt.ap[0:kn, 0:qn], t == 0, t == 32,
                                    [Vh.name, pt.name], [psO.name])
                            self.mm(psS.ap[:, 0:qn], self.ones.ap[0:kn, :], pt.ap[0:kn, 0:qn], t == 0, t == 32,
                                    [self.ones.name, pt.name], [psS.name])
                    rra = rr.ap[:, 0:qn]
                    pss = psS.ap[:, 0:qn]
                    self.P.add("dve", lambda e, rra=rra, pss=pss: e.reciprocal(rra, pss), [psS.name], [rr.name])
                    self.tt(oo.ap[:, 0:qn], psO.ap[:, 0:qn], rra, ALU.mult, [psO.name, rr.name], [oo.name])
                self.stt(o0.ap[:, 0:qn], o1.ap[:, 0:qn], neglam, o0.ap[:, 0:qn], ALU.mult, ALU.add,
                         [o0.name, o1.name, lt.name], [o0.name])
                self.act(osq.ap[:, 0:qn], o0.ap[:, 0:qn], AF.Square, [o0.name], [osq.name])
                ps = self.psr.get()
                self.mm(ps.ap[:, 0:qn], self.ones.ap, osq.ap[:, 0:qn], True, True, [osq.name, self.ones.name], [ps.name])
                self.act(orr.ap[:, 0:qn], ps.ap[:, 0:qn], AF.Sqrt, [ps.name, self.cst.name], [orr.name],
                         bias=self.eps, scale=1.0 / 128)
                ora = orr.ap[:, 0:qn]
                self.P.add("dve", lambda e, ora=ora: e.reciprocal(ora, ora), [orr.name], [orr.name])
                self.tt(o0.ap[:, 0:qn], o0.ap[:, 0:qn], ora, ALU.mult, [o0.name, orr.name], [o0.name])
                self.ts(Oh.ap[:, q0:q0 + qn], o0.ap[:, 0:qn], vec.ap[:, 32:33], 1.0 - lambda_init, ALU.mult, ALU.mult,
                        [o0.name, vec.name], [Oh.name + "#%d" % qi])
            self.dma(oall[rows, :], Oh.ap, [Oh.name], ["oall%d#%d" % (l, hh)])

    def phaseA_lru(self, l):
        hT = self.dr["hT%d" % l]
        ksx = self.dr["ksx%d" % l]
        ggd = self.dr["gg%d" % l]
        win = self.dr["lru_win"]
        self.phase(self.base)
        self.psr = self.psr8
        vec = self.load_vecs(l)
        hR = Rot([self.alloc("h%d" % i, [8, SBW], F32) for i in range(2)])
        nb = (self.alloc("sq", [8, SBW], BF16), self.alloc("rstd", [SBW], F32), self.alloc("u", [8, SBW], BF16))
        wrot = Rot([self.alloc("w%d" % i, [8, 128], BF16) for i in range(3)])
        xoR = Rot([self.alloc("xo%d" % i, [SBW], F32) for i in range(2)])
        goR = Rot([self.alloc("go%d" % i, [SBW], BF16) for i in range(2)])
        gx = self.alloc("gx", [512], F32)
        g2 = self.alloc("g2", [512], F32)
        hn = self.load_h(hT, SUPER[0][0], SUPER[0][1], hR.get())
        for k, (s0, subs) in enumerate(SUPER):
            W = sum(n for _, n in subs)
            h = hn
            if k + 1 < len(SUPER):
                hn = self.load_h(hT, SUPER[k + 1][0], SUPER[k + 1][1], hR.get())
            u = self.norm_to_bf16(h, subs, vec, 0, 8, 1024, nb)
            for m in range(24):
                wt = self.wload(wrot, win[m], [8, 128])
                xo = xoR.get() if m < 12 else goR.get()
                for si, (c0, n) in enumerate(subs):
                    ps = self.psr.get()
                    for kc in range(8):
                        self.mm(ps.ap[:, 0:n], wt.ap[:, kc, :], u.ap[:, kc, c0:c0 + n], kc == 0, kc == 7,
                                [wt.name, u.name + "#%d" % si], [ps.name])
                    if m < 12:
                        self.copy(xo.ap[:, c0:c0 + n], ps.ap[:, 0:n], [ps.name], [xo.name + "#%d" % si])
                    else:
                        gxa = gx.ap[:, 0:n]
                        g2a = g2.ap[:, 0:n]
                        self.copy(gxa, ps.ap[:, 0:n], [ps.name], [gx.name], eng="act")
                        self.act(g2a, ps.ap[:, 0:n], AF.Square, [ps.name], [g2.name], scale=math.sqrt(0.044715))
                        self.stt(g2a, g2a, 1.0, gxa, ALU.add, ALU.mult, [g2.name, gx.name], [g2.name])
                        self.act(g2a, g2a, AF.Sigmoid, [g2.name], [g2.name], scale=1.5957691216057308)
                        self.tt(xo.ap[:, c0:c0 + n], g2a, gxa, ALU.mult, [g2.name, gx.name], [xo.name + "#%d" % si],
                                eng="pool")
                if m < 12:
                    self.dma(ksx[m * 128:(m + 1) * 128, s0:s0 + W], xo.ap[:, 0:W], [xo.name], ["ksx%d#%d_%d" % (l, m, s0)])
                else:
                    mm_ = m - 12
                    self.dma(ggd[mm_ * 128:(mm_ + 1) * 128, s0:s0 + W], xo.ap[:, 0:W], [xo.name],
                             ["gg%d#%d_%d" % (l, mm_, s0)])

    def phaseB_lru(self, l):
        xf = self.dr["ksfx%d" % l]
        ggd = self.dr["gg%d" % l]
        oall = self.dr["oall%d" % l]
        wg = self.dr["lru_wg"]
        self.phase(self.base)
        self.psr = self.psr8
        vec = self.load_vecs(l)
        cn = self.alloc("cneg", [24], F32)
        cn2 = self.alloc("cneg2", [24], F32)
        self.act(cn.ap, vec.ap[:, 140:164], AF.Exp, [vec.name], [cn.name], scale=-1.0)
        self.act(cn.ap, cn.ap, AF.Ln, [cn.name, self.cst.name], [cn.name], bias=self.one, scale=1.0)
        self.ts(cn.ap, cn.ap, -8.0, None, ALU.mult, None, [cn.name], [cn.name])
        self.ts(cn2.ap, cn.ap, 2.0, None, ALU.mult, None, [cn.name], [cn2.name])
        L = NFULL
        xb = self.alloc("xb", [2, L + 3], F32)
        xc = self.alloc("xc", [2, L], F32)
        xcb = self.alloc("xcb", [2, L], BF16)
        taA = self.alloc("ta", [L], F32)
        tiA = self.alloc("ti", [L], F32)
        hsA = self.alloc("hsA", [L], F32)
        hsB = self.alloc("hsB", [L], F32)
        yo = self.alloc("yo", [NOWN], F32)
        ggt = self.alloc("ggt", [NOWN], BF16)
        yb = self.alloc("yb", [NOWN], BF16)
        wgR = Rot([self.alloc("wg%d" % i, [2, 128], BF16) for i in range(16)])

        def load_gate_w(b_):
            out = {}
            for c2_ in range(2):
                for d_ in range(2):
                    for g_ in range(2):
                        out[(c2_, d_, g_)] = self.wload(wgR, wg[((d_ * 2 + g_) * 6 + b_) * 2 + c2_], [2, 128])
            return out

        gw_next = load_gate_w(0)
        m0 = self.cst.ap[:, 2:3]
        m1 = self.cst.ap[:, 3:4]
        setA = (taA.ap, taA.name, tiA.ap, tiA.name, hsA)
        setB = (xb.ap[:, 0, 0:L], xb.name + "#0", xb.ap[:, 1, 0:L], xb.name + "#1", hsB)
        for b in range(6):
            for c2 in range(2):
                ct = 2 * b + c2
                xn = xb.name + "#%d" % c2
                self.memset(xb.ap[:, c2, 0:2], 0.0, [xn])
                self.memset(xb.ap[:, c2, L + 2:L + 3], 0.0, [xn])
                self.dma(xb.ap[:, c2, 2:18], xf[ct, 0:128, 2048:2064], ["ksfx%d" % l], [xn])
                self.dma(xb.ap[:, c2, 18:2066], xf[ct, 0:128, 0:2048], ["ksfx%d" % l], [xn])
                self.dma(xb.ap[:, c2, 2066:4114], xf[ct, 128:256, 0:2048], ["ksfx%d" % l], [xn])
                xca = xc.ap[:, c2, :]
                self.act(xca, xb.ap[:, c2, 0:L], AF.Identity, [xn, vec.name], [xc.name + "#%d" % c2],
                         bias=vec.ap[:, 80 + ct:81 + ct], scale=vec.ap[:, 32 + ct:33 + ct])
                for jj in range(1, 4):
                    col = 32 + jj * 12 + ct
                    self.stt(xca, xb.ap[:, c2, jj:jj + L], vec.ap[:, col:col + 1], xca, ALU.mult, ALU.add,
                             [xn, vec.name, xc.name + "#%d" % c2], [xc.name + "#%d" % c2])
                self.copy(xcb.ap[:, c2, :], xca, [xc.name + "#%d" % c2], [xcb.name + "#%d" % c2], eng="dve")
            gw = gw_next
            if b + 1 < 6:
                gw_next = load_gate_w(b + 1)
            for c2 in range(2):
                ct = 2 * b + c2
                for d in range(2):
                    ta_ap, ta_n, ti_ap, ti_n, hs = setA if d == 0 else setB
                    wr = gw[(c2, d, 0)]
                    wi = gw[(c2, d, 1)]
                    br = 92 + (d * 2 + 0) * 12 + ct
                    bi = 92 + (d * 2 + 1) * 12 + ct
                    for (c0, n) in KB:
                        for (wt, bcol, dap, dn) in ((wr, br, ta_ap, ta_n), (wi, bi, ti_ap, ti_n)):
                            ps = self.psr.get()
                            for kc in range(2):
                                self.mm(ps.ap[:, 0:n], wt.ap[:, kc, :], xcb.ap[:, kc, c0:c0 + n], kc == 0, kc == 1,
                                        [wt.name, xcb.name], [ps.name])
                            self.act(dap[:, c0:c0 + n], ps.ap[:, 0:n], AF.Sigmoid, [ps.name, vec.name],
                                     [dn], bias=vec.ap[:, bcol:bcol + 1], scale=1.0)
                    ccol = d * 12 + ct
                    self.act(hs.ap, ta_ap, AF.Exp, [ta_n, cn2.name], [hs.name], scale=cn2.ap[:, ccol:ccol + 1])
                    self.act(hs.ap, hs.ap, AF.Sqrt, [hs.name, self.cst.name], [hs.name], bias=self.one, scale=-1.0)
                    self.act(ta_ap, ta_ap, AF.Exp, [ta_n, cn.name], [ta_n], scale=cn.ap[:, ccol:ccol + 1])
                    self.tt(ti_ap, ti_ap, xc.ap[:, c2, :], ALU.mult, [ti_n, xc.name + "#%d" % c2], [ti_n])
                    self.tt(ti_ap, ti_ap, hs.ap, ALU.mult, [ti_n, hs.name], [ti_n])
                    dst = hs
                    if d == 0:
                        o_, a_, u_ = dst.ap, ta_ap, ti_ap
                    else:
                        o_, a_, u_ = dst.ap[:, ::-1], ta_ap[:, ::-1], ti_ap[:, ::-1]
                    self.P.add("dve", lambda e, o_=o_, a_=a_, u_=u_: e.tensor_tensor_scan(o_, a_, u_, 0.0, ALU.mult, ALU.add),
                               [ta_n, ti_n], [dst.name])
                self.ts(yo.ap[:, 0:2048], hsA.ap[:, 16:2064], m0, None, ALU.mult, None, [hsA.name, self.cst.name], [yo.name])
                self.stt(yo.ap[:, 0:2048], hsB.ap[:, 16:2064], m0, yo.ap[:, 0:2048], ALU.mult, ALU.add,
                         [hsB.name, yo.name, self.cst.name], [yo.name])
                self.stt(yo.ap[:, 0:2048], hsA.ap[:, 2064:4112], m1, yo.ap[:, 0:2048], ALU.mult, ALU.add,
                         [hsA.name, yo.name, self.cst.name], [yo.name])
                self.stt(yo.ap[:, 0:2048], hsB.ap[:, 2064:4112], m1, yo.ap[:, 0:2048], ALU.mult, ALU.add,
                         [hsB.name, yo.name, self.cst.name], [yo.name])
                self.tt(yo.ap[:, 2048:2064], hsA.ap[:, 0:16], hsB.ap[:, 0:16], ALU.add, [hsA.name, hsB.name], [yo.name])
                rows = slice(ct * 128, (ct + 1) * 128)
                self.dma(ggt.ap, ggd[rows, :], ["gg%d" % l], [ggt.name])
                self.tt(yb.ap, yo.ap, ggt.ap, ALU.mult, [yo.name, ggt.name], [yb.name], eng="pool")
                self.dma(oall[rows, :], yb.ap, [yb.name], ["oall%d#%d" % (l, ct)])

    def phaseC(self, l, KCo, wo_name):
        hT = self.dr["hT%d" % l]
        hTo = self.dr["hT%d" % (l + 1)]
        oall = self.dr["oall%d" % l]
        wo = self.dr[wo_name]
        w1 = self.dr["ffn_w1"]
        w2 = self.dr["ffn_w2"]
        self.phase(self.base)
        self.psr = self.psr8
        vec = self.load_vecs(l)
        hR = Rot([self.alloc("h%d" % i, [8, SBW], F32) for i in range(2)])
        u1 = self.alloc("u1", [8, SBW], F32)
        sq = self.alloc("sqc", [8, SBW], BF16)
        rstd = self.alloc("rstdc", [SBW], F32)
        u3 = self.alloc("u3", [8, SBW], BF16)
        hid = self.alloc("hid", [22, SBW], BF16)
        oin = self.alloc("oin", [KCo, SBW], BF16)
        sgR = Rot([self.alloc("sg%d" % i, [512], F32) for i in range(2)])
        w1R = Rot([self.alloc("w1_%d" % i, [8, 256], BF16) for i in range(2)])
        w2R = Rot([self.alloc("w2_%d" % i, [22, 128], BF16) for i in range(2)])
        woR = Rot([self.alloc("wo_%d" % i, [KCo, 128], BF16) for i in range(2)])
        src = oall.rearrange("(c p) n -> p c n", p=128)
        dst = hTo.rearrange("(c p) n -> p c n", p=128)

        def load_oin(s0, W):
            for kc in range(KCo):
                self.dma(oin.ap[:, kc, 0:W], src[:, kc, s0:s0 + W], ["oall%d" % l], [oin.name])

        hn = self.load_h(hT, SUPER[0][0], SUPER[0][1], hR.get())
        load_oin(SUPER[0][0], SBW)
        for k, (s0, subs) in enumerate(SUPER):
            h = hn
            for m in range(8):
                wt = self.wload(woR, wo[JIDX[l], m] if wo_name == "mla_wo" else wo[m], [KCo, 128])
                for si, (c0, n) in enumerate(subs):
                    ps = self.psr.get()
                    for kc in range(KCo):
                        self.mm(ps.ap[:, 0:n], wt.ap[:, kc, :], oin.ap[:, kc, c0:c0 + n], kc == 0, kc == KCo - 1,
                                [wt.name, oin.name], [ps.name])
                    self.copy(u1.ap[:, m, c0:c0 + n], ps.ap[:, 0:n], [ps.name], [u1.name + "#%d" % si])
            if k + 1 < len(SUPER):
                hn = self.load_h(hT, SUPER[k + 1][0], SUPER[k + 1][1], hR.get())
                load_oin(SUPER[k + 1][0], SBW)
            self.resid_norm(h, u1, subs, sq, rstd, vec, 8)
            self.rms_stats(h, 8, subs, sq, rstd, 1024)
            for si, (c0, n) in enumerate(subs):
                for c in range(8):
                    self.stt(u3.ap[:, c, c0:c0 + n], h.ap[:, c, c0:c0 + n], vec.ap[:, 16 + c:17 + c],
                             rstd.ap[:, c0:c0 + n], ALU.mult, ALU.mult,
                             [h.name + "#%d" % si, rstd.name + "#%d" % si, vec.name], [u3.name + "#%d" % si])
            for jn in range(22):
                wt = self.wload(w1R, w1[l, jn], [8, 256])
                for si, (c0, n) in enumerate(subs):
                    pg = self.psr.get()
                    for kc in range(8):
                        self.mm(pg.ap[:, 0:n], wt.ap[:, kc, 0:128], u3.ap[:, kc, c0:c0 + n], kc == 0, kc == 7,
                                [wt.name, u3.name + "#%d" % si], [pg.name])
                    pu = self.psr.get()
                    for kc in range(8):
                        self.mm(pu.ap[:, 0:n], wt.ap[:, kc, 128:256], u3.ap[:, kc, c0:c0 + n], kc == 0, kc == 7,
                                [wt.name, u3.name + "#%d" % si], [pu.name])
                    sg = sgR.get()
                    self.act(sg.ap[:, 0:n], pg.ap[:, 0:n], AF.Silu, [pg.name], [sg.name])
                    self.tt(hid.ap[:, jn, c0:c0 + n], sg.ap[:, 0:n], pu.ap[:, 0:n], ALU.mult, [sg.name, pu.name],
                            [hid.name + "#%d" % si])
            for m in range(8):
                wt = self.wload(w2R, w2[l, m], [22, 128])
                for si, (c0, n) in enumerate(subs):
                    ps = self.psr.get()
                    for kc in range(22):
                        self.mm(ps.ap[:, 0:n], wt.ap[:, kc, :], hid.ap[:, kc, c0:c0 + n], kc == 0, kc == 21,
                                [wt.name, hid.name + "#%d" % si], [ps.name])
                    self.copy(u1.ap[:, m, c0:c0 + n], ps.ap[:, 0:n], [ps.name], [u1.name + "#%d" % si])
            self.resid_norm(h, u1, subs, sq, rstd, vec, 24)
            for si, (c0, n) in enumerate(subs):
                self.dma(dst[:, :, s0 + c0:s0 + c0 + n], h.ap[:, :, c0:c0 + n], [h.name + "#%d" % si],
                         ["hT%d#%d" % (l + 1, s0 + c0)])

    def resid_norm(self, h, u1, subs, sq, rstd, vec, gcol):
        self.rms_stats(u1, 8, subs, sq, rstd, 1024)
        for si, (c0, n) in enumerate(subs):
            for c in range(8):
                ua = u1.ap[:, c, c0:c0 + n]
                self.tt(ua, ua, rstd.ap[:, c0:c0 + n], ALU.mult, [u1.name + "#%d" % si, rstd.name + "#%d" % si],
                        [u1.name + "#%d" % si], eng="pool")
                ha = h.ap[:, c, c0:c0 + n]
                self.stt(ha, ua, vec.ap[:, gcol + c:gcol + c + 1], ha, ALU.mult, ALU.add,
                         [u1.name + "#%d" % si, h.name + "#%d" % si, vec.name], [h.name + "#%d" % si])


WEIGHT_SHAPES = {
    "vecs": ([4, 128, NV], F32),
    "cst": ([128, 4], F32),
    "cs": ([128, NOWN], F32),
    "ttr": ([128, 6032], F32),
    "ttm": ([128, 4000], F32),
    "mla_win": ([2, 6, 128, 8, 128], F32),
    "mla_wuq": ([2, 16, 128, 3, 256], F32),
    "mla_wukv": ([2, 16, 128, 2, 256], F32),
    "mla_wo": ([2, 8, 128, 16, 128], F32),
    "diff_wqk": ([16, 128, 8, 128], F32),
    "diff_wv": ([2, 128, 8, 512], F32),
    "diff_lam": ([1, 256], F32),
    "diff_wo": ([8, 128, 8, 128], F32),
    "lru_win": ([24, 128, 8, 128], F32),
    "lru_wg": ([48, 128, 2, 128], F32),
    "lru_wo": ([8, 128, 12, 128], F32),
    "ffn_w1": ([4, 22, 128, 8, 256], F32),
    "ffn_w2": ([4, 8, 128, 22, 128], F32),
}


def act_shapes(l):
    k = KINDS[l]
    d = {"hT%d" % l: ([D, NOWN], F32), "hT%d" % (l + 1): ([D, NOWN], F32)}
    if k == "mla":
        d.update({"ks%d" % l: ([320, NOWN], BF16), "cqn%d" % l: ([384, NOWN], BF16),
                  "ksf%d" % l: ([2, 320, NOWN], BF16), "oall%d" % l: ([2048, NOWN], BF16)})
    elif k == "diff":
        d.update({"ksk%d" % l: ([D, NOWN], BF16), "ksv%d" % l: ([NOWN, D], BF16), "qT%d" % l: ([D, NOWN], BF16),
                  "ksfk%d" % l: ([4, 512, NOWN], BF16), "ksfv%d" % l: ([4, 1056, D], BF16),
                  "oall%d" % l: ([D, NOWN], BF16)})
    else:
        d.update({"ksx%d" % l: ([1536, NOWN], F32), "gg%d" % l: ([1536, NOWN], BF16),
                  "ksfx%d" % l: ([12, 256, NOWN], F32), "oall%d" % l: ([1536, NOWN], BF16)})
    return d


def build_program(stages, ext_in, ext_out):
    nc = bass.Bass("TRN2", target_bir_lowering=False)
    with ExitStack() as st:
        b = Bld(nc, st)
        shapes = {}
        for (_, l) in stages:
            shapes.update(act_shapes(l))
        used_w = set(["vecs", "cst"])
        for (ph, l) in stages:
            k = KINDS[l]
            if ph == "A":
                used_w |= {"mla": {"mla_win", "cs"}, "diff": {"diff_wqk", "diff_wv"}, "lru": {"lru_win"}}[k]
            elif ph == "B":
                used_w |= {"mla": {"mla_wuq", "mla_wukv", "cs"}, "diff": {"diff_lam", "ttr", "ttm"}, "lru": {"lru_wg"}}[k]
            elif ph == "C":
                used_w |= {"ffn_w1", "ffn_w2", {"mla": "mla_wo", "diff": "diff_wo", "lru": "lru_wo"}[k]}
        for name in sorted(used_w):
            shp, dt = WEIGHT_SHAPES[name]
            b.dram(name, shp, dt, "ExternalInput")
        needed = set()
        for (ph, l) in stages:
            k = KINDS[l]
            if ph == "A":
                needed |= {"hT%d" % l} | {"mla": {"ks%d" % l, "cqn%d" % l}, "diff": {"ksk%d" % l, "ksv%d" % l, "qT%d" % l},
                                          "lru": {"ksx%d" % l, "gg%d" % l}}[k]
            elif ph == "B":
                needed |= {"oall%d" % l} | {"mla": {"ksf%d" % l, "cqn%d" % l}, "diff": {"ksfk%d" % l, "ksfv%d" % l, "qT%d" % l},
                                            "lru": {"ksfx%d" % l, "gg%d" % l}}[k]
            elif ph == "C":
                needed |= {"oall%d" % l, "hT%d" % l, "hT%d" % (l + 1)}
            elif ph == "X":
                needed |= {"mla": {"ks%d" % l, "ksf%d" % l}, "diff": {"ksk%d" % l, "ksv%d" % l, "ksfk%d" % l, "ksfv%d" % l},
                           "lru": {"ksx%d" % l, "ksfx%d" % l}}[k]
        for name in sorted(needed):
            shp, dt = shapes[name]
            kind = "ExternalInput" if name in ext_in else ("ExternalOutput" if name in ext_out else "Internal")
            b.dram(name, shp, dt, kind)
        b.setup_consts(b.dr["cst"])
        for (ph, l) in stages:
            k = KINDS[l]
            if ph == "A":
                getattr(b, "phaseA_" + k)(l)
            elif ph == "B":
                getattr(b, "phaseB_" + k)(l)
            elif ph == "X":
                b.exchange(l)
            elif ph == "C":
                b.phaseC(l, {"mla": 16, "diff": 8, "lru": 12}[k], {"mla": "mla_wo", "diff": "diff_wo", "lru": "lru_wo"}[k])
        b.P.barrier()
        b.P.add("sp", lambda e: None, [], [])
        b.P.emit()
    return nc, sorted(used_w)


def tile_w(w, KC, mw):
    K, M = w.shape
    assert K == KC * 128 and M % mw == 0
    return np.ascontiguousarray(w.reshape(KC, 128, M // mw, mw).transpose(2, 1, 0, 3))


def col128(v):
    return v.reshape(-1, 128).T


def prep_weights(inp):
    f = np.float32
    W = {}
    vecs = np.zeros((4, 128, NV), f)
    for l in range(4):
        for k in range(4):
            vecs[l, :, 8 * k:8 * k + 8] = col128(inp["norm_g"][l, k])
        kind, j = KINDS[l], JIDX[l]
        if kind == "mla":
            vecs[l, :, 32:35] = col128(inp["mla_q_norm"][j])
            vecs[l, :, 35:37] = col128(inp["mla_kv_norm"][j])
        elif kind == "diff":
            vecs[l, :, 32:33] = col128(inp["diff_subln"][j])
        else:
            for jj in range(4):
                vecs[l, :, 32 + jj * 12:44 + jj * 12] = col128(inp["lru_conv_w"][j, jj])
            vecs[l, :, 80:92] = col128(inp["lru_conv_b"][j])
            for d in range(2):
                for g in range(2):
                    c0 = 92 + (d * 2 + g) * 12
                    vecs[l, :, c0:c0 + 12] = col128(inp["lru_b_gates"][j, d, g])
                vecs[l, :, 140 + d * 12:152 + d * 12] = col128(inp["lru_lambda"][j, d])
    W["vecs"] = vecs
    sw = np.concatenate([np.arange(32, 64), np.arange(0, 32)])
    win = []
    wuq = []
    for j in range(2):
        w = inp["mla_w_in"][j]
        ext = np.concatenate([w, w[:, 640:704][:, sw]], axis=1)
        win.append(tile_w(ext, 8, 128))
        q = inp["mla_w_uq"][j].reshape(384, 16, 192)
        qe = np.concatenate([q, q[:, :, 128:192][:, :, sw]], axis=2)
        wuq.append(tile_w(qe.reshape(384, 4096), 3, 256))
    W["mla_win"] = np.stack(win)
    W["mla_wuq"] = np.stack(wuq)
    W["mla_wukv"] = np.stack([tile_w(inp["mla_w_ukv"][j], 2, 256) for j in range(2)])
    W["mla_wo"] = np.stack([tile_w(inp["mla_w_o"][j], 16, 128) for j in range(2)])
    dw = inp["diff_w_in"][0]
    W["diff_wqk"] = tile_w(dw[:, 0:2048], 8, 128)
    W["diff_wv"] = tile_w(dw[:, 2048:3072], 8, 512)
    W["diff_lam"] = np.ascontiguousarray(inp["diff_lambda"][0].reshape(1, 256))
    W["diff_wo"] = tile_w(inp["diff_w_o"][0], 8, 128)
    W["lru_win"] = tile_w(inp["lru_w_in"][0], 8, 128)
    wg = inp["lru_w_gates"][0]
    wg = wg.reshape(2, 2, 6, 2, 128, 2, 128)
    W["lru_wg"] = np.ascontiguousarray(wg.transpose(0, 1, 2, 5, 4, 3, 6)).reshape(48, 128, 2, 128)
    W["lru_wo"] = tile_w(inp["lru_w_o"][0], 12, 128)
    w1 = []
    w2 = []
    for l in range(4):
        wi = inp["ffn_w_in"][l]
        g = wi[:, :2816].reshape(1024, 22, 128)
        u = wi[:, 2816:].reshape(1024, 22, 128)
        w1.append(tile_w(np.concatenate([g, u], axis=2).reshape(1024, 22 * 256), 8, 256))
        w2.append(tile_w(inp["ffn_w_out"][l], 22, 128))
    W["ffn_w1"] = np.stack(w1)
    W["ffn_w2"] = np.stack(w2)
    return {k: np.ascontiguousarray(v, dtype=f) for k, v in W.items()}


def core_consts(r):
    f = np.float32
    pos = np.concatenate([NMETA + NREAL * r + np.arange(NREAL), np.arange(NMETA)]).astype(f)
    inv = (10000.0 ** (-np.arange(0, 64, 2, dtype=f) / 64)).astype(f)
    ang = pos[None, :] * inv[:, None]
    c, s = np.cos(ang).astype(f), np.sin(ang).astype(f)
    cs = np.concatenate([c, c, -s, s], axis=0)
    p = np.arange(128, dtype=np.float64)[:, None]
    ttr = np.abs(np.arange(6032, dtype=np.float64)[None, :] - 3968 - p + NREAL * r).astype(f)
    ttm = np.abs(np.arange(4000, dtype=np.float64)[None, :] - 3984 - p).astype(f)
    cst = np.zeros((128, 4), f)
    cst[:, 0] = EPS
    cst[:, 1] = 1.0
    cst[:, 2 + r] = 1.0
    return {"cs": cs, "ttr": ttr, "ttm": ttm, "cst": cst}


LAUNCHES = [
    ([("A", 0)], ["hT0"], ["ks0", "cqn0"]),
    ([("B", 0), ("C", 0), ("A", 1)], ["ksf0", "cqn0", "hT0"], ["hT1", "ksk1", "ksv1", "qT1"]),
    ([("B", 1), ("C", 1), ("A", 2)], ["ksfk1", "ksfv1", "qT1", "hT1"], ["hT2", "ksx2", "gg2"]),
    ([("B", 2), ("C", 2), ("A", 3)], ["ksfx2", "gg2", "hT2"], ["hT3", "ks3", "cqn3"]),
    ([("B", 3), ("C", 3)], ["ksf3", "cqn3", "hT3"], ["hT4"]),
]
FUSED_LAUNCH = ([(ph, l) for l in range(4) for ph in ("A", "X", "B", "C")], ["hT0"], ["hT4"])
FUSED = True
EXCH = {"ks0": "ksf0", "ksk1": "ksfk1", "ksv1": "ksfv1", "ksx2": "ksfx2", "ks3": "ksf3"}

_PROG_CACHE = {}


def run_launch(li, W, consts, state, ncores):
    stages, ext_in, ext_out = FUSED_LAUNCH if li == "fused" else LAUNCHES[li]
    if li not in _PROG_CACHE:
        _PROG_CACHE[li] = build_program(stages, ext_in, ext_out)
    nc, used_w = _PROG_CACHE[li]
    in_maps = []
    for c in range(ncores):
        m = {}
        for name in used_w:
            m[name] = consts[c % 2][name] if name in consts[0] else W[name]
        for name in ext_in:
            m[name] = state[name][c]
        in_maps.append(m)
    res = run_bass_kernel_spmd(nc, in_maps, core_ids=list(range(ncores)))
    for name in ext_out:
        state[name] = [res.results[c][name] for c in range(ncores)]
    for name in ext_out:
        if name in EXCH:
            full = []
            for c in range(ncores):
                p = c // 2
                full.append(np.stack([state[name][2 * p], state[name][2 * p + 1]]))
            state[EXCH[name]] = full


def kernel(**inp):
    inp = {k: np.asarray(v) for k, v in inp.items()}
    x = inp["x"]
    Bn = x.shape[0]
    ncores = 2 * Bn
    W = prep_weights(inp)
    consts = [core_consts(0), core_consts(1)]
    meta = inp["meta_tokens"].astype(np.float32)
    state = {"hT0": []}
    for c in range(ncores):
        b, r = c // 2, c % 2
        own = np.concatenate([x[b, r * NREAL:(r + 1) * NREAL], meta], axis=0)
        state["hT0"].append(np.ascontiguousarray(own.T))
    if FUSED:
        run_launch("fused", W, consts, state, ncores)
    else:
        for li in range(len(LAUNCHES)):
            run_launch(li, W, consts, state, ncores)
    out = np.zeros((Bn, 2 * NREAL, D), np.float32)
    for c in range(ncores):
        b, r = c // 2, c % 2
        out[b, r * NREAL:(r + 1) * NREAL] = state["hT4"][c][:, 0:NREAL].T
    return out
```
